# Optimizing a Trainium2 kernel written in Bass

```python
import math
import jax, jax.numpy as jnp
from jax import lax
import numpy as np

D_MODEL = 1024
BATCH = 8
SEQ = 2048
DEPTH = 2
DEC_BATCH = 128
DEC_SEQ = 4
PAST_LEN = 8192
PAGE_SIZE = 128

N_EVEN = (DEPTH + 1) // 2
N_ODD = DEPTH // 2
HEAD_DIM = 64
N_Q_HEADS = 8
N_KV_HEADS = 2
Q_PER_KV = N_Q_HEADS // N_KV_HEADS
WINDOW = 128
ATTN_WIDTH = N_Q_HEADS * HEAD_DIM
KV_WIDTH = N_KV_HEADS * HEAD_DIM
SSM_WIDTH = D_MODEL // 2
SSM_GROUP = 16
SSM_GROUPS = SSM_WIDTH // SSM_GROUP
SSM_STATE = 64
IN_AB_WIDTH = ATTN_WIDTH + 2 * KV_WIDTH + SSM_WIDTH
MIX_WIDTH = ATTN_WIDTH + SSM_WIDTH
CONV_WIDTH = 31
D_CONV = D_MODEL
D_FF = -(-8 * D_MODEL // (3 * 256)) * 256
RMS_EPS = 1e-6
LN_EPS = 1e-5

kernel_name = 'swa_s5_conformer_hybrid_step'

F32 = jnp.float32


def rms_norm(x, g):
    xf = x.astype(F32)
    y = xf * lax.rsqrt(jnp.mean(xf * xf, axis=-1, keepdims=True) + RMS_EPS) * g.astype(F32)
    return y.astype(x.dtype)


def sink_attention(q, k, v, mask, sinks):
    s = jnp.einsum('...tgrd,...sgd->...grts', q, k, preferred_element_type=F32) * (HEAD_DIM ** -0.5)
    s = jnp.where(mask, s, -jnp.inf)
    sink = jnp.broadcast_to(sinks.astype(F32).reshape(N_KV_HEADS, Q_PER_KV, 1, 1), s.shape[:-1] + (1,))
    p = jax.nn.softmax(jnp.concatenate([s, sink], axis=-1), axis=-1)[..., :-1]
    return jnp.einsum('...grts,...sgd->...tgrd', p.astype(v.dtype), v)


def swa_prompt(q, k, v, sinks):
    n, L = q.shape[:2]
    nb = L // WINDOW
    qb = q.reshape(n, nb, WINDOW, N_KV_HEADS, Q_PER_KV, HEAD_DIM)

    def band(t):
        cur = t.reshape(n, nb, WINDOW, N_KV_HEADS, HEAD_DIM)
        prev = jnp.concatenate([jnp.zeros_like(cur[:, :1]), cur[:, :-1]], axis=1)
        return jnp.concatenate([prev, cur], axis=2)

    i = jnp.arange(WINDOW)[:, None]
    j = jnp.arange(2 * WINDOW)[None, :]
    rel = i + WINDOW - j
    blk = jnp.arange(nb)[:, None, None]
    mask = (rel >= 0) & (rel < WINDOW) & (blk * WINDOW - WINDOW + j >= 0)
    o = sink_attention(qb, band(k), band(v), mask[None, :, None, None], sinks)
    return o.reshape(n, L, ATTN_WIDTH)


def swa_sample(q, k_new, v_new, k_buf, v_buf, sinks):
    n, T = q.shape[:2]
    wb = k_buf.shape[1]
    k_all = jnp.concatenate([k_buf.astype(k_new.dtype), k_new], axis=1)
    v_all = jnp.concatenate([v_buf.astype(v_new.dtype), v_new], axis=1)
    kpos = jnp.concatenate([jnp.arange(wb) - wb, jnp.arange(T)])
    rel = jnp.arange(T)[:, None] - kpos[None, :]
    mask = (rel >= 0) & (rel < WINDOW)
    o = sink_attention(q, k_all, v_all, mask, sinks)
    return o.reshape(n, T, ATTN_WIDTH), k_all[:, -wb:], v_all[:, -wb:]


def s5_scan(u, h0_re, h0_im, lam_re, lam_im, log_step, b_re, b_im, c_re, c_im, d_skip):
    n, L, _ = u.shape
    ug = u.astype(F32).reshape(n, L, SSM_GROUPS, SSM_GROUP)
    lam_re = lam_re.astype(F32)
    lam_im = lam_im.astype(F32)
    dt = jnp.exp(log_step.astype(F32))[:, None]
    mag = jnp.exp(lam_re * dt)
    lb_re = mag * jnp.cos(lam_im * dt)
    lb_im = mag * jnp.sin(lam_im * dt)
    den = lam_re * lam_re + lam_im * lam_im
    coef_re = ((lb_re - 1.0) * lam_re + lb_im * lam_im) / den
    coef_im = (lb_im * lam_re - (lb_re - 1.0) * lam_im) / den
    b_re = b_re.astype(F32)
    b_im = b_im.astype(F32)
    bb_re = coef_re[..., None] * b_re - coef_im[..., None] * b_im
    bb_im = coef_re[..., None] * b_im + coef_im[..., None] * b_re
    bu_re = jnp.einsum('nlgc,gpc->nlgp', ug, bb_re)
    bu_im = jnp.einsum('nlgc,gpc->nlgp', ug, bb_im)
    a_re = jnp.broadcast_to(lb_re, bu_re.shape)
    a_im = jnp.broadcast_to(lb_im, bu_im.shape)

    def combine(e1, e2):
        a1r, a1i, b1r, b1i = e1
        a2r, a2i, b2r, b2i = e2
        return (a2r * a1r - a2i * a1i,
                a2r * a1i + a2i * a1r,
                a2r * b1r - a2i * b1i + b2r,
                a2r * b1i + a2i * b1r + b2i)

    acc_re, acc_im, s_re, s_im = lax.associative_scan(combine, (a_re, a_im, bu_re, bu_im), axis=1)
    h0_re = h0_re.astype(F32)[:, None]
    h0_im = h0_im.astype(F32)[:, None]
    h_re = s_re + acc_re * h0_re - acc_im * h0_im
    h_im = s_im + acc_re * h0_im + acc_im * h0_re
    y = (jnp.einsum('gcp,nlgp->nlgc', c_re.astype(F32), h_re)
         - jnp.einsum('gcp,nlgp->nlgc', c_im.astype(F32), h_im))
    y = y + d_skip.astype(F32) * ug
    return y.reshape(n, L, SSM_WIDTH), h_re[:, -1], h_im[:, -1]


def ab_mixer(h, k_buf, v_buf, h0_re, h0_im, w_in, w_out, sinks, ssm, w_glu):
    n, L, _ = h.shape
    z = h @ w_in
    q, k, v, u = jnp.split(z, [ATTN_WIDTH, ATTN_WIDTH + KV_WIDTH, ATTN_WIDTH + 2 * KV_WIDTH], axis=-1)
    q = q.reshape(n, L, N_KV_HEADS, Q_PER_KV, HEAD_DIM)
    k = k.reshape(n, L, N_KV_HEADS, HEAD_DIM)
    v = v.reshape(n, L, N_KV_HEADS, HEAD_DIM)
    if k_buf is None:
        a = swa_prompt(q, k, v, sinks)
        wb = min(WINDOW, L)
        k_keep, v_keep = k[:, L - wb:], v[:, L - wb:]
    else:
        a, k_keep, v_keep = swa_sample(q, k, v, k_buf, v_buf, sinks)
    y, h_re, h_im = s5_scan(u, h0_re, h0_im, *ssm)
    g = jax.nn.gelu(y)
    s = g * jax.nn.sigmoid(g @ w_glu.astype(F32))
    out = jnp.concatenate([a, s.astype(h.dtype)], axis=-1) @ w_out
    return out, k_keep, v_keep, h_re, h_im


def conv_module(h, prefix, w_pw1, b_pw1, w_dw, b_dw, ln_g, ln_b, w_pw2, b_pw2):
    a = h @ w_pw1 + b_pw1
    a1, a2 = jnp.split(a, 2, axis=-1)
    g = a1 * jax.nn.sigmoid(a2)
    gp = jnp.concatenate([prefix.astype(g.dtype), g], axis=1)
    c = lax.conv_general_dilated(gp, w_dw[:, None, :].astype(g.dtype), (1,), 'VALID',
                                 dimension_numbers=('NWC', 'WIO', 'NWC'),
                                 feature_group_count=D_CONV) + b_dw
    cf = c.astype(F32)
    mu = jnp.mean(cf, axis=-1, keepdims=True)
    var = jnp.mean(jnp.square(cf - mu), axis=-1, keepdims=True)
    cn = (cf - mu) * lax.rsqrt(var + LN_EPS) * ln_g.astype(F32) + ln_b.astype(F32)
    out = jax.nn.silu(cn).astype(h.dtype) @ w_pw2 + b_pw2
    return out, gp[:, -(CONV_WIDTH - 1):]


def swiglu(h, wg, wu, wd):
    return (jax.nn.silu(h @ wg) * (h @ wu)) @ wd


def setup_inputs(seed: int = 0) -> dict:
    key = jax.random.key(seed)
    ks = iter(jax.random.split(key, 64))

    def nrm(shape, scale):
        return jax.random.normal(next(ks), shape, F32) * scale

    def gain(shape):
        return 1.0 + nrm(shape, 0.05)

    win_buf = min(WINDOW, PAST_LEN)
    lam_im0 = math.pi * jnp.arange(SSM_STATE, dtype=F32)
    return {
        'x_prompt': nrm((BATCH, SEQ, D_MODEL), 1.0),
        'x_sample': nrm((DEC_BATCH, DEC_SEQ, D_MODEL), 1.0),
        'cache_k': nrm((N_EVEN, DEC_BATCH, win_buf, N_KV_HEADS, HEAD_DIM), 1.0),
        'cache_v': nrm((N_EVEN, DEC_BATCH, win_buf, N_KV_HEADS, HEAD_DIM), 1.0),
        'state_s5_re': nrm((N_EVEN, DEC_BATCH, SSM_GROUPS, SSM_STATE), 1.0),
        'state_s5_im': nrm((N_EVEN, DEC_BATCH, SSM_GROUPS, SSM_STATE), 1.0),
        'state_conv': nrm((N_ODD, DEC_BATCH, CONV_WIDTH - 1, D_CONV), 0.5),
        'norm_pre_mix': gain((DEPTH, D_MODEL)),
        'norm_post_mix': gain((DEPTH, D_MODEL)),
        'norm_pre_ffn': gain((DEPTH, D_MODEL)),
        'norm_post_ffn': gain((DEPTH, D_MODEL)),
        'w_in_ab': nrm((N_EVEN, D_MODEL, IN_AB_WIDTH), D_MODEL ** -0.5),
        'attn_sinks': nrm((N_EVEN, N_Q_HEADS), 0.5),
        's5_lambda_re': -0.5 + nrm((N_EVEN, SSM_GROUPS, SSM_STATE), 0.01),
        's5_lambda_im': lam_im0 + nrm((N_EVEN, SSM_GROUPS, SSM_STATE), 0.01),
        's5_log_step': jax.random.uniform(next(ks), (N_EVEN, SSM_GROUPS), F32, math.log(1e-3), math.log(1e-1)),
        's5_b_re': nrm((N_EVEN, SSM_GROUPS, SSM_STATE, SSM_GROUP), (2 * SSM_GROUP) ** -0.5),
        's5_b_im': nrm((N_EVEN, SSM_GROUPS, SSM_STATE, SSM_GROUP), (2 * SSM_GROUP) ** -0.5),
        's5_c_re': nrm((N_EVEN, SSM_GROUPS, SSM_GROUP, SSM_STATE), SSM_STATE ** -0.5),
        's5_c_im': nrm((N_EVEN, SSM_GROUPS, SSM_GROUP, SSM_STATE), SSM_STATE ** -0.5),
        's5_d': nrm((N_EVEN, SSM_GROUPS, SSM_GROUP), 1.0),
        'w_glu': nrm((N_EVEN, SSM_WIDTH, SSM_WIDTH), SSM_WIDTH ** -0.5),
        'w_out_ab': nrm((N_EVEN, MIX_WIDTH, D_MODEL), MIX_WIDTH ** -0.5),
        'w_pw1': nrm((N_ODD, D_MODEL, 2 * D_CONV), D_MODEL ** -0.5),
        'b_pw1': nrm((N_ODD, 2 * D_CONV), 0.02),
        'w_dw': nrm((N_ODD, CONV_WIDTH, D_CONV), CONV_WIDTH ** -0.5),
        'b_dw': nrm((N_ODD, D_CONV), 0.02),
        'conv_ln_g': gain((N_ODD, D_CONV)),
        'conv_ln_b': nrm((N_ODD, D_CONV), 0.02),
        'w_pw2': nrm((N_ODD, D_CONV, D_MODEL), D_CONV ** -0.5),
        'b_pw2': nrm((N_ODD, D_MODEL), 0.02),
        'w_ffn_gate': nrm((DEPTH, D_MODEL, D_FF), D_MODEL ** -0.5),
        'w_ffn_up': nrm((DEPTH, D_MODEL, D_FF), D_MODEL ** -0.5),
        'w_ffn_down': nrm((DEPTH, D_FF, D_MODEL), D_FF ** -0.5),
    }


def reference(x_prompt, x_sample, cache_k, cache_v, state_s5_re, state_s5_im, state_conv,
              norm_pre_mix, norm_post_mix, norm_pre_ffn, norm_post_ffn,
              w_in_ab, attn_sinks, s5_lambda_re, s5_lambda_im, s5_log_step,
              s5_b_re, s5_b_im, s5_c_re, s5_c_im, s5_d, w_glu, w_out_ab,
              w_pw1, b_pw1, w_dw, b_dw, conv_ln_g, conv_ln_b, w_pw2, b_pw2,
              w_ffn_gate, w_ffn_up, w_ffn_down):
    xp, xs = x_prompt, x_sample
    nP, nS = xp.shape[0], xs.shape[0]
    kp_l, vp_l, srp_l, sip_l, cp_l = [], [], [], [], []
    ks_l, vs_l, srs_l, sis_l, cs_l = [], [], [], [], []
    for layer in range(DEPTH):
        hp = rms_norm(xp, norm_pre_mix[layer])
        hs = rms_norm(xs, norm_pre_mix[layer])
        if layer % 2 == 0:
            e = layer // 2
            ssm = (s5_lambda_re[e], s5_lambda_im[e], s5_log_step[e], s5_b_re[e], s5_b_im[e],
                   s5_c_re[e], s5_c_im[e], s5_d[e])
            zeros_h = jnp.zeros((nP, SSM_GROUPS, SSM_STATE), F32)
            mp, kp, vp, srp, sip = ab_mixer(hp, None, None, zeros_h, zeros_h, w_in_ab[e], w_out_ab[e],
                                            attn_sinks[e], ssm, w_glu[e])
            ms, kS, vS, srs, sis = ab_mixer(hs, cache_k[e], cache_v[e], state_s5_re[e], state_s5_im[e],
                                            w_in_ab[e], w_out_ab[e], attn_sinks[e], ssm, w_glu[e])
            kp_l.append(kp); vp_l.append(vp); srp_l.append(srp); sip_l.append(sip)
            ks_l.append(kS); vs_l.append(vS); srs_l.append(srs); sis_l.append(sis)
        else:
            o = layer // 2
            cparams = (w_pw1[o], b_pw1[o], w_dw[o], b_dw[o], conv_ln_g[o], conv_ln_b[o], w_pw2[o], b_pw2[o])
            prefix0 = jnp.zeros((nP, CONV_WIDTH - 1, D_CONV), hp.dtype)
            mp, cp = conv_module(hp, prefix0, *cparams)
            ms, cS = conv_module(hs, state_conv[o], *cparams)
            cp_l.append(cp); cs_l.append(cS)
        xp = xp + rms_norm(mp, norm_post_mix[layer])
        xs = xs + rms_norm(ms, norm_post_mix[layer])
        fp = swiglu(rms_norm(xp, norm_pre_ffn[layer]), w_ffn_gate[layer], w_ffn_up[layer], w_ffn_down[layer])
        fs = swiglu(rms_norm(xs, norm_pre_ffn[layer]), w_ffn_gate[layer], w_ffn_up[layer], w_ffn_down[layer])
        xp = xp + rms_norm(fp, norm_post_ffn[layer])
        xs = xs + rms_norm(fs, norm_post_ffn[layer])
    k_prompt = jnp.stack(kp_l)
    v_prompt = jnp.stack(vp_l)
    s5_re_prompt = jnp.stack(srp_l)
    s5_im_prompt = jnp.stack(sip_l)
    conv_prompt = jnp.stack(cp_l)
    k_sample = jnp.stack(ks_l)
    v_sample = jnp.stack(vs_l)
    s5_re_sample = jnp.stack(srs_l)
    s5_im_sample = jnp.stack(sis_l)
    conv_sample = jnp.stack(cs_l)
    return (xp, xs, k_prompt, v_prompt, s5_re_prompt, s5_im_prompt, conv_prompt,
            k_sample, v_sample, s5_re_sample, s5_im_sample, conv_sample)
```

```python
import numpy as np
from contextlib import ExitStack
import concourse.bass as bass
import concourse.mybir as mybir
from concourse.bass_utils import run_bass_kernel_spmd

AF = mybir.ActivationFunctionType
ALU = mybir.AluOpType
F32 = mybir.dt.float32
BF16 = mybir.dt.bfloat16
I32 = mybir.dt.int32

STAGE = 99
SUB = 99
NCORES = 8
SAME_ENG_SYNC = True
PI = float(np.pi)


class Tk:
    __slots__ = ("name", "w", "r", "x")

    def __init__(self, name="", x=False):
        self.name = name
        self.w = []
        self.r = {}
        self.x = x


class Eng:
    def __init__(self, name, obj, sem, unit=1):
        self.name = name
        self.obj = obj
        self.sem = sem
        self.unit = unit
        self.count = 0
        self.seen = {}


class Prog:
    NSLOT = 8

    def __init__(self, nc, stack):
        self.nc = nc
        mk = lambda n: stack.enter_context(nc.semaphore(n))
        self.pe = Eng("pe", nc.tensor, mk("s_pe"))
        self.act = Eng("act", nc.scalar, mk("s_act"))
        self.dve = Eng("dve", nc.vector, mk("s_dve"))
        self.pool = Eng("pool", nc.gpsimd, mk("s_pool"))
        self.sp = Eng("sp", nc.sync, None)
        self.compute = [self.pe, self.act, self.dve, self.pool]
        self.slots = {}
        self.slot_i = {}
        for q in (self.sp, self.pool):
            self.slots[q.name] = [Eng("d_%s%d" % (q.name, i), None, mk("s_d%s%d" % (q.name, i)), 16)
                                  for i in range(self.NSLOT)]
            self.slot_i[q.name] = 0
        self.n_inst = 0

    def _wait(self, eng, dep, cnt):
        if eng.seen.get(dep, 0) >= cnt:
            return
        eng.obj.wait_ge(dep.sem, cnt * dep.unit)
        eng.seen[dep] = cnt

    def _deps(self, eng, reads, writes, part=False):
        deps = {}

        def add(e, c):
            if deps.get(e, 0) < c:
                deps[e] = c
        reads, xr = [t for t in reads if not t.x], [t for t in reads if t.x]
        writes = list(writes) + xr
        for t in reads:
            for e, c in t.w:
                add(e, c)
        for t in writes:
            if not part:
                for e, c in t.w:
                    add(e, c)
            for e, c in t.r.items():
                add(e, c)
        for e, c in deps.items():
            if e is eng and (eng is self.pe or not SAME_ENG_SYNC):
                continue
            self._wait(eng, e, c)

    def _mark(self, eng, reads, writes, part=False):
        writes = list(writes) + [t for t in reads if t.x]
        for t in reads:
            if not t.x:
                t.r[eng] = eng.count
        for t in writes:
            if part:
                t.w = [(e, c) for e, c in t.w if e is not eng] + [(eng, eng.count)]
            else:
                t.w = [(eng, eng.count)]
                t.r = {}

    def op(self, eng, fn, reads=(), writes=()):
        self._deps(eng, reads, writes)
        ins = fn()
        eng.count += 1
        ins.then_inc(eng.sem, 1)
        self._mark(eng, reads, writes)
        self.n_inst += 1

    def mm(self, fns, reads=(), writes=()):
        eng = self.pe
        self._deps(eng, reads, writes)
        ins = None
        for fn in fns:
            ins = fn()
            self.n_inst += 1
        eng.count += 1
        ins.then_inc(eng.sem, 1)
        self._mark(eng, reads, writes)

    def dma(self, q, out, in_, reads=(), writes=(), part=False, **kw):
        self._deps(q, reads, writes, part)
        sl = self.slots[q.name]
        i = self.slot_i[q.name]
        self.slot_i[q.name] = (i + 1) % len(sl)
        s = sl[i]
        if s.count > 0:
            self._wait(q, s, s.count)
        ins = q.obj.dma_start(out=out, in_=in_, **kw)
        s.count += 1
        ins.then_inc(s.sem, 16)
        self._mark(s, reads, writes, part)
        self.n_inst += 1

    def barrier(self):
        allsl = [s for v in self.slots.values() for s in v if s.count > 0]
        for e in self.compute + [self.sp]:
            for o in self.compute:
                if o is not e and o.count > 0:
                    self._wait(e, o, o.count)
            for s in allsl:
                self._wait(e, s, s.count)

    def finish(self):
        allsl = [s for v in self.slots.values() for s in v if s.count > 0]
        for o in self.compute:
            if o.count > 0:
                self._wait(self.sp, o, o.count)
        for s in allsl:
            self._wait(self.sp, s, s.count)


NT = 17
NTOK = 2112
WIDE = [(0, 512), (512, 512), (1024, 512), (1536, 512), (2048, 64)]
GROUPS = [list(range(0, 6)), list(range(6, 12)), list(range(12, 17))]
EPS = 1e-6
LN_EPS = 1e-5


def rows(i):
    return 128 if i < 16 else 64


def tiles_of(c0, n):
    return [i for i in range(NT) if i * 128 >= c0 and i * 128 < c0 + n]


def build_nc():
    nc = bass.Bass("TRN2", target_bir_lowering=False)
    di = lambda n, shp: nc.dram_tensor(n, shp, F32, kind="ExternalInput").ap()
    do = lambda n, shp: nc.dram_tensor(n, shp, F32, kind="ExternalOutput").ap()
    xin = di("xin", [NTOK, 1024]); ck = di("ck", [16, 128, 128]); cv = di("cv", [16, 128, 128])
    s5re0 = di("s5re0", [16, 2048]); s5im0 = di("s5im0", [16, 2048]); sconv = di("sconv", [16, 30, 1024])
    npm = di("npm", [2, 1024]); npo = di("npo", [2, 1024]); nfp = di("nfp", [2, 1024]); nfo = di("nfo", [2, 1024])
    win = di("win", [1024, 1536]); sinks = di("sinks", [1, 8])
    lre_d = di("lre", [2048]); lim_d = di("lim", [2048]); lst_d = di("lst", [32])
    bre_d = di("bre", [2048, 16]); bim_d = di("bim", [2048, 16]); cre_d = di("cre", [512, 64]); cim_d = di("cim", [512, 64])
    dsk_d = di("dsk", [512]); wglu_d = di("wglu", [512, 512]); wout_d = di("wout", [1024, 1024])
    wpw1_d = di("wpw1", [1024, 2048]); bpw1_d = di("bpw1", [2048]); wdw_d = di("wdw", [31, 1024]); bdw_d = di("bdw", [1024])
    lng_d = di("lng", [1024]); lnb_d = di("lnb", [1024]); wpw2_d = di("wpw2", [1024, 1024]); bpw2_d = di("bpw2", [1, 1024])
    wg_d = di("wg", [2, 1024, 2816]); wu_d = di("wu", [2, 1024, 2816]); wd_d = di("wd", [2, 2816, 1024])
    y_o = do("y", [NTOK, 1024]); kp_o = do("kp", [128, 128]); vp_o = do("vp", [128, 128])
    s5rp_o = do("s5rp", [16, 128]); s5ip_o = do("s5ip", [16, 128]); convp_o = do("convp", [30, 1024])
    ks_o = do("ks", [16, 128, 128]); vs_o = do("vs", [16, 128, 128])
    s5rs_o = do("s5rs", [16, 2048]); s5is_o = do("s5is", [16, 2048]); convs_o = do("convs", [16, 30, 1024])

    with ExitStack() as st:
        P = Prog(nc, st)
        V = lambda fn, r=(), w=(): P.op(P.dve, fn, r, w)
        A = lambda fn, r=(), w=(): P.op(P.act, fn, r, w)
        G = lambda fn, r=(), w=(): P.op(P.pool, fn, r, w)
        vec, act, pe = nc.vector, nc.scalar, nc.tensor

        uid = [0]

        def sb(stk, n, shp, dt):
            uid[0] += 1
            return stk.enter_context(nc.sbuf_tensor("%s_%d" % (n, uid[0]), shp, dt))

        banks = [(st.enter_context(nc.psum_tensor("psA%d" % i, [128, 512], F32)), Tk("psA%d" % i, True)) for i in range(6)]
        tbanks = [(st.enter_context(nc.psum_tensor("psT%d" % i, [128, 1024], BF16)), Tk("psT%d" % i, True)) for i in range(2)]
        bi = [0, 0]

        def nb():
            bi[0] = (bi[0] + 1) % 6
            return banks[bi[0]]

        def ntb():
            bi[1] = (bi[1] + 1) % 2
            return tbanks[bi[1]]

        X = sb(st, "X", [128, NT, 1024], F32)
        kX = [Tk("X%d" % i) for i in range(NT)]
        io = sb(st, "io", [128, 128], I32); k_io = Tk()
        idb = sb(st, "idb", [128, 128], BF16); idf = sb(st, "idf", [128, 128], F32); k_id = Tk()
        onesb = sb(st, "onesb", [128, 128], BF16); onesln = sb(st, "onesln", [128, 128], BF16)
        maskP = sb(st, "maskP", [128, 128], BF16); maskC = sb(st, "maskC", [128, 128], BF16)
        maskN = sb(st, "maskN", [64, 64], BF16); tm4 = sb(st, "tm4", [64, 64], I32); mtmp = sb(st, "mtmp", [64, 64], BF16)
        k_const = Tk()
        small = sb(st, "small", [128, 64], F32)
        k_small = Tk()

        G(lambda: nc.gpsimd.iota(io[:], pattern=[[1, 128]], base=0, channel_multiplier=-1), w=[k_io])
        G(lambda: nc.gpsimd.iota(tm4[:], pattern=[[0, 16], [1, 4]], base=0, channel_multiplier=0), w=[k_io])
        V(lambda: vec.tensor_single_scalar(idb[:], io[:], 0, ALU.is_equal), [k_io], [k_id])
        V(lambda: vec.tensor_single_scalar(idf[:], io[:], 0, ALU.is_equal), [k_io], [k_id])
        V(lambda: vec.memset(onesb[:], 1.0), w=[k_const])
        V(lambda: vec.memset(onesln[:], 1.0 / 1024.0), w=[k_const])
        V(lambda: vec.tensor_single_scalar(maskP[:], io[:], 0, ALU.is_lt), [k_io], [k_const])
        V(lambda: vec.tensor_single_scalar(maskC[:], io[:], 0, ALU.is_ge), [k_io], [k_const])
        V(lambda: vec.tensor_single_scalar(maskN[:], io[0:64, 0:64], 0, ALU.is_ge), [k_io], [k_const])
        V(lambda: vec.tensor_tensor(mtmp[:], io[0:64, 0:64], tm4[:], ALU.is_le), [k_io], [k_const])
        V(lambda: vec.tensor_tensor(maskN[:], maskN[:], mtmp[:], ALU.mult), [k_const], [k_const])

        def load_X(tiles):
            for i in tiles:
                r = rows(i)
                P.dma(P.sp, X[0:r, i, :], xin[i * 128:i * 128 + r, :], writes=[kX[i]])
        load_X(range(2))

        def colvec(stk, name, dram_flat, ncol):
            t = sb(stk, name, [128, ncol], F32)
            k = Tk(name)
            with nc.allow_non_contiguous_dma(reason="small param vector"):
                P.dma(P.sp, t[:], dram_flat.rearrange("(c p) -> p c", p=128), writes=[k])
            return t, k

        def load_w(stk, name, dram2d, K, N, n0=0):
            kc = K // 128
            t = sb(stk, name, [128, kc, N], BF16)
            k = Tk(name)
            src = dram2d.rearrange("(c p) n -> p c n", p=128)
            c = 0
            while c < N:
                nbk = min(1024, N - c)
                P.dma(P.pool, t[:, :, c:c + nbk], src[:, :, n0 + c:n0 + c + nbk], writes=[k], part=(c > 0))
                c += nbk
            return t, k

        def rstd_from_ssq(ssq_ap, out_ap, n, scale, eps, k=None):
            k = k or k_small
            A(lambda: act.activation(out_ap, ssq_ap, AF.Sqrt, scale=scale, bias=eps_t[0:n, 0:1] if eps == EPS else lneps_t[0:n, 0:1]),
              [k, k_const], [k])
            V(lambda: vec.reciprocal(out_ap, out_ap), [k], [k])

        k_sm_n = [Tk() for _ in range(4)]
        k_sm_p = [Tk() for _ in range(4)]
        pn_cnt = [0]

        eps_t = sb(st, "eps_t", [128, 1], F32); lneps_t = sb(st, "lneps_t", [128, 1], F32)
        halfpi = sb(st, "halfpi", [128, 1], F32)
        V(lambda: vec.memset(eps_t[:], EPS), w=[k_const])
        V(lambda: vec.memset(lneps_t[:], LN_EPS), w=[k_const])
        V(lambda: vec.memset(halfpi[:], PI / 2), w=[k_const])

        def norm_bufs(stk):
            junk = sb(stk, "nt_junk", [128, 1024], BF16)
            hb = [sb(stk, "nt_hb%d" % j, [128, 1024], BF16) for j in range(2)]
            return (junk, Tk(), hb, [Tk(), Tk()])

        def norm_T(nbufs, tiles, gcol, k_g, hT, k_hT, col0):
            junk, k_junk, hb, k_hb = nbufs
            for n_, i in enumerate(tiles):
                r = rows(i)
                j = n_ % 2
                sl = n_ % 4
                ks = k_sm_n[sl]
                ssq = small[0:r, 16 + 2 * sl:17 + 2 * sl]; rs = small[0:r, 17 + 2 * sl:18 + 2 * sl]
                A(lambda: act.activation(junk[0:r, :], X[0:r, i, :], AF.Square, accum_out=ssq), [kX[i]], [ks])
                rstd_from_ssq(ssq, rs, r, 1.0 / 1024.0, EPS, ks)
                V(lambda: vec.tensor_scalar(hb[j][0:r, :], X[0:r, i, :], rs, None, ALU.mult), [kX[i], ks], [k_hb[j]])
                tb, k_tb = ntb()
                P.mm([lambda c=c: pe.transpose(tb[:, c * 128:c * 128 + r], hb[j][0:r, c * 128:(c + 1) * 128], idb[0:r, 0:r])
                      for c in range(8)], [k_hb[j], k_id], [k_tb])
                c0 = i * 128 - col0
                V(lambda: vec.tensor_tensor(hT[:, :, c0:c0 + r],
                                            tb[:].rearrange("p (c t) -> p c t", c=8)[:, :, 0:r],
                                            gcol[:, :].unsqueeze(2).to_broadcast([128, 8, r]), ALU.mult),
                  [k_tb, k_g], [k_hT[i]])

        def post_norm_residual(stk_tmp_bufs, i, bk, gpost, k_gpost):
            r = rows(i)
            junk, k_junk, tmp, k_tmp = stk_tmp_bufs
            sl = pn_cnt[0] % 4
            pn_cnt[0] += 1
            ks = k_sm_p[sl]
            b0 = 32 + 4 * sl
            for dh in range(2):
                A(lambda dh=dh: act.activation(junk[0:r, :], bk[dh][0][0:r, :], AF.Square, accum_out=small[0:r, b0 + dh:b0 + dh + 1]),
                  [bk[dh][1]], [ks])
            V(lambda: vec.tensor_tensor(small[0:r, b0 + 2:b0 + 3], small[0:r, b0:b0 + 1], small[0:r, b0 + 1:b0 + 2], ALU.add), [ks], [ks])
            rstd_from_ssq(small[0:r, b0 + 2:b0 + 3], small[0:r, b0 + 3:b0 + 4], r, 1.0 / 1024.0, EPS, ks)
            for dh in range(2):
                V(lambda dh=dh: vec.scalar_tensor_tensor(tmp[0:r, :], bk[dh][0][0:r, :], small[0:r, b0 + 3:b0 + 4],
                                                         gpost[0:r, dh * 512:(dh + 1) * 512], ALU.mult, ALU.mult),
                  [bk[dh][1], ks, k_gpost], [k_tmp])
                V(lambda dh=dh: vec.tensor_tensor(X[0:r, i, dh * 512:(dh + 1) * 512], X[0:r, i, dh * 512:(dh + 1) * 512],
                                                  tmp[0:r, :], ALU.add), [k_tmp, kX[i]], [kX[i]])

        def load_gpost(stk, name, dram_row):
            t = sb(stk, name, [128, 1024], F32); k = Tk(name)
            P.dma(P.sp, t[:], dram_row.partition_broadcast(128), writes=[k])
            return t, k

        def ffn(layer):
            with ExitStack() as ph:
                gcol, k_g = colvec(ph, "ffn_g", nfp[layer], 8)
                gpost, k_gpost = load_gpost(ph, "ffn_gpost", nfo[layer])
                wd_sb = sb(ph, "wd_sb", [128, 22, 1024], BF16)
                k_wd = Tk()
                wgu = [(sb(ph, "wg%d" % j, [128, 8, 256], BF16), sb(ph, "wu%d" % j, [128, 8, 256], BF16), Tk()) for j in range(3)]
                hT = sb(ph, "ffn_hT", [128, 8, 768], BF16); k_hT = [Tk() for _ in range(NT)]
                hid = sb(ph, "ffn_hid", [128, 22, 768], BF16); k_hidf = [Tk() for _ in range(22)]
                sg = [sb(ph, "ffn_sg%d" % j, [128, 512], F32) for j in range(2)]; k_sg = [Tk(), Tk()]
                junk = sb(ph, "ffn_junk", [128, 512], BF16); tmp = sb(ph, "ffn_tmp", [128, 512], F32)
                pbufs = (junk, Tk(), tmp, Tk())
                nbufs = norm_bufs(ph)
                wdsrc = wd_d[layer].rearrange("(c p) n -> p c n", p=128)
                wgsrc = wg_d[layer].rearrange("(c p) n -> p c n", p=128)
                wusrc = wu_d[layer].rearrange("(c p) n -> p c n", p=128)
                first = True
                cnt = 0
                norm_T(nbufs, GROUPS[0], gcol, k_g, hT, k_hT, GROUPS[0][0] * 128)
                for gi, grp in enumerate(GROUPS):
                    col0 = grp[0] * 128
                    ncols = sum(rows(i) for i in grp)
                    pieces = []
                    c = 0
                    while c < ncols:
                        n = min(512, ncols - c)
                        pieces.append((c, n))
                        c += n
                    for fg in range(11):
                        wgt, wut, k_w = wgu[cnt % 3]
                        cnt += 1
                        P.dma(P.pool, wgt[:, :, :], wgsrc[:, :, fg * 256:(fg + 1) * 256], writes=[k_w])
                        P.dma(P.pool, wut[:, :, :], wusrc[:, :, fg * 256:(fg + 1) * 256], writes=[k_w], part=True)
                        if first and fg == 2:
                            P.dma(P.pool, wd_sb[:, 0:11, :], wdsrc[:, 0:11, :], writes=[k_wd])
                            P.dma(P.pool, wd_sb[:, 11:22, :], wdsrc[:, 11:22, :], writes=[k_wd], part=True)
                            first = False
                        for (pc, pn) in pieces:
                            kr = [k_hT[i] for i in tiles_of(col0 + pc, pn)]
                            for fc in range(2):
                                f = fg * 2 + fc
                                bg, k_bg = nb()
                                P.mm([lambda k=k: pe.matmul(bg[:, 0:pn], lhsT=wgt[:, k, fc * 128:(fc + 1) * 128], rhs=hT[:, k, pc:pc + pn],
                                                            start=(k == 0), stop=(k == 7)) for k in range(8)], kr + [k_w], [k_bg])
                                bu, k_bu = nb()
                                P.mm([lambda k=k: pe.matmul(bu[:, 0:pn], lhsT=wut[:, k, fc * 128:(fc + 1) * 128], rhs=hT[:, k, pc:pc + pn],
                                                            start=(k == 0), stop=(k == 7)) for k in range(8)], kr + [k_w], [k_bu])
                                j = f % 2
                                A(lambda: act.activation(sg[j][:, 0:pn], bg[:, 0:pn], AF.Silu), [k_bg], [k_sg[j]])
                                V(lambda: vec.tensor_tensor(hid[:, f, pc:pc + pn], sg[j][:, 0:pn], bu[:, 0:pn], ALU.mult),
                                  [k_sg[j], k_bu], [k_hidf[f]])
                    if gi + 1 < len(GROUPS):
                        norm_T(nbufs, GROUPS[gi + 1], gcol, k_g, hT, k_hT, GROUPS[gi + 1][0] * 128)
                    for i in grp:
                        r = rows(i)
                        c0 = i * 128 - col0
                        bk = [nb(), nb()]
                        for dh in range(2):
                            P.mm([lambda f=f: pe.matmul(bk[dh][0][0:r, :], lhsT=hid[:, f, c0:c0 + r], rhs=wd_sb[:, f, dh * 512:(dh + 1) * 512],
                                                        start=(f == 0), stop=(f == 21)) for f in range(22)], k_hidf + [k_wd], [bk[dh][1]])
                        post_norm_residual(pbufs, i, bk, gpost, k_gpost)
                P.barrier()

        with ExitStack() as L0:
            aT = sb(L0, "aT", [128, 4, NTOK], BF16); k_aT = Tk()
            uT = sb(L0, "uT", [128, 4, NTOK], BF16); k_uT = [Tk() for _ in range(5)]
            k_p = Tk()
            lre = sb(L0, "s5_lre", [128, 16], F32); lim = sb(L0, "s5_lim", [128, 16], F32); dtt = sb(L0, "s5_dtt", [128, 16], F32)
            with ExitStack() as Lq:
                qT = sb(Lq, "qT", [128, 4, NTOK], BF16); k_qTm = [[Tk() for _ in range(4)] for _ in range(5)]
                kT2 = sb(Lq, "kT2", [128, 2, NTOK], BF16); k_kTm = [[Tk() for _ in range(2)] for _ in range(5)]
                vtok = sb(Lq, "vtok", [128, NT, 128], BF16); vpad = sb(Lq, "vpad", [128, NT, 2, 128], BF16)
                k_v = [Tk() for _ in range(NT)]
                kvf = sb(Lq, "kvf", [128, 2, 256], F32); k_kvf = Tk()
                V(lambda: vec.memset(vpad[:], 0.0), w=k_v)
                with ExitStack() as ph:
                    gcol, k_g = colvec(ph, "g_pm0", npm[0], 8)
                    load_X(range(2, NT))
                    win_sb, k_win = load_w(ph, "win_sb", win, 1024, 1536)
                    hT = sb(ph, "hT", [128, 8, NTOK], BF16); k_hT = [Tk() for _ in range(NT)]
                    with nc.allow_non_contiguous_dma(reason="s5 params"):
                        P.dma(P.sp, lre[:], lre_d.rearrange("(j q) -> q j", q=128), writes=[k_p])
                        P.dma(P.sp, lim[:], lim_d.rearrange("(j q) -> q j", q=128), writes=[k_p])
                        lst2 = lst_d.rearrange("(j t) -> t j", t=2)
                        for g2 in range(2):
                            P.dma(P.sp, dtt[g2 * 64:(g2 + 1) * 64, :], lst2[g2:g2 + 1, :].to_broadcast([64, 16]), writes=[k_p])
                    norm_T(norm_bufs(ph), range(NT), gcol, k_g, hT, k_hT, 0)
                    for w, (c0, n) in enumerate(WIDE if SUB >= 2 else []):
                        kr = [k_hT[i] for i in tiles_of(c0, n)] + [k_win]
                        for m in range(10):
                            woff = m * 128 if m < 4 else (512 + (m - 4) * 128 if m < 6 else 1024 + (m - 6) * 128)
                            b, k_b = nb()
                            P.mm([lambda k=k: pe.matmul(b[:, 0:n], lhsT=win_sb[:, k, woff:woff + 128], rhs=hT[:, k, c0:c0 + n],
                                                        start=(k == 0), stop=(k == 7)) for k in range(8)], kr, [k_b])
                            if m < 4:
                                A(lambda: act.activation(qT[:, m, c0:c0 + n], b[:, 0:n], AF.Copy, scale=0.125), [k_b], [k_qTm[w][m]])
                            elif m < 6:
                                V(lambda: vec.tensor_copy(kT2[:, m - 4, c0:c0 + n], b[:, 0:n]), [k_b], [k_kTm[w][m - 4]])
                            else:
                                A(lambda: act.copy(uT[:, m - 6, c0:c0 + n], b[:, 0:n]), [k_b], [k_uT[w]])
                    for i in (range(NT) if SUB >= 3 else []):
                        r = rows(i)
                        b, k_b = nb()
                        P.mm([lambda k=k: pe.matmul(b[0:r, 0:256], lhsT=hT[:, k, i * 128:i * 128 + r], rhs=win_sb[:, k, 768:1024],
                                                    start=(k == 0), stop=(k == 7)) for k in range(8)], [k_hT[i], k_win], [k_b])
                        V(lambda: vec.tensor_copy(vtok[0:r, i, :], b[0:r, 128:256]), [k_b], [k_v[i]])
                        for g_ in (range(2) if SUB >= 5 else []):
                            A(lambda g_=g_: act.copy(vpad[0:r, i, g_, 64:128], b[0:r, 128 + g_ * 64:192 + g_ * 64]), [k_b], [k_v[i]])
                        if i >= 15 and SUB >= 6:
                            V(lambda: vec.tensor_copy(kvf[0:r, i - 15, :], b[0:r, 0:256]), [k_b], [k_kvf])
                    if SUB >= 8:
                      P.dma(P.sp, kp_o[:, :], kvf[:, 0, 0:128], reads=[k_kvf])
                      P.dma(P.sp, vp_o[:, :], kvf[:, 0, 128:256], reads=[k_kvf])
                    for m in (range(16) if SUB >= 7 else []):
                        P.dma(P.sp, ks_o[m, 124:128, :], kvf[4 * m:4 * m + 4, 1, 0:128], reads=[k_kvf])
                        P.dma(P.sp, vs_o[m, 124:128, :], kvf[4 * m:4 * m + 4, 1, 128:256], reads=[k_kvf])
                    P.dma(P.sp, ks_o[:, 0:124, :], ck[:, 4:128, :])
                    P.dma(P.sp, vs_o[:, 0:124, :], cv[:, 4:128, :])
                    P.barrier()
                if STAGE >= 2:
                    with ExitStack() as ph:
                        sk = sb(ph, "sk", [128, 8], F32); k_sk = Tk()
                        P.dma(P.sp, sk[:], sinks[0:1, :].to_broadcast([128, 8]), writes=[k_sk])
                        A(lambda: act.activation(sk[:], sk[:], AF.Exp), [k_sk], [k_sk])
                        PT = [sb(ph, "PT%d" % j, [128, 2, 128], BF16) for j in range(8)]; k_PT = [Tk() for _ in range(8)]
                        dn2 = [sb(ph, "dn%d" % i_, [128, 2, 128], F32) for i_ in range(2)]; k_dn2 = [Tk(), Tk()]; ndn = [0]

                        def normalize(R, g, par, O, k_O, Dn, k_Dn, c0, n):
                            h0 = 4 * g + par
                            dnb, k_dnb = dn2[ndn[0] % 2], k_dn2[ndn[0] % 2]
                            ndn[0] += 1
                            for ci in range(2):
                                A(lambda ci=ci: act.activation(dnb[R, ci, 0:n], Dn[:, ci, :], AF.Ln, bias=sk[R, h0 + 2 * ci:h0 + 2 * ci + 1]),
                                  [k_Dn, k_sk], [k_dnb])
                            A(lambda: act.activation(dnb[R, :, 0:n], dnb[R, :, 0:n], AF.Exp, scale=-1.0), [k_dnb], [k_dnb])
                            V(lambda: vec.tensor_tensor(aT[R, 2 * g:2 * g + 2, c0:c0 + n], O, dnb[R, :, 0:n], ALU.mult), [k_O, k_dnb], [k_aT])

                        units = [(blk, g, par) for blk in range(16) for g in range(2) for par in range(2)]

                        def stage_a(u):
                            blk, g, par = u
                            w, c0 = blk // 4, blk * 128
                            R = slice(par * 64, par * 64 + 64)
                            pts = []
                            for kb in ([blk - 1] if blk > 0 else []) + [blk]:
                                S, k_S = nb()
                                P.mm([lambda: pe.matmul(S[:, 0:256], lhsT=kT2[R, g, kb * 128:(kb + 1) * 128],
                                                        rhs=qT[R, 2 * g:2 * g + 2, c0:c0 + 128], start=True, stop=True)],
                                     [k_kTm[kb // 4][g], k_qTm[w][2 * g], k_qTm[w][2 * g + 1]], [k_S])
                                j = pti[0] % 8
                                pti[0] += 1
                                A(lambda: act.activation(PT[j][:].rearrange("p a t -> p (a t)"), S[:, 0:256], AF.Exp), [k_S], [k_PT[j]])
                                msk = maskC if kb == blk else maskP
                                V(lambda: vec.tensor_tensor(PT[j][:], PT[j][:], msk[:].unsqueeze(1).to_broadcast([128, 2, 128]), ALU.mult),
                                  [k_PT[j], k_const], [k_PT[j]])
                                pts.append((j, kb))
                            return pts

                        def stage_b(u, pts):
                            blk, g, par = u
                            c0 = blk * 128
                            R = slice(par * 64, par * 64 + 64)
                            O, k_O = nb()
                            fns = []
                            for ci in range(2):
                                for n_, (j, kb) in enumerate(pts):
                                    if par == 0:
                                        fns.append(lambda ci=ci, j=j, kb=kb, n_=n_: pe.matmul(
                                            O[0:64, ci * 128:(ci + 1) * 128], lhsT=vtok[:, kb, g * 64:(g + 1) * 64], rhs=PT[j][:, ci, :],
                                            start=(n_ == 0), stop=(n_ == len(pts) - 1)))
                                    else:
                                        fns.append(lambda ci=ci, j=j, kb=kb, n_=n_: pe.matmul(
                                            O[:, ci * 128:(ci + 1) * 128], lhsT=vpad[:, kb, g, :], rhs=PT[j][:, ci, :],
                                            start=(n_ == 0), stop=(n_ == len(pts) - 1)))
                            P.mm(fns, [k_PT[j] for j, _ in pts] + [k_v[kb] for _, kb in pts], [k_O])
                            Dn, k_Dn = nb()
                            P.mm([lambda j=j, n_=n_: pe.matmul(Dn[:, 0:256], lhsT=onesb[:], rhs=PT[j][:].rearrange("p a t -> p (a t)"),
                                                                start=(n_ == 0), stop=(n_ == len(pts) - 1)) for n_, (j, kb) in enumerate(pts)],
                                 [k_PT[j] for j, _ in pts] + [k_const], [k_Dn])
                            normalize(R, g, par, O[R, 0:256].rearrange("p (a t) -> p a t", a=2), k_O,
                                      Dn[R, 0:256].rearrange("p (a t) -> p a t", a=2), k_Dn, c0, 128)

                        pti = [0]
                        prev = None
                        for u in units:
                            pts_u = stage_a(u)
                            if prev is not None:
                                stage_b(*prev)
                            prev = (u, pts_u)
                        stage_b(*prev)
                        kcd = sb(ph, "kcd", [128, 16, 2, 2, 64], BF16); k_kcd = Tk()
                        vc = sb(ph, "vc", [128, 16, 128], BF16); vcp = sb(ph, "vcp", [128, 16, 2, 128], BF16); k_vc = Tk()
                        kcT2 = sb(ph, "kcT2", [128, 2, 16, 128], BF16); k_kcT = Tk()
                        V(lambda: vec.memset(vcp[:], 0.0), w=[k_vc])
                        cksrc = ck.rearrange("m s (g d) -> s m g d", g=2)
                        for g in range(2):
                            for dup in range(2):
                                P.dma(P.pool, kcd[:, :, g, dup, :], cksrc[:, :, g, :], writes=[k_kcd])
                        P.dma(P.pool, vc[:], cv.rearrange("m s c -> s m c"), writes=[k_vc])
                        for g in range(2):
                            P.dma(P.pool, vcp[:, :, g, 64:128], cv.rearrange("m s (g d) -> s m g d", g=2)[:, :, g, :], writes=[k_vc])
                        for g in range(2):
                            for half in range(2):
                                tb, k_tb = ntb()
                                P.mm([lambda m=m: pe.transpose(tb[:, (m % 8) * 128:(m % 8 + 1) * 128],
                                                               kcd[:, m, g, :, :].rearrange("p a d -> p (a d)"), idb[:])
                                      for m in range(half * 8, half * 8 + 8)], [k_kcd, k_id], [k_tb])
                                V(lambda: vec.tensor_copy(kcT2[:, g, half * 8:half * 8 + 8, :], tb[:].rearrange("p (m s) -> p m s", m=8)),
                                  [k_tb], [k_kcT])
                        PTc = sb(ph, "PTc", [128, 16, 2, 4], BF16); k_PTc = Tk()
                        PTn = sb(ph, "PTn", [64, 2, 64], BF16); k_PTn = Tk()
                        sc0 = 2048
                        for g in range(2):
                            for par in range(2):
                                R = slice(par * 64, par * 64 + 64)
                                S, k_S = nb()
                                P.mm([lambda m=m: pe.matmul(S[:, m * 8:(m + 1) * 8], lhsT=kcT2[R, g, m, :],
                                                            rhs=qT[R, 2 * g:2 * g + 2, sc0 + 4 * m:sc0 + 4 * m + 4], start=True, stop=True)
                                      for m in range(16)], [k_kcT, k_qTm[4][2 * g], k_qTm[4][2 * g + 1]], [k_S])
                                A(lambda: act.activation(PTc[:].rearrange("p m a t -> p (m a t)"), S[:, 0:128], AF.Exp), [k_S], [k_PTc])
                                V(lambda: vec.tensor_tensor(PTc[:].rearrange("p m a t -> p (m a) t"), PTc[:].rearrange("p m a t -> p (m a) t"),
                                                            maskP[:, 0:4].unsqueeze(1).to_broadcast([128, 32, 4]), ALU.mult),
                                  [k_PTc, k_const], [k_PTc])
                                S2, k_S2 = nb()
                                P.mm([lambda: pe.matmul(S2[0:64, 0:128], lhsT=kT2[R, g, sc0:sc0 + 64], rhs=qT[R, 2 * g:2 * g + 2, sc0:sc0 + 64],
                                                        start=True, stop=True)], [k_kTm[4][g], k_qTm[4][2 * g], k_qTm[4][2 * g + 1]], [k_S2])
                                A(lambda: act.activation(PTn[:].rearrange("p a t -> p (a t)"), S2[0:64, 0:128], AF.Exp), [k_S2], [k_PTn])
                                V(lambda: vec.tensor_tensor(PTn[:], PTn[:], maskN[:].unsqueeze(1).to_broadcast([64, 2, 64]), ALU.mult),
                                  [k_PTn, k_const], [k_PTn])
                                O, k_O = nb()
                                Ov = O[:, 0:128].rearrange("p (a t) -> p a t", a=2)
                                fns = []
                                if par == 0:
                                    fns.append(lambda: pe.matmul(O[0:64, 0:128], lhsT=vtok[0:64, 16, g * 64:(g + 1) * 64],
                                                                 rhs=PTn[:].rearrange("p a t -> p (a t)"), start=True, stop=False))
                                    for m in range(16):
                                        fns.append(lambda m=m: pe.matmul(Ov[0:64, :, 4 * m:4 * m + 4], lhsT=vc[:, m, g * 64:(g + 1) * 64],
                                                                         rhs=PTc[:, m, :, :], start=False, stop=(m == 15), skip_group_check=True))
                                else:
                                    fns.append(lambda: pe.matmul(O[:, 0:128], lhsT=vpad[0:64, 16, g, :],
                                                                 rhs=PTn[:].rearrange("p a t -> p (a t)"), start=True, stop=False))
                                    for m in range(16):
                                        fns.append(lambda m=m: pe.matmul(Ov[:, :, 4 * m:4 * m + 4], lhsT=vcp[:, m, g, :],
                                                                         rhs=PTc[:, m, :, :], start=False, stop=(m == 15), skip_group_check=True))
                                P.mm(fns, [k_PTn, k_PTc, k_vc, k_v[16]], [k_O])
                                Dn, k_Dn = nb()
                                Dv = Dn[:, 0:128].rearrange("p (a t) -> p a t", a=2)
                                fns = [lambda: pe.matmul(Dn[:, 0:128], lhsT=onesb[0:64, :], rhs=PTn[:].rearrange("p a t -> p (a t)"),
                                                         start=True, stop=False)]
                                for m in range(16):
                                    fns.append(lambda m=m: pe.matmul(Dv[:, :, 4 * m:4 * m + 4], lhsT=onesb[:], rhs=PTc[:, m, :, :],
                                                                     start=False, stop=(m == 15), skip_group_check=True))
                                P.mm(fns, [k_PTn, k_PTc, k_const], [k_Dn])
                                normalize(R, g, par, Ov[R], k_O, Dv[R], k_Dn, sc0, 64)
                        P.barrier()
            if STAGE >= 3:
                gT = sb(L0, "gT", [128, 4, NTOK], BF16); k_gT = Tk()
                with ExitStack() as ph:
                    pt = lambda n: sb(ph, "s5_" + n, [128, 16], F32)
                    mag = pt("mag"); th = pt("th"); kf = pt("kf")
                    sn = pt("sn"); cs = pt("cs"); ab = pt("ab"); lbr = pt("lbr"); lbi = pt("lbi"); den = pt("den")
                    a1 = pt("a1"); cfr = pt("cfr"); cfi = pt("cfi"); t0 = pt("t0"); t1s = pt("t1s")
                    C256 = pt("C256"); S256 = pt("S256"); Enc = pt("Enc"); Ens = pt("Ens")
                    ki = sb(ph, "ki", [128, 16], I32)
                    TT = lambda o, a, b, op: V(lambda: vec.tensor_tensor(o, a, b, op), [k_p], [k_p])
                    A(lambda: act.activation(dtt[:], dtt[:], AF.Exp), [k_p], [k_p])
                    TT(t0[:], lre[:], dtt[:], ALU.mult)
                    A(lambda: act.activation(mag[:], t0[:], AF.Exp), [k_p], [k_p])
                    TT(th[:], lim[:], dtt[:], ALU.mult)
                    A(lambda: act.activation(ki[:], th[:], AF.Copy, scale=1.0 / (2 * PI)), [k_p], [k_p])
                    V(lambda: vec.tensor_copy(kf[:], ki[:]), [k_p], [k_p])
                    V(lambda: vec.scalar_tensor_tensor(th[:], kf[:], -2 * PI, th[:], ALU.mult, ALU.add), [k_p], [k_p])
                    V(lambda: vec.tensor_scalar(th[:], th[:], PI, -PI, ALU.min, ALU.max), [k_p], [k_p])
                    A(lambda: act.activation(sn[:], th[:], AF.Sin), [k_p], [k_p])
                    A(lambda: act.activation(ab[:], th[:], AF.Abs), [k_p], [k_p])
                    A(lambda: act.activation(cs[:], ab[:], AF.Sin, scale=-1.0, bias=halfpi[:, 0:1]), [k_p, k_const], [k_p])
                    TT(lbr[:], mag[:], cs[:], ALU.mult); TT(lbi[:], mag[:], sn[:], ALU.mult)
                    TT(den[:], lre[:], lre[:], ALU.mult); TT(t0[:], lim[:], lim[:], ALU.mult); TT(den[:], den[:], t0[:], ALU.add)
                    V(lambda: vec.reciprocal(den[:], den[:]), [k_p], [k_p])
                    V(lambda: vec.tensor_scalar(a1[:], lbr[:], -1.0, None, ALU.add), [k_p], [k_p])
                    TT(t0[:], a1[:], lre[:], ALU.mult); TT(t1s[:], lbi[:], lim[:], ALU.mult); TT(t0[:], t0[:], t1s[:], ALU.add)
                    TT(cfr[:], t0[:], den[:], ALU.mult)
                    TT(t0[:], lbi[:], lre[:], ALU.mult); TT(t1s[:], a1[:], lim[:], ALU.mult); TT(t0[:], t0[:], t1s[:], ALU.subtract)
                    TT(cfi[:], t0[:], den[:], ALU.mult)
                    Bpr = sb(ph, "Bpr", [128, 16, 128], BF16); Bpi = sb(ph, "Bpi", [128, 16, 128], BF16); k_Bp = Tk()
                    Cpr = sb(ph, "Cpr", [128, 16, 128], BF16); Cpi = sb(ph, "Cpi", [128, 16, 128], BF16); k_Cp = Tk()
                    Cnr = sb(ph, "Cnr", [128, 16, 128], BF16)
                    Dd = sb(ph, "Dd", [128, 4, 128], BF16); k_Dd = Tk()
                    ysb = sb(ph, "ysb", [128, 512], F32); y2 = sb(ph, "y2", [128, 512], F32); k_ge = Tk(); k_ysb = Tk()
                    k_gTq = [k_gT, Tk(), Tk(), Tk()]
                    pp = ExitStack()
                    cosT = sb(pp, "cosT", [128, 16, 256], F32); sinT = sb(pp, "sinT", [128, 16, 256], F32); k_E = Tk()
                    with ExitStack() as ph2:
                        bre_t = sb(ph2, "bre_t", [128, 16, 16], F32); bim_t = sb(ph2, "bim_t", [128, 16, 16], F32); k_bb = Tk()
                        P.dma(P.sp, bre_t[:], bre_d.rearrange("(j q) c -> q j c", q=128), writes=[k_bb])
                        P.dma(P.sp, bim_t[:], bim_d.rearrange("(j q) c -> q j c", q=128), writes=[k_bb], part=True)
                        bbr = sb(ph2, "bbr", [128, 16, 16], F32); bbi = sb(ph2, "bbi", [128, 16, 16], F32); tb1 = sb(ph2, "tb1", [128, 16, 16], F32)
                        BTr = sb(ph2, "BTr", [128, 16, 128], F32); BTi = sb(ph2, "BTi", [128, 16, 128], F32)
                        CN = sb(ph2, "CN", [128, 1, 4, 128], F32); CT = sb(ph2, "CT", [128, 2, 4, 128], F32)
                        crb = cfr[:, :].unsqueeze(2).to_broadcast([128, 16, 16]); cib = cfi[:, :].unsqueeze(2).to_broadcast([128, 16, 16])
                        V(lambda: vec.tensor_tensor(bbr[:], bre_t[:], crb, ALU.mult), [k_p, k_bb], [k_p])
                        TT(tb1[:], bim_t[:], cib, ALU.mult); TT(bbr[:], bbr[:], tb1[:], ALU.subtract)
                        TT(bbi[:], bim_t[:], crb, ALU.mult); TT(tb1[:], bre_t[:], cib, ALU.mult); TT(bbi[:], bbi[:], tb1[:], ALU.add)
                        V(lambda: vec.memset(BTr[:], 0.0), [k_p], [k_p]); V(lambda: vec.memset(BTi[:], 0.0), [k_p], [k_p])
                        for g2 in range(2):
                            H = slice(g2 * 64, g2 * 64 + 64)
                            for b_ in range(4):
                                cs_ = slice((2 * b_ + g2) * 16, (2 * b_ + g2) * 16 + 16)
                                V(lambda: vec.tensor_copy(BTr[H, b_::4, cs_], bbr[H, b_::4, :]), [k_p], [k_p])
                                V(lambda: vec.tensor_copy(BTi[H, b_::4, cs_], bbi[H, b_::4, :]), [k_p], [k_p])
                        for (BT, Bp) in ((BTr, Bpr), (BTi, Bpi)):
                            for jj in range(4):
                                bk, k_bk = nb()
                                P.mm([lambda j=j: pe.transpose(bk[:, (j % 4) * 128:(j % 4 + 1) * 128], BT[:, j, :], idf[:])
                                      for j in range(4 * jj, 4 * jj + 4)], [k_p, k_id], [k_bk])
                                A(lambda: act.copy(Bp[:, 4 * jj:4 * jj + 4, :].rearrange("p a b -> p (a b)"), bk[:, :]), [k_bk], [k_Bp])
                        for ri, cd in enumerate((cre_d, cim_d)):
                            src = cd.rearrange("(i q) p -> q i p", q=128)
                            P.dma(P.sp, CN[:, 0, :, 0:64], src, writes=[k_p])
                            P.dma(P.sp, CN[:, 0, :, 64:128], src, writes=[k_p])
                            bk, k_bk = nb()
                            P.mm([lambda i=i: pe.transpose(bk[:, i * 128:(i + 1) * 128], CN[:, 0, i, :], idf[:]) for i in range(4)],
                                 [k_p, k_id], [k_bk])
                            A(lambda: act.copy(CT[:, ri, :, :].rearrange("p a b -> p (a b)"), bk[:, :]), [k_bk], [k_p])
                        V(lambda: vec.memset(Cpr[:], 0.0), w=[k_Cp]); V(lambda: vec.memset(Cpi[:], 0.0), w=[k_Cp])
                        for g2 in range(2):
                            H = slice(g2 * 64, g2 * 64 + 64)
                            for b_ in range(4):
                                cs_ = slice((2 * b_ + g2) * 16, (2 * b_ + g2) * 16 + 16)
                                V(lambda: vec.tensor_copy(Cpr[H, b_::4, cs_], CT[H, 0, :, cs_]), [k_p], [k_Cp])
                                V(lambda: vec.tensor_scalar(Cpi[H, b_::4, cs_], CT[H, 1, :, cs_], -1.0, None, ALU.mult), [k_p], [k_Cp])
                        V(lambda: vec.tensor_scalar(Cnr[:], Cpr[:], -1.0, None, ALU.mult), [k_Cp], [k_Cp])
                        dT, k_dT = colvec(ph2, "dT", dsk_d, 4)
                        for i in range(4):
                            V(lambda: vec.tensor_scalar(Dd[:, i, :], idf[:], dT[:, i:i + 1], None, ALU.mult), [k_dT, k_id], [k_Dd])
                        P.barrier()
                        T1, T2 = BTr, BTi
                        V(lambda: vec.memset(cosT[:, :, 0:1], 1.0), w=[k_E]); V(lambda: vec.memset(sinT[:, :, 0:1], 0.0), w=[k_E])
                        TE = lambda o, a, b, op: V(lambda: vec.tensor_tensor(o, a, b, op), [k_E, k_p], [k_E])
                        n = 1
                        while n <= 256:
                            if n == 1:
                                V(lambda: vec.tensor_copy(Enc[:], cs[:]), [k_p], [k_p]); V(lambda: vec.tensor_copy(Ens[:], sn[:]), [k_p], [k_p])
                            else:
                                TE(t0[:], cosT[:, :, n - 1], cs[:], ALU.mult); TE(t1s[:], sinT[:, :, n - 1], sn[:], ALU.mult)
                                TE(Enc[:], t0[:], t1s[:], ALU.subtract)
                                TE(t0[:], sinT[:, :, n - 1], cs[:], ALU.mult); TE(t1s[:], cosT[:, :, n - 1], sn[:], ALU.mult)
                                TE(Ens[:], t0[:], t1s[:], ALU.add)
                            if n == 256:
                                V(lambda: vec.tensor_copy(C256[:], Enc[:]), [k_p, k_E], [k_p]); V(lambda: vec.tensor_copy(S256[:], Ens[:]), [k_p, k_E], [k_p])
                                break
                            cb_ = Enc[:, :].unsqueeze(2).to_broadcast([128, 16, n]); sb_ = Ens[:, :].unsqueeze(2).to_broadcast([128, 16, n])
                            TE(T1[:, :, 0:n], cosT[:, :, 0:n], cb_, ALU.mult); TE(T2[:, :, 0:n], sinT[:, :, 0:n], sb_, ALU.mult)
                            TE(cosT[:, :, n:2 * n], T1[:, :, 0:n], T2[:, :, 0:n], ALU.subtract)
                            TE(T1[:, :, 0:n], sinT[:, :, 0:n], cb_, ALU.mult); TE(T2[:, :, 0:n], cosT[:, :, 0:n], sb_, ALU.mult)
                            TE(sinT[:, :, n:2 * n], T1[:, :, 0:n], T2[:, :, 0:n], ALU.add)
                            n *= 2
                        P.barrier()

                    def gelu_out(yb, k_yb, q, c0, n):
                        A(lambda: act.copy(ysb[:, 0:n], yb[:, 0:n]), [k_yb], [k_ysb])
                        A(lambda: act.activation(y2[:, 0:n], yb[:, 0:n], AF.Square), [k_yb], [k_ge])
                        V(lambda: vec.tensor_scalar(y2[:, 0:n], y2[:, 0:n], 0.044715, 1.0, ALU.mult, ALU.add), [k_ge], [k_ge])
                        V(lambda: vec.tensor_tensor(y2[:, 0:n], y2[:, 0:n], ysb[:, 0:n], ALU.mult), [k_ge, k_ysb], [k_ge])
                        A(lambda: act.activation(y2[:, 0:n], y2[:, 0:n], AF.Sigmoid, scale=1.5957691216057308), [k_ge], [k_ge])
                        V(lambda: vec.tensor_tensor(gT[:, q, c0:c0 + n], y2[:, 0:n], ysb[:, 0:n], ALU.mult), [k_ge, k_ysb], [k_gTq[q]])

                    W = lambda n_: sb(pp, n_, [128, 512], F32)
                    xre = W("xre"); xim = W("xim"); w1 = W("w1"); w2 = W("w2"); w3 = W("w3"); w4 = W("w4")
                    vre2 = [W("vre0"), W("vre1")]; vim2 = [W("vim0"), W("vim1")]
                    hh2 = [sb(pp, "hh%d" % i_, [128, 4, 512], BF16) for i_ in range(2)]
                    pend = []
                    k_x = Tk(); k_vv2 = [Tk(), Tk()]; k_h2 = [Tk(), Tk()]; k_pw = Tk()
                    k_w = [Tk() for _ in range(4)]; k_xr = Tk(); k_xi = Tk()
                    k_vr2 = [Tk(), Tk()]; k_vi2 = [Tk(), Tk()]; k_cr = Tk(); k_ci = Tk(); k_t0 = Tk(); k_t1 = Tk()
                    gp_ = nc.gpsimd
                    car = sb(pp, "car", [128, 16, 2], F32); ctmp = sb(pp, "ctmp", [128, 2], F32); k_car = Tk()
                    fin = sb(pp, "fin", [128, 2, 16], F32); k_fin = Tk()
                    v3 = lambda t: t[:, :].rearrange("p (a t) -> p a t", a=2)
                    def emit_bu(w_, j_):
                        for ri_, Bp_ in enumerate((Bpr, Bpi)):
                            bk_, k_bk_ = banks[(2 * j_ + ri_) % 4]
                            P.mm([lambda: pe.matmul(bk_[:, :], lhsT=Bp_[:, j_, :], rhs=uT[:, j_ // 4, w_ * 512:w_ * 512 + 512], start=True, stop=True)],
                                 [k_Bp, k_uT[w_]], [k_bk_])

                    emit_bu(0, 0)
                    for w in range(4):
                        c0 = w * 512
                        for j in range(16):
                            q = j // 4
                            br_, k_br = banks[(2 * j) % 4]
                            bi_, k_bi = banks[(2 * j + 1) % 4]
                            cb_ = cosT[:, j, :].unsqueeze(1).to_broadcast([128, 2, 256]); sb_ = sinT[:, j, :].unsqueeze(1).to_broadcast([128, 2, 256])
                            V(lambda: vec.tensor_tensor(v3(w1), v3(br_), cb_, ALU.mult), [k_br, k_E], [k_w[0]])
                            V(lambda: vec.tensor_tensor(v3(w2), v3(bi_), sb_, ALU.mult), [k_bi, k_E], [k_w[1]])
                            V(lambda: vec.tensor_tensor(v3(w3), v3(bi_), cb_, ALU.mult), [k_bi, k_E], [k_w[2]])
                            V(lambda: vec.tensor_tensor(v3(w4), v3(br_), sb_, ALU.mult), [k_br, k_E], [k_w[3]])
                            V(lambda: vec.tensor_tensor(xre[:], w1[:], w2[:], ALU.add), [k_w[0], k_w[1]], [k_xr])
                            V(lambda: vec.tensor_tensor(xim[:], w3[:], w4[:], ALU.subtract), [k_w[2], k_w[3]], [k_xi])
                            vre, vim, k_vv = vre2[j % 2], vim2[j % 2], k_vv2[j % 2]
                            hh, k_h = hh2[j % 2], k_h2[j % 2]
                            rb = mag[:, j:j + 1].to_broadcast([128, 256])
                            k_vr, k_vi = k_vr2[j % 2], k_vi2[j % 2]
                            for c in range(2):
                                cs_ = slice(c * 256, c * 256 + 256)
                                first = (w == 0 and c == 0)
                                sc_re = lambda: V(lambda: vec.tensor_tensor_scan(vre[:, cs_], rb, xre[:, cs_], 0.0 if first else car[:, j, 0:1], ALU.mult, ALU.add),
                                                  [k_xr, k_cr, k_p], [k_vr])
                                sc_im = lambda: V(lambda: vec.tensor_tensor_scan(vim[:, cs_], rb, xim[:, cs_], 0.0 if first else car[:, j, 1:2], ALU.mult, ALU.add),
                                                  [k_xi, k_ci, k_p], [k_vi])
                                if c == 0:
                                    sc_re(); sc_im()
                                else:
                                    sc_im(); sc_re()
                                lr = vre[:, c * 256 + 255:c * 256 + 256]; li = vim[:, c * 256 + 255:c * 256 + 256]
                                if w == 3 and c == 1:
                                    V(lambda: vec.tensor_tensor(ctmp[:, 0:1], li, sinT[:, j, 255:256], ALU.mult), [k_vi, k_E], [k_t0])
                                    V(lambda: vec.tensor_tensor(ctmp[:, 1:2], lr, sinT[:, j, 255:256], ALU.mult), [k_vr, k_E], [k_t1])
                                    V(lambda: vec.scalar_tensor_tensor(fin[:, 0, j:j + 1], lr, cosT[:, j, 255:256], ctmp[:, 0:1], ALU.mult, ALU.subtract),
                                      [k_vr, k_E, k_t0], [k_fin])
                                    V(lambda: vec.scalar_tensor_tensor(fin[:, 1, j:j + 1], li, cosT[:, j, 255:256], ctmp[:, 1:2], ALU.mult, ALU.add),
                                      [k_vi, k_E, k_t1], [k_fin])
                                else:
                                    V(lambda: vec.tensor_tensor(ctmp[:, 1:2], lr, S256[:, j:j + 1], ALU.mult), [k_vr, k_p], [k_t1])
                                    V(lambda: vec.tensor_tensor(ctmp[:, 0:1], li, S256[:, j:j + 1], ALU.mult), [k_vi, k_p], [k_t0])
                                    V(lambda: vec.scalar_tensor_tensor(car[:, j, 1:2], li, C256[:, j:j + 1], ctmp[:, 1:2], ALU.mult, ALU.add),
                                      [k_vi, k_p, k_t1], [k_ci])
                                    V(lambda: vec.scalar_tensor_tensor(car[:, j, 0:1], lr, C256[:, j:j + 1], ctmp[:, 0:1], ALU.mult, ALU.subtract),
                                      [k_vr, k_p, k_t0], [k_cr])
                            k_vv = k_vv2[j % 2]
                            if j < 15:
                                emit_bu(w, j + 1)
                            elif w < 3:
                                emit_bu(w + 1, 0)
                            hv = lambda i_: hh[:, i_, :].rearrange("p (a t) -> p a t", a=2)
                            G(lambda: gp_.tensor_tensor(hv(0), v3(vre), cb_, ALU.mult), [k_vr, k_E], [k_h])
                            G(lambda: gp_.tensor_tensor(hv(1), v3(vim), sb_, ALU.mult), [k_vi, k_E], [k_h])
                            G(lambda: gp_.tensor_tensor(hv(2), v3(vre), sb_, ALU.mult), [k_vr, k_E], [k_h])
                            G(lambda: gp_.tensor_tensor(hv(3), v3(vim), cb_, ALU.mult), [k_vi, k_E], [k_h])
                            if pend:
                                gelu_out(*pend.pop())
                            if j % 4 == 0:
                                yb, k_yb = banks[4 + (q % 2)]
                                P.mm([lambda: pe.matmul(yb[:, :], lhsT=Dd[:, q, :], rhs=uT[:, q, c0:c0 + 512], start=True, stop=False)],
                                     [k_Dd, k_uT[w]], [k_yb])
                            P.mm([lambda: pe.matmul(yb[:, :], lhsT=Cpr[:, j, :], rhs=hh[:, 0, :], start=False, stop=False),
                                  lambda: pe.matmul(yb[:, :], lhsT=Cnr[:, j, :], rhs=hh[:, 1, :], start=False, stop=False),
                                  lambda: pe.matmul(yb[:, :], lhsT=Cpi[:, j, :], rhs=hh[:, 2, :], start=False, stop=False),
                                  lambda: pe.matmul(yb[:, :], lhsT=Cpi[:, j, :], rhs=hh[:, 3, :], start=False, stop=(j % 4 == 3))],
                                 [k_Cp, k_h], [k_yb])
                            if j % 4 == 3:
                                pend.append((yb, k_yb, q, c0, 512))
                    if pend:
                        gelu_out(*pend.pop())
                    for ri, o_ in enumerate((s5rp_o, s5ip_o)):
                        bk, k_bk = nb()
                        P.mm([lambda: pe.transpose(bk[0:16, 0:128], fin[:, ri, :], idf[:])], [k_fin, k_id], [k_bk])
                        V(lambda: vec.tensor_copy(w1[0:16, 0:128], bk[0:16, 0:128]), [k_bk], [k_x])
                        P.dma(P.sp, o_[:, :], w1[0:16, 0:128], reads=[k_x])
                        P.barrier()
                    pp.close()
                    SX = sb(ph, "SX", [16, 2048], F32); k_S0 = Tk()
                    hs = sb(ph, "hs", [128, 2, 16, 16, 5], F32); k_hs = Tk()
                    bus = sb(ph, "bus", [128, 2, 16, 64], F32); k_bus = Tk()
                    hsb = sb(ph, "hsb", [128, 2, 16, 64], BF16); k_hsb = Tk()
                    st1 = sb(ph, "st1", [128, 16, 16], F32); st2 = sb(ph, "st2", [128, 16, 16], F32); k_st = Tk()
                    k_SF = k_S0
                    for ri, s_ in enumerate((s5re0, s5im0)):
                        P.dma(P.sp, SX[:, :], s_[:, :], writes=[k_S0])
                        bk, k_bk = nb()
                        P.mm([lambda j=j: pe.transpose(bk[:, j * 16:(j + 1) * 16], SX[0:16, j * 128:(j + 1) * 128], idf[0:16, 0:16])
                              for j in range(16)], [k_S0, k_id], [k_bk])
                        V(lambda: vec.tensor_copy(hs[:, ri, :, :, 0], bk[:, 0:256].rearrange("p (j m) -> p j m", j=16)), [k_bk], [k_hs])
                    for j in range(16):
                        for ri, Bp in enumerate((Bpr, Bpi)):
                            bk, k_bk = nb()
                            P.mm([lambda: pe.matmul(bk[:, 0:64], lhsT=Bp[:, j, :], rhs=uT[:, j // 4, 2048:2112], start=True, stop=True)],
                                 [k_Bp, k_uT[4]], [k_bk])
                            if ri == 0:
                                A(lambda: act.copy(bus[:, ri, j, :], bk[:, 0:64]), [k_bk], [k_bus])
                            else:
                                V(lambda: vec.tensor_copy(bus[:, ri, j, :], bk[:, 0:64]), [k_bk], [k_bus])
                    lrb = lbr[:, :].unsqueeze(2).to_broadcast([128, 16, 16]); lib = lbi[:, :].unsqueeze(2).to_broadcast([128, 16, 16])
                    busv = bus[:].rearrange("p r j (m t) -> p r j m t", t=4)
                    for t in range(4):
                        pr_, pi_ = hs[:, 0, :, :, t], hs[:, 1, :, :, t]
                        R_ = [k_hs, k_p, k_st, k_bus]
                        V(lambda: vec.tensor_tensor(st1[:], pr_, lrb, ALU.mult), R_, [k_st])
                        V(lambda: vec.tensor_tensor(st2[:], pi_, lib, ALU.mult), R_, [k_st])
                        V(lambda: vec.tensor_tensor(st1[:], st1[:], st2[:], ALU.subtract), R_, [k_st])
                        V(lambda: vec.tensor_tensor(hs[:, 0, :, :, t + 1], st1[:], busv[:, 0, :, :, t], ALU.add), R_, [k_hs])
                        V(lambda: vec.tensor_tensor(st1[:], pi_, lrb, ALU.mult), R_, [k_st])
                        V(lambda: vec.tensor_tensor(st2[:], pr_, lib, ALU.mult), R_, [k_st])
                        V(lambda: vec.tensor_tensor(st1[:], st1[:], st2[:], ALU.add), R_, [k_st])
                        V(lambda: vec.tensor_tensor(hs[:, 1, :, :, t + 1], st1[:], busv[:, 1, :, :, t], ALU.add), R_, [k_hs])
                    for ri in range(2):
                        V(lambda: vec.tensor_copy(hsb[:, ri, :, :].rearrange("p j (m t) -> p j m t", t=4), hs[:, ri, :, :, 1:5]), [k_hs], [k_hsb])
                    for q in range(4):
                        yb, k_yb = nb()
                        fns = [lambda: pe.matmul(yb[:, 0:64], lhsT=Dd[:, q, :], rhs=uT[:, q, 2048:2112], start=True, stop=False)]
                        for j in range(4 * q, 4 * q + 4):
                            fns.append(lambda j=j: pe.matmul(yb[:, 0:64], lhsT=Cpr[:, j, :], rhs=hsb[:, 0, j, :], start=False, stop=False))
                            fns.append(lambda j=j: pe.matmul(yb[:, 0:64], lhsT=Cpi[:, j, :], rhs=hsb[:, 1, j, :], start=False, stop=(j == 4 * q + 3)))
                        P.mm(fns, [k_Dd, k_Cp, k_hsb, k_uT[4]], [k_yb])
                        gelu_out(yb, k_yb, q, 2048, 64)
                    for ri, o_ in enumerate((s5rs_o, s5is_o)):
                        for jj in range(4):
                            bk, k_bk = nb()
                            P.mm([lambda j=j: pe.transpose(bk[0:16, (j % 4) * 128:(j % 4 + 1) * 128], hs[:, ri, j, :, 4], idf[:])
                                  for j in range(4 * jj, 4 * jj + 4)], [k_hs, k_id], [k_bk])
                            V(lambda: vec.tensor_copy(SX[:, jj * 512:(jj + 1) * 512], bk[0:16, :]), [k_bk], [k_SF])
                        P.dma(P.sp, o_[:, :], SX[:, :], reads=[k_SF])
                    P.barrier()
                if STAGE >= 4:
                    with ExitStack() as ph:
                        wglu_sb, k_wglu = load_w(ph, "wglu_sb", wglu_d, 512, 512)
                        wout_sb, k_wout = load_w(ph, "wout_sb", wout_d, 1024, 1024)
                        gpost, k_gpost = load_gpost(ph, "gpost0", npo[0])
                        sT = sb(ph, "sT", [128, 4, NTOK], BF16); k_sTw = [Tk() for _ in range(5)]
                        sg = sb(ph, "sgl", [128, 512], F32); k_sg = Tk()
                        junk = sb(ph, "pjunk", [128, 512], BF16); tmp = sb(ph, "ptmp", [128, 512], F32)
                        pbufs = (junk, Tk(), tmp, Tk())
                        for w, (c0, n) in enumerate(WIDE):
                            for m in range(4):
                                bk, k_bk = nb()
                                P.mm([lambda k=k: pe.matmul(bk[:, 0:n], lhsT=wglu_sb[:, k, m * 128:(m + 1) * 128], rhs=gT[:, k, c0:c0 + n],
                                                            start=(k == 0), stop=(k == 3)) for k in range(4)], [k_wglu] + k_gTq, [k_bk])
                                A(lambda: act.activation(sg[:, 0:n], bk[:, 0:n], AF.Sigmoid), [k_bk], [k_sg])
                                V(lambda: vec.tensor_tensor(sT[:, m, c0:c0 + n], gT[:, m, c0:c0 + n], sg[:, 0:n], ALU.mult), [k_sg] + k_gTq, [k_sTw[w]])
                        for i in range(NT):
                            r = rows(i)
                            bk2 = [nb(), nb()]
                            for dh in range(2):
                                fns = [lambda c=c: pe.matmul(bk2[dh][0][0:r, :], lhsT=aT[:, c, i * 128:i * 128 + r], rhs=wout_sb[:, c, dh * 512:(dh + 1) * 512],
                                                             start=(c == 0), stop=False) for c in range(4)]
                                fns += [lambda c=c: pe.matmul(bk2[dh][0][0:r, :], lhsT=sT[:, c, i * 128:i * 128 + r], rhs=wout_sb[:, 4 + c, dh * 512:(dh + 1) * 512],
                                                              start=False, stop=(c == 3)) for c in range(4)]
                                P.mm(fns, [k_aT, k_sTw[min(i // 4, 4)], k_wout], [bk2[dh][1]])
                            post_norm_residual(pbufs, i, bk2, gpost, k_gpost)
                        P.barrier()
        P.barrier()
        if STAGE >= 4:
            ffn(0)
        if STAGE >= 5:
            with ExitStack() as L1:
                gpT = sb(L1, "gpT", [128, 8, 30 + 2048], BF16); k_gpw = [Tk() for _ in range(5)]; k_gp = k_gpw[4]
                gsT = sb(L1, "gsT", [128, 8, 16, 34], BF16); k_gs = Tk()
                b1, k_b1 = colvec(L1, "b_pw1", bpw1_d, 16)
                V(lambda: vec.memset(gpT[:, :, 0:30], 0.0), w=[k_gp])
                with ExitStack() as ph:
                    gtail = sb(ph, "gtail", [128, 8, 30], F32); gsn = sb(ph, "gsn", [128, 8, 64], F32); k_gt = Tk()
                    gcol, k_g = colvec(ph, "g_pm1", npm[1], 8)
                    w1_sb, k_w1 = load_w(ph, "wpw1_sb", wpw1_d, 1024, 2048)
                    hT = sb(ph, "hT1", [128, 8, NTOK], BF16); k_hT = [Tk() for _ in range(NT)]
                    norm_T(norm_bufs(ph), range(NT), gcol, k_g, hT, k_hT, 0)
                    sgb = sb(ph, "sgb", [128, 512], F32); k_sgb = Tk()
                    SC = sb(ph, "SC", [120, 4, 1024], BF16); k_SC = Tk()
                    for tI in range(4):
                        for m_ in range(4):
                            P.dma(P.pool, SC[30 * m_:30 * m_ + 30, tI, :], sconv[4 * tI + m_, :, :], writes=[k_SC], max_dma_last_dim=4096)
                    for tI in range(4):
                        tb, k_tb = ntb()
                        P.mm([lambda c=c: pe.transpose(tb[:, c * 120:(c + 1) * 120], SC[0:120, tI, c * 128:(c + 1) * 128], idb[0:120, 0:120])
                              for c in range(8)], [k_SC, k_id], [k_tb])
                        V(lambda: vec.tensor_copy(gsT[:, :, 4 * tI:4 * tI + 4, 0:30], tb[:, 0:960].rearrange("p (c m r) -> p c m r", c=8, m=4)),
                          [k_tb], [k_gs])
                    P.dma(P.sp, convs_o[:, 0:26, :], sconv[:, 4:30, :])
                    for w, (c0, n) in enumerate(WIDE):
                        kr = [k_hT[i] for i in tiles_of(c0, n)] + [k_w1]
                        for c in range(8):
                            ba, k_ba = nb()
                            P.mm([lambda k=k: pe.matmul(ba[:, 0:n], lhsT=w1_sb[:, k, c * 128:(c + 1) * 128], rhs=hT[:, k, c0:c0 + n],
                                                        start=(k == 0), stop=(k == 7)) for k in range(8)], kr, [k_ba])
                            bb_, k_bb = nb()
                            P.mm([lambda k=k: pe.matmul(bb_[:, 0:n], lhsT=w1_sb[:, k, 1024 + c * 128:1024 + (c + 1) * 128], rhs=hT[:, k, c0:c0 + n],
                                                        start=(k == 0), stop=(k == 7)) for k in range(8)], kr, [k_bb])
                            A(lambda: act.activation(sgb[:, 0:n], bb_[:, 0:n], AF.Sigmoid, bias=b1[:, 8 + c:9 + c]), [k_bb, k_b1], [k_sgb])
                            if w < 4:
                                V(lambda: vec.scalar_tensor_tensor(gpT[:, c, 30 + c0:30 + c0 + n], ba[:, 0:n], b1[:, c:c + 1], sgb[:, 0:n], ALU.add, ALU.mult),
                                  [k_ba, k_b1, k_sgb], [k_gpw[w]])
                                if w == 3:
                                    V(lambda: vec.scalar_tensor_tensor(gtail[:, c, :], ba[:, 482:512], b1[:, c:c + 1], sgb[:, 482:512], ALU.add, ALU.mult),
                                      [k_ba, k_b1, k_sgb], [k_gt])
                            else:
                                V(lambda: vec.scalar_tensor_tensor(gsn[:, c, :], ba[:, 0:64], b1[:, c:c + 1], sgb[:, 0:64], ALU.add, ALU.mult),
                                  [k_ba, k_b1, k_sgb], [k_gt])
                                V(lambda: vec.tensor_copy(gsT[:, c, :, 30:34], gsn[:, c, :].rearrange("p (m t) -> p m t", t=4)), [k_gt], [k_gs])
                    OT = sb(ph, "OT", [64, 1024], F32); k_OT = Tk()
                    for (src_, nr) in ((gtail, 30), (gsn, 64)):
                        for h in range(2):
                            bk, k_bk = nb()
                            P.mm([lambda c=c: pe.transpose(bk[0:nr, (c % 4) * 128:(c % 4 + 1) * 128], src_[:, c, :], idf[:])
                                  for c in range(4 * h, 4 * h + 4)], [k_gt, k_id], [k_bk])
                            V(lambda: vec.tensor_copy(OT[0:nr, h * 512:(h + 1) * 512], bk[0:nr, :]), [k_bk], [k_OT])
                        if nr == 30:
                            P.dma(P.sp, convp_o[:, :], OT[0:30, :], reads=[k_OT])
                        else:
                            for m_ in range(16):
                                P.dma(P.sp, convs_o[m_, 26:30, :], OT[4 * m_:4 * m_ + 4, :], reads=[k_OT])
                    P.barrier()
                with ExitStack() as ph:
                    wdwT = sb(ph, "wdwT", [128, 8, 31], F32); k_wdw = Tk()
                    with ExitStack() as tmps:
                        wdn = sb(tmps, "wdn", [31, 1024], F32); k_wdn = Tk()
                        P.dma(P.sp, wdn[:, :], wdw_d[:, :], writes=[k_wdn])
                        bkw, k_bkw = nb()
                        P.mm([lambda c_=c_: pe.transpose(bkw[:, c_ * 31:(c_ + 1) * 31], wdn[0:31, c_ * 128:(c_ + 1) * 128], idf[0:31, 0:31])
                              for c_ in range(8)], [k_wdn, k_id], [k_bkw])
                        V(lambda: vec.tensor_copy(wdwT[:].rearrange("p c j -> p (c j)"), bkw[:, 0:248]), [k_bkw], [k_wdw])
                        P.barrier()
                    bdw, k_bdw = colvec(ph, "bdw", bdw_d, 8); lng, k_lng = colvec(ph, "lng", lng_d, 8); lnb, k_lnb = colvec(ph, "lnb", lnb_d, 8)
                    w2_sb, k_w2 = load_w(ph, "wpw2_sb", wpw2_d, 1024, 1024)
                    b2row = sb(ph, "b2row", [1, 1024], BF16); k_b2 = Tk()
                    P.dma(P.pool, b2row[:], bpw2_d[:, :], writes=[k_b2], max_dma_last_dim=4096)
                    gpost, k_gpost = load_gpost(ph, "gpost1", npo[1])
                    wdg2 = [sb(ph, "wdg%d" % i_, [128, 31, 128], BF16) for i_ in range(2)]; k_wdg2 = [Tk(), Tk()]
                    cT = sb(ph, "cT", [128, 8, 512], F32); k_cTc = [Tk() for _ in range(8)]
                    cb = sb(ph, "cb", [128, 8, 512], BF16); c2b = sb(ph, "c2b", [128, 8, 512], BF16); k_cb = Tk()
                    actT = sb(ph, "actT", [128, 8, 512], BF16); k_actc = [Tk() for _ in range(8)]; k_c2b = Tk()
                    F5 = lambda n_: sb(ph, n_, [128, 512], F32)
                    mean_sb = F5("mean_sb"); m2 = F5("m2"); rstd_sb = F5("rstd_sb"); tt = [F5("tt0"), F5("tt1")]
                    k_st = Tk(); k_tt = [Tk(), Tk()]
                    junk = sb(ph, "cjunk", [128, 512], BF16); tmp = sb(ph, "ctmp2", [128, 512], F32)
                    pbufs = (junk, Tk(), tmp, Tk())
                    cTs = sb(ph, "cTs", [128, 8, 64], F32); k_cTs = Tk()

                    def conv_w(w):
                        if w == 4:
                            return
                        c0, n = WIDE[w]
                        for c in range(8):
                            wdg, k_wdg = wdg2[c % 2], k_wdg2[c % 2]
                            on_pool = c not in (3, 5, 7)
                            bld = (lambda f: G(f, [k_id, k_wdw], [k_wdg])) if on_pool else (lambda f: V(f, [k_id, k_wdw], [k_wdg]))
                            eng_ = nc.gpsimd if on_pool else vec
                            bld(lambda: eng_.tensor_tensor(wdg[:], idb[:].unsqueeze(1).to_broadcast([128, 31, 128]),
                                                           wdwT[:, c, :].unsqueeze(2).to_broadcast([128, 31, 128]), ALU.mult))
                            bk, k_bk = nb()
                            P.mm([lambda j=j: pe.matmul(bk[:, 0:n], lhsT=wdg[:, j, :], rhs=gpT[:, c, c0 + j:c0 + j + n],
                                                        start=(j == 0), stop=(j == 30)) for j in range(31)], [k_wdg] + k_gpw, [k_bk])
                            A(lambda: act.activation(cT[:, c, 0:n], bk[:, 0:n], AF.Identity, bias=bdw[:, c:c + 1]), [k_bk, k_bdw], [k_cTc[c]])
                            if w == 3:
                                bk2_, k_bk2_ = nb()
                                P.mm([lambda j=j: pe.matmul(bk2_[:, 0:64].rearrange("p (m t) -> p m t", t=4), lhsT=wdg[:, j, :], rhs=gsT[:, c, :, j:j + 4],
                                                            start=(j == 0), stop=(j == 30)) for j in range(31)], [k_wdg, k_gs], [k_bk2_])
                                A(lambda: act.activation(cTs[:, c, :], bk2_[:, 0:64], AF.Identity, bias=bdw[:, c:c + 1]), [k_bk2_, k_bdw], [k_cTs])
                    conv_w(0)
                    for w, (c0, n) in enumerate(WIDE):
                        cX = cT if w < 4 else cTs
                        k_cX = k_cTc if w < 4 else [k_cTs] * 8
                        V(lambda: vec.tensor_copy(cb[:, :, 0:n], cX[:, :, 0:n]), k_cX, [k_cb])
                        A(lambda: act.activation(c2b[:, :, 0:n], cX[:, :, 0:n], AF.Square), k_cX, [k_c2b])
                        bm, k_bm = nb()
                        P.mm([lambda c=c: pe.matmul(bm[:, 0:n], lhsT=onesln[:], rhs=cb[:, c, 0:n], start=(c == 0), stop=(c == 7)) for c in range(8)],
                             [k_cb, k_const], [k_bm])
                        bq, k_bq = nb()
                        P.mm([lambda c=c: pe.matmul(bq[:, 0:n], lhsT=onesln[:], rhs=c2b[:, c, 0:n], start=(c == 0), stop=(c == 7)) for c in range(8)],
                             [k_c2b, k_const], [k_bq])
                        A(lambda: act.copy(mean_sb[:, 0:n], bm[:, 0:n]), [k_bm], [k_st])
                        V(lambda: vec.tensor_tensor(m2[:, 0:n], mean_sb[:, 0:n], mean_sb[:, 0:n], ALU.mult), [k_st], [k_st])
                        V(lambda: vec.tensor_tensor(m2[:, 0:n], bq[:, 0:n], m2[:, 0:n], ALU.subtract), [k_bq, k_st], [k_st])
                        A(lambda: act.activation(rstd_sb[:, 0:n], m2[:, 0:n], AF.Sqrt, bias=lneps_t[:, 0:1]), [k_st, k_const], [k_st])
                        V(lambda: vec.reciprocal(rstd_sb[:, 0:n], rstd_sb[:, 0:n]), [k_st], [k_st])
                        for c in range(8):
                            t_, k_t = tt[c % 2], k_tt[c % 2]
                            V(lambda: vec.tensor_tensor(t_[:, 0:n], cX[:, c, 0:n], mean_sb[:, 0:n], ALU.subtract), [k_cX[c], k_st], [k_t])
                            V(lambda: vec.tensor_tensor(t_[:, 0:n], t_[:, 0:n], rstd_sb[:, 0:n], ALU.mult), [k_t, k_st], [k_t])
                            A(lambda: act.activation(actT[:, c, 0:n], t_[:, 0:n], AF.Silu, scale=lng[:, c:c + 1], bias=lnb[:, c:c + 1]),
                              [k_t, k_lng, k_lnb], [k_actc[c]])
                        if w + 1 < len(WIDE):
                            conv_w(w + 1)
                        for i in tiles_of(c0, n):
                            r = rows(i)
                            lc = i * 128 - c0
                            bk2 = [nb(), nb()]
                            for dh in range(2):
                                fns = [lambda c=c: pe.matmul(bk2[dh][0][0:r, :], lhsT=actT[:, c, lc:lc + r], rhs=w2_sb[:, c, dh * 512:(dh + 1) * 512],
                                                             start=(c == 0), stop=False) for c in range(8)]
                                fns.append(lambda: pe.matmul(bk2[dh][0][0:r, :], lhsT=onesb[0:1, 0:r], rhs=b2row[0:1, dh * 512:(dh + 1) * 512],
                                                             start=False, stop=True))
                                P.mm(fns, k_actc + [k_w2, k_b2, k_const], [bk2[dh][1]])
                            post_norm_residual(pbufs, i, bk2, gpost, k_gpost)
                    P.barrier()
            if STAGE >= 6:
                ffn(1)
        if True:
            for i in range(NT):
                r = rows(i)
                P.dma(P.sp, y_o[i * 128:i * 128 + r, :], X[0:r, i, :], reads=[kX[i]])
        P.finish()
    return nc


_NC_CACHE = {}


def kernel(**inp):
    f = lambda a: np.ascontiguousarray(np.asarray(a, dtype=np.float32))
    I = {k: f(v) for k, v in inp.items()}
    w_in = I["w_in_ab"][0]
    q, k, v, u = w_in[:, 0:512], w_in[:, 512:640], w_in[:, 640:768], w_in[:, 768:1280]
    win = np.concatenate([q, k[:, 0:64], k[:, 0:64], k[:, 64:128], k[:, 64:128], k, v, u], axis=1)
    shared = dict(
        npm=I["norm_pre_mix"], npo=I["norm_post_mix"], nfp=I["norm_pre_ffn"], nfo=I["norm_post_ffn"],
        win=f(win), sinks=I["attn_sinks"], lre=I["s5_lambda_re"].reshape(2048), lim=I["s5_lambda_im"].reshape(2048),
        lst=I["s5_log_step"].reshape(32), bre=I["s5_b_re"].reshape(2048, 16), bim=I["s5_b_im"].reshape(2048, 16),
        cre=I["s5_c_re"].reshape(512, 64), cim=I["s5_c_im"].reshape(512, 64), dsk=I["s5_d"].reshape(512),
        wglu=I["w_glu"][0], wout=I["w_out_ab"][0], wpw1=I["w_pw1"][0], bpw1=I["b_pw1"].reshape(2048),
        wdw=I["w_dw"][0], bdw=I["b_dw"].reshape(1024), lng=I["conv_ln_g"].reshape(1024), lnb=I["conv_ln_b"].reshape(1024),
        wpw2=I["w_pw2"][0], bpw2=I["b_pw2"].reshape(1, 1024), wg=I["w_ffn_gate"], wu=I["w_ffn_up"], wd=I["w_ffn_down"])
    in_maps = []
    for b in range(8):
        sl = slice(16 * b, 16 * b + 16)
        m = dict(shared)
        m["xin"] = f(np.concatenate([I["x_prompt"][b], I["x_sample"][sl].reshape(64, 1024)], axis=0))
        m["ck"] = f(I["cache_k"][0, sl].reshape(16, 128, 128)); m["cv"] = f(I["cache_v"][0, sl].reshape(16, 128, 128))
        m["s5re0"] = f(I["state_s5_re"][0, sl].reshape(16, 2048)); m["s5im0"] = f(I["state_s5_im"][0, sl].reshape(16, 2048))
        m["sconv"] = f(I["state_conv"][0, sl])
        in_maps.append(m)
    if "nc" not in _NC_CACHE:
        _NC_CACHE["nc"] = build_nc()
    res = run_bass_kernel_spmd(_NC_CACHE["nc"], in_maps[:NCORES], core_ids=list(range(NCORES)))
    R = res.results
    cat = lambda key: np.stack([np.asarray(R[min(b, NCORES - 1)][key], dtype=np.float32) for b in range(8)])
    y = cat("y")
    y_prompt = y[:, :2048, :]
    y_sample = y[:, 2048:, :].reshape(128, 4, 1024)
    k_prompt = cat("kp").reshape(1, 8, 128, 2, 64); v_prompt = cat("vp").reshape(1, 8, 128, 2, 64)
    s5rp = cat("s5rp").reshape(1, 8, 32, 64); s5ip = cat("s5ip").reshape(1, 8, 32, 64)
    convp = cat("convp").reshape(1, 8, 30, 1024)
    k_sample = cat("ks").reshape(1, 128, 128, 2, 64); v_sample = cat("vs").reshape(1, 128, 128, 2, 64)
    s5rs = cat("s5rs").reshape(1, 128, 32, 64); s5is = cat("s5is").reshape(1, 128, 32, 64)
    convs = cat("convs").reshape(1, 128, 30, 1024)
    return (np.ascontiguousarray(y_prompt), np.ascontiguousarray(y_sample), k_prompt, v_prompt, s5rp, s5ip, convp,
            k_sample, v_sample, s5rs, s5is, convs)
```

```python
import numpy as np
from contextlib import ExitStack
import concourse.bass as bass
import concourse.mybir as mybir
from concourse.bass_utils import run_bass_kernel_spmd

AF = mybir.ActivationFunctionType
ALU = mybir.AluOpType
F32 = mybir.dt.float32
BF16 = mybir.dt.bfloat16
I32 = mybir.dt.int32

STAGE = 99
SUB = 99
NCORES = 8
SAME_ENG_SYNC = True
PI = float(np.pi)


class Tk:
    __slots__ = ("name", "w", "r", "x")

    def __init__(self, name="", x=False):
        self.name = name
        self.w = []
        self.r = {}
        self.x = x


class Eng:
    def __init__(self, name, obj, sem, unit=1):
        self.name = name
        self.obj = obj
        self.sem = sem
        self.unit = unit
        self.count = 0
        self.seen = {}


class Prog:
    NSLOT = 8

    def __init__(self, nc, stack):
        self.nc = nc
        mk = lambda n: stack.enter_context(nc.semaphore(n))
        self.pe = Eng("pe", nc.tensor, mk("s_pe"))
        self.act = Eng("act", nc.scalar, mk("s_act"))
        self.dve = Eng("dve", nc.vector, mk("s_dve"))
        self.pool = Eng("pool", nc.gpsimd, mk("s_pool"))
        self.sp = Eng("sp", nc.sync, None)
        self.compute = [self.pe, self.act, self.dve, self.pool]
        self.slots = {}
        self.slot_i = {}
        for q in (self.sp, self.pool):
            self.slots[q.name] = [Eng("d_%s%d" % (q.name, i), None, mk("s_d%s%d" % (q.name, i)), 16)
                                  for i in range(self.NSLOT)]
            self.slot_i[q.name] = 0
        self.n_inst = 0

    def _wait(self, eng, dep, cnt):
        if eng.seen.get(dep, 0) >= cnt:
            return
        eng.obj.wait_ge(dep.sem, cnt * dep.unit)
        eng.seen[dep] = cnt

    def _deps(self, eng, reads, writes, part=False):
        deps = {}

        def add(e, c):
            if deps.get(e, 0) < c:
                deps[e] = c
        reads, xr = [t for t in reads if not t.x], [t for t in reads if t.x]
        writes = list(writes) + xr
        for t in reads:
            for e, c in t.w:
                add(e, c)
        for t in writes:
            if not part:
                for e, c in t.w:
                    add(e, c)
            for e, c in t.r.items():
                add(e, c)
        for e, c in deps.items():
            if e is eng and (eng is self.pe or not SAME_ENG_SYNC):
                continue
            self._wait(eng, e, c)

    def _mark(self, eng, reads, writes, part=False):
        writes = list(writes) + [t for t in reads if t.x]
        for t in reads:
            if not t.x:
                t.r[eng] = eng.count
        for t in writes:
            if part:
                t.w = [(e, c) for e, c in t.w if e is not eng] + [(eng, eng.count)]
            else:
                t.w = [(eng, eng.count)]
                t.r = {}

    def op(self, eng, fn, reads=(), writes=()):
        self._deps(eng, reads, writes)
        ins = fn()
        eng.count += 1
        ins.then_inc(eng.sem, 1)
        self._mark(eng, reads, writes)
        self.n_inst += 1

    def mm(self, fns, reads=(), writes=()):
        eng = self.pe
        self._deps(eng, reads, writes)
        ins = None
        for fn in fns:
            ins = fn()
            self.n_inst += 1
        eng.count += 1
        ins.then_inc(eng.sem, 1)
        self._mark(eng, reads, writes)

    def dma(self, q, out, in_, reads=(), writes=(), part=False, **kw):
        self._deps(q, reads, writes, part)
        sl = self.slots[q.name]
        i = self.slot_i[q.name]
        self.slot_i[q.name] = (i + 1) % len(sl)
        s = sl[i]
        if s.count > 0:
            self._wait(q, s, s.count)
        ins = q.obj.dma_start(out=out, in_=in_, **kw)
        s.count += 1
        ins.then_inc(s.sem, 16)
        self._mark(s, reads, writes, part)
        self.n_inst += 1

    def barrier(self):
        allsl = [s for v in self.slots.values() for s in v if s.count > 0]
        for e in self.compute + [self.sp]:
            for o in self.compute:
                if o is not e and o.count > 0:
                    self._wait(e, o, o.count)
            for s in allsl:
                self._wait(e, s, s.count)

    def finish(self):
        allsl = [s for v in self.slots.values() for s in v if s.count > 0]
        for o in self.compute:
            if o.count > 0:
                self._wait(self.sp, o, o.count)
        for s in allsl:
            self._wait(self.sp, s, s.count)


NT = 17
NTOK = 2112
WIDE = [(0, 512), (512, 512), (1024, 512), (1536, 512), (2048, 64)]
GROUPS = [list(range(0, 6)), list(range(6, 12)), list(range(12, 17))]
EPS = 1e-6
LN_EPS = 1e-5


def rows(i):
    return 128 if i < 16 else 64


def tiles_of(c0, n):
    return [i for i in range(NT) if i * 128 >= c0 and i * 128 < c0 + n]


def build_nc():
    nc = bass.Bass("TRN2", target_bir_lowering=False)
    di = lambda n, shp: nc.dram_tensor(n, shp, F32, kind="ExternalInput").ap()
    do = lambda n, shp: nc.dram_tensor(n, shp, F32, kind="ExternalOutput").ap()
    xin = di("xin", [NTOK, 1024]); ck = di("ck", [16, 128, 128]); cv = di("cv", [16, 128, 128])
    s5re0 = di("s5re0", [16, 2048]); s5im0 = di("s5im0", [16, 2048]); sconv = di("sconv", [16, 30, 1024])
    npm = di("npm", [2, 1024]); npo = di("npo", [2, 1024]); nfp = di("nfp", [2, 1024]); nfo = di("nfo", [2, 1024])
    win = di("win", [1024, 1536]); sinks = di("sinks", [1, 8])
    lre_d = di("lre", [2048]); lim_d = di("lim", [2048]); lst_d = di("lst", [32])
    bre_d = di("bre", [2048, 16]); bim_d = di("bim", [2048, 16]); cre_d = di("cre", [512, 64]); cim_d = di("cim", [512, 64])
    dsk_d = di("dsk", [512]); wglu_d = di("wglu", [512, 512]); wout_d = di("wout", [1024, 1024])
    wpw1_d = di("wpw1", [1024, 2048]); bpw1_d = di("bpw1", [2048]); wdw_d = di("wdw", [31, 1024]); bdw_d = di("bdw", [1024])
    lng_d = di("lng", [1024]); lnb_d = di("lnb", [1024]); wpw2_d = di("wpw2", [1024, 1024]); bpw2_d = di("bpw2", [1, 1024])
    wg_d = di("wg", [2, 1024, 2816]); wu_d = di("wu", [2, 1024, 2816]); wd_d = di("wd", [2, 2816, 1024])
    y_o = do("y", [NTOK, 1024]); kp_o = do("kp", [128, 128]); vp_o = do("vp", [128, 128])
    s5rp_o = do("s5rp", [16, 128]); s5ip_o = do("s5ip", [16, 128]); convp_o = do("convp", [30, 1024])
    ks_o = do("ks", [16, 128, 128]); vs_o = do("vs", [16, 128, 128])
    s5rs_o = do("s5rs", [16, 2048]); s5is_o = do("s5is", [16, 2048]); convs_o = do("convs", [16, 30, 1024])

    with ExitStack() as st:
        P = Prog(nc, st)
        V = lambda fn, r=(), w=(): P.op(P.dve, fn, r, w)
        A = lambda fn, r=(), w=(): P.op(P.act, fn, r, w)
        G = lambda fn, r=(), w=(): P.op(P.pool, fn, r, w)
        vec, act, pe = nc.vector, nc.scalar, nc.tensor

        uid = [0]

        def sb(stk, n, shp, dt):
            uid[0] += 1
            return stk.enter_context(nc.sbuf_tensor("%s_%d" % (n, uid[0]), shp, dt))

        banks = [(st.enter_context(nc.psum_tensor("psA%d" % i, [128, 512], F32)), Tk("psA%d" % i, True)) for i in range(6)]
        tbanks = [(st.enter_context(nc.psum_tensor("psT%d" % i, [128, 1024], BF16)), Tk("psT%d" % i, True)) for i in range(2)]
        bi = [0, 0]

        def nb():
            bi[0] = (bi[0] + 1) % 6
            return banks[bi[0]]

        def ntb():
            bi[1] = (bi[1] + 1) % 2
            return tbanks[bi[1]]

        X = sb(st, "X", [128, NT, 1024], F32)
        kX = [Tk("X%d" % i) for i in range(NT)]
        io = sb(st, "io", [128, 128], I32); k_io = Tk()
        idb = sb(st, "idb", [128, 128], BF16); idf = sb(st, "idf", [128, 128], F32); k_id = Tk()
        onesb = sb(st, "onesb", [128, 128], BF16); onesln = sb(st, "onesln", [128, 128], BF16)
        maskP = sb(st, "maskP", [128, 128], BF16); maskC = sb(st, "maskC", [128, 128], BF16)
        maskN = sb(st, "maskN", [64, 64], BF16); tm4 = sb(st, "tm4", [64, 64], I32); mtmp = sb(st, "mtmp", [64, 64], BF16)
        k_const = Tk()
        small = sb(st, "small", [128, 64], F32)
        k_small = Tk()

        G(lambda: nc.gpsimd.iota(io[:], pattern=[[1, 128]], base=0, channel_multiplier=-1), w=[k_io])
        G(lambda: nc.gpsimd.iota(tm4[:], pattern=[[0, 16], [1, 4]], base=0, channel_multiplier=0), w=[k_io])
        V(lambda: vec.tensor_single_scalar(idb[:], io[:], 0, ALU.is_equal), [k_io], [k_id])
        V(lambda: vec.tensor_single_scalar(idf[:], io[:], 0, ALU.is_equal), [k_io], [k_id])
        V(lambda: vec.memset(onesb[:], 1.0), w=[k_const])
        V(lambda: vec.memset(onesln[:], 1.0 / 1024.0), w=[k_const])
        V(lambda: vec.tensor_single_scalar(maskP[:], io[:], 0, ALU.is_lt), [k_io], [k_const])
        V(lambda: vec.tensor_single_scalar(maskC[:], io[:], 0, ALU.is_ge), [k_io], [k_const])
        V(lambda: vec.tensor_single_scalar(maskN[:], io[0:64, 0:64], 0, ALU.is_ge), [k_io], [k_const])
        V(lambda: vec.tensor_tensor(mtmp[:], io[0:64, 0:64], tm4[:], ALU.is_le), [k_io], [k_const])
        V(lambda: vec.tensor_tensor(maskN[:], maskN[:], mtmp[:], ALU.mult), [k_const], [k_const])

        def load_X(tiles):
            for i in tiles:
                r = rows(i)
                P.dma(P.sp, X[0:r, i, :], xin[i * 128:i * 128 + r, :], writes=[kX[i]])
        load_X(range(2))

        def colvec(stk, name, dram_flat, ncol):
            t = sb(stk, name, [128, ncol], F32)
            k = Tk(name)
            with nc.allow_non_contiguous_dma(reason="small param vector"):
                P.dma(P.sp, t[:], dram_flat.rearrange("(c p) -> p c", p=128), writes=[k])
            return t, k

        def load_w(stk, name, dram2d, K, N, n0=0):
            kc = K // 128
            t = sb(stk, name, [128, kc, N], BF16)
            k = Tk(name)
            src = dram2d.rearrange("(c p) n -> p c n", p=128)
            c = 0
            while c < N:
                nbk = min(1024, N - c)
                P.dma(P.pool, t[:, :, c:c + nbk], src[:, :, n0 + c:n0 + c + nbk], writes=[k], part=(c > 0))
                c += nbk
            return t, k

        def rstd_from_ssq(ssq_ap, out_ap, n, scale, eps, k=None):
            k = k or k_small
            A(lambda: act.activation(out_ap, ssq_ap, AF.Sqrt, scale=scale, bias=eps_t[0:n, 0:1] if eps == EPS else lneps_t[0:n, 0:1]),
              [k, k_const], [k])
            V(lambda: vec.reciprocal(out_ap, out_ap), [k], [k])

        k_sm_n = [Tk() for _ in range(4)]
        k_sm_p = [Tk() for _ in range(4)]
        pn_cnt = [0]

        eps_t = sb(st, "eps_t", [128, 1], F32); lneps_t = sb(st, "lneps_t", [128, 1], F32)
        halfpi = sb(st, "halfpi", [128, 1], F32)
        V(lambda: vec.memset(eps_t[:], EPS), w=[k_const])
        V(lambda: vec.memset(lneps_t[:], LN_EPS), w=[k_const])
        V(lambda: vec.memset(halfpi[:], PI / 2), w=[k_const])

        def norm_bufs(stk):
            junk = sb(stk, "nt_junk", [128, 1024], BF16)
            hb = [sb(stk, "nt_hb%d" % j, [128, 1024], BF16) for j in range(2)]
            return (junk, Tk(), hb, [Tk(), Tk()])

        def norm_T(nbufs, tiles, gcol, k_g, hT, k_hT, col0):
            junk, k_junk, hb, k_hb = nbufs
            for n_, i in enumerate(tiles):
                r = rows(i)
                j = n_ % 2
                sl = n_ % 4
                ks = k_sm_n[sl]
                ssq = small[0:r, 16 + 2 * sl:17 + 2 * sl]; rs = small[0:r, 17 + 2 * sl:18 + 2 * sl]
                A(lambda: act.activation(junk[0:r, :], X[0:r, i, :], AF.Square, accum_out=ssq), [kX[i]], [ks])
                rstd_from_ssq(ssq, rs, r, 1.0 / 1024.0, EPS, ks)
                V(lambda: vec.tensor_scalar(hb[j][0:r, :], X[0:r, i, :], rs, None, ALU.mult), [kX[i], ks], [k_hb[j]])
                tb, k_tb = ntb()
                P.mm([lambda c=c: pe.transpose(tb[:, c * 128:c * 128 + r], hb[j][0:r, c * 128:(c + 1) * 128], idb[0:r, 0:r])
                      for c in range(8)], [k_hb[j], k_id], [k_tb])
                c0 = i * 128 - col0
                V(lambda: vec.tensor_tensor(hT[:, :, c0:c0 + r],
                                            tb[:].rearrange("p (c t) -> p c t", c=8)[:, :, 0:r],
                                            gcol[:, :].unsqueeze(2).to_broadcast([128, 8, r]), ALU.mult),
                  [k_tb, k_g], [k_hT[i]])

        def post_norm_residual(stk_tmp_bufs, i, bk, gpost, k_gpost):
            r = rows(i)
            junk, k_junk, tmp, k_tmp = stk_tmp_bufs[:4]
            tmps = [(tmp, k_tmp), (stk_tmp_bufs[4], stk_tmp_bufs[5]) if len(stk_tmp_bufs) > 4 else (tmp, k_tmp)]
            sl = pn_cnt[0] % 4
            pn_cnt[0] += 1
            ks = k_sm_p[sl]
            b0 = 32 + 4 * sl
            for dh in range(2):
                A(lambda dh=dh: act.activation(junk[0:r, :], bk[dh][0][0:r, :], AF.Square, accum_out=small[0:r, b0 + dh:b0 + dh + 1]),
                  [bk[dh][1]], [ks])
            V(lambda: vec.tensor_tensor(small[0:r, b0 + 2:b0 + 3], small[0:r, b0:b0 + 1], small[0:r, b0 + 1:b0 + 2], ALU.add), [ks], [ks])
            rstd_from_ssq(small[0:r, b0 + 2:b0 + 3], small[0:r, b0 + 3:b0 + 4], r, 1.0 / 1024.0, EPS, ks)
            for dh in range(2):
                V(lambda dh=dh: vec.scalar_tensor_tensor(tmps[dh][0][0:r, :], bk[dh][0][0:r, :], small[0:r, b0 + 3:b0 + 4],
                                                         gpost[0:r, dh * 512:(dh + 1) * 512], ALU.mult, ALU.mult),
                  [bk[dh][1], ks, k_gpost], [tmps[dh][1]])
                if len(stk_tmp_bufs) <= 4:
                    V(lambda dh=dh: vec.tensor_tensor(X[0:r, i, dh * 512:(dh + 1) * 512], X[0:r, i, dh * 512:(dh + 1) * 512],
                                                      tmp[0:r, :], ALU.add), [k_tmp, kX[i]], [kX[i]])
            if len(stk_tmp_bufs) > 4:
                for dh in range(2):
                    V(lambda dh=dh: vec.tensor_tensor(X[0:r, i, dh * 512:(dh + 1) * 512], X[0:r, i, dh * 512:(dh + 1) * 512],
                                                      tmps[dh][0][0:r, :], ALU.add), [tmps[dh][1], kX[i]], [kX[i]])

        def load_gpost(stk, name, dram_row):
            t = sb(stk, name, [128, 1024], F32); k = Tk(name)
            P.dma(P.sp, t[:], dram_row.partition_broadcast(128), writes=[k])
            return t, k

        def ffn(layer):
            with ExitStack() as ph:
                gcol, k_g = colvec(ph, "ffn_g", nfp[layer], 8)
                gpost, k_gpost = load_gpost(ph, "ffn_gpost", nfo[layer])
                wd_sb = sb(ph, "wd_sb", [128, 22, 1024], BF16)
                k_wd = Tk()
                wgu = [(sb(ph, "wg%d" % j, [128, 8, 256], BF16), sb(ph, "wu%d" % j, [128, 8, 256], BF16), Tk()) for j in range(3)]
                hT = sb(ph, "ffn_hT", [128, 8, 768], BF16); k_hT = [Tk() for _ in range(NT)]
                hid = sb(ph, "ffn_hid", [128, 22, 768], BF16); k_hidf = [Tk() for _ in range(22)]
                sg = [sb(ph, "ffn_sg%d" % j, [128, 512], F32) for j in range(2)]; k_sg = [Tk(), Tk()]
                junk = sb(ph, "ffn_junk", [128, 512], BF16); tmp = sb(ph, "ffn_tmp", [128, 512], F32)
                pbufs = (junk, Tk(), tmp, Tk(), sb(ph, "ffn_tmp2", [128, 512], F32), Tk())
                nbufs = norm_bufs(ph)
                wdsrc = wd_d[layer].rearrange("(c p) n -> p c n", p=128)
                wgsrc = wg_d[layer].rearrange("(c p) n -> p c n", p=128)
                wusrc = wu_d[layer].rearrange("(c p) n -> p c n", p=128)
                first = True
                cnt = 0
                norm_T(nbufs, GROUPS[0], gcol, k_g, hT, k_hT, GROUPS[0][0] * 128)
                for gi, grp in enumerate(GROUPS):
                    col0 = grp[0] * 128
                    ncols = sum(rows(i) for i in grp)
                    pieces = []
                    c = 0
                    while c < ncols:
                        n = min(512, ncols - c)
                        pieces.append((c, n))
                        c += n
                    for fg in range(11):
                        wgt, wut, k_w = wgu[cnt % 3]
                        cnt += 1
                        P.dma(P.pool, wgt[:, :, :], wgsrc[:, :, fg * 256:(fg + 1) * 256], writes=[k_w])
                        P.dma(P.pool, wut[:, :, :], wusrc[:, :, fg * 256:(fg + 1) * 256], writes=[k_w], part=True)
                        if first and fg == 2:
                            P.dma(P.pool, wd_sb[:, 0:11, :], wdsrc[:, 0:11, :], writes=[k_wd])
                            P.dma(P.pool, wd_sb[:, 11:22, :], wdsrc[:, 11:22, :], writes=[k_wd], part=True)
                            first = False
                        for (pc, pn) in pieces:
                            kr = [k_hT[i] for i in tiles_of(col0 + pc, pn)]
                            for fc in range(2):
                                f = fg * 2 + fc
                                bg, k_bg = nb()
                                P.mm([lambda k=k: pe.matmul(bg[:, 0:pn], lhsT=wgt[:, k, fc * 128:(fc + 1) * 128], rhs=hT[:, k, pc:pc + pn],
                                                            start=(k == 0), stop=(k == 7)) for k in range(8)], kr + [k_w], [k_bg])
                                bu, k_bu = nb()
                                P.mm([lambda k=k: pe.matmul(bu[:, 0:pn], lhsT=wut[:, k, fc * 128:(fc + 1) * 128], rhs=hT[:, k, pc:pc + pn],
                                                            start=(k == 0), stop=(k == 7)) for k in range(8)], kr + [k_w], [k_bu])
                                j = f % 2
                                A(lambda: act.activation(sg[j][:, 0:pn], bg[:, 0:pn], AF.Silu), [k_bg], [k_sg[j]])
                                V(lambda: vec.tensor_tensor(hid[:, f, pc:pc + pn], sg[j][:, 0:pn], bu[:, 0:pn], ALU.mult),
                                  [k_sg[j], k_bu], [k_hidf[f]])
                    if gi + 1 < len(GROUPS):
                        norm_T(nbufs, GROUPS[gi + 1], gcol, k_g, hT, k_hT, GROUPS[gi + 1][0] * 128)
                    for i in grp:
                        r = rows(i)
                        c0 = i * 128 - col0
                        bk = [nb(), nb()]
                        for dh in range(2):
                            P.mm([lambda f=f: pe.matmul(bk[dh][0][0:r, :], lhsT=hid[:, f, c0:c0 + r], rhs=wd_sb[:, f, dh * 512:(dh + 1) * 512],
                                                        start=(f == 0), stop=(f == 21)) for f in range(22)], k_hidf + [k_wd], [bk[dh][1]])
                        post_norm_residual(pbufs, i, bk, gpost, k_gpost)
                P.barrier()

        with ExitStack() as L0:
            aT = sb(L0, "aT", [128, 4, NTOK], BF16); k_aT = Tk()
            uT = sb(L0, "uT", [128, 4, NTOK], BF16); k_uTm = [[Tk() for _ in range(4)] for _ in range(5)]
            k_p = Tk()
            lre = sb(L0, "s5_lre", [128, 16], F32); lim = sb(L0, "s5_lim", [128, 16], F32); dtt = sb(L0, "s5_dtt", [128, 16], F32)
            with ExitStack() as Lq:
                qT = sb(Lq, "qT", [128, 4, NTOK], BF16); k_qTm = [[Tk() for _ in range(4)] for _ in range(5)]
                kT2 = sb(Lq, "kT2", [128, 2, NTOK], BF16); k_kTm = [[Tk() for _ in range(2)] for _ in range(5)]
                vtok = sb(Lq, "vtok", [128, NT, 128], BF16); vpad = sb(Lq, "vpad", [128, NT, 2, 128], BF16)
                k_v = [Tk() for _ in range(NT)]
                kvf = sb(Lq, "kvf", [128, 2, 256], F32); k_kvf = Tk()
                V(lambda: vec.memset(vpad[:], 0.0), w=k_v)
                with ExitStack() as ph:
                    gcol, k_g = colvec(ph, "g_pm0", npm[0], 8)
                    load_X(range(2, NT))
                    win_sb, k_win = load_w(ph, "win_sb", win, 1024, 1536)
                    hT = sb(ph, "hT", [128, 8, NTOK], BF16); k_hT = [Tk() for _ in range(NT)]
                    with nc.allow_non_contiguous_dma(reason="s5 params"):
                        P.dma(P.sp, lre[:], lre_d.rearrange("(j q) -> q j", q=128), writes=[k_p])
                        P.dma(P.sp, lim[:], lim_d.rearrange("(j q) -> q j", q=128), writes=[k_p])
                        lst2 = lst_d.rearrange("(j t) -> t j", t=2)
                        for g2 in range(2):
                            P.dma(P.sp, dtt[g2 * 64:(g2 + 1) * 64, :], lst2[g2:g2 + 1, :].to_broadcast([64, 16]), writes=[k_p])
                    norm_T(norm_bufs(ph), range(NT), gcol, k_g, hT, k_hT, 0)
                    for w, (c0, n) in enumerate(WIDE if SUB >= 2 else []):
                        kr = [k_hT[i] for i in tiles_of(c0, n)] + [k_win]
                        for m in range(10):
                            woff = m * 128 if m < 4 else (512 + (m - 4) * 128 if m < 6 else 1024 + (m - 6) * 128)
                            b, k_b = nb()
                            P.mm([lambda k=k: pe.matmul(b[:, 0:n], lhsT=win_sb[:, k, woff:woff + 128], rhs=hT[:, k, c0:c0 + n],
                                                        start=(k == 0), stop=(k == 7)) for k in range(8)], kr, [k_b])
                            if m < 4:
                                A(lambda: act.activation(qT[:, m, c0:c0 + n], b[:, 0:n], AF.Copy, scale=0.125), [k_b], [k_qTm[w][m]])
                            elif m < 6:
                                V(lambda: vec.tensor_copy(kT2[:, m - 4, c0:c0 + n], b[:, 0:n]), [k_b], [k_kTm[w][m - 4]])
                            else:
                                A(lambda: act.copy(uT[:, m - 6, c0:c0 + n], b[:, 0:n]), [k_b], [k_uTm[w][m - 6]])
                    for i in (range(NT) if SUB >= 3 else []):
                        r = rows(i)
                        b, k_b = nb()
                        P.mm([lambda k=k: pe.matmul(b[0:r, 0:256], lhsT=hT[:, k, i * 128:i * 128 + r], rhs=win_sb[:, k, 768:1024],
                                                    start=(k == 0), stop=(k == 7)) for k in range(8)], [k_hT[i], k_win], [k_b])
                        V(lambda: vec.tensor_copy(vtok[0:r, i, :], b[0:r, 128:256]), [k_b], [k_v[i]])
                        for g_ in (range(2) if SUB >= 5 else []):
                            A(lambda g_=g_: act.copy(vpad[0:r, i, g_, 64:128], b[0:r, 128 + g_ * 64:192 + g_ * 64]), [k_b], [k_v[i]])
                        if i >= 15 and SUB >= 6:
                            V(lambda: vec.tensor_copy(kvf[0:r, i - 15, :], b[0:r, 0:256]), [k_b], [k_kvf])
                    if SUB >= 8:
                      P.dma(P.sp, kp_o[:, :], kvf[:, 0, 0:128], reads=[k_kvf])
                      P.dma(P.sp, vp_o[:, :], kvf[:, 0, 128:256], reads=[k_kvf])
                    for m in (range(16) if SUB >= 7 else []):
                        P.dma(P.sp, ks_o[m, 124:128, :], kvf[4 * m:4 * m + 4, 1, 0:128], reads=[k_kvf])
                        P.dma(P.sp, vs_o[m, 124:128, :], kvf[4 * m:4 * m + 4, 1, 128:256], reads=[k_kvf])
                    P.dma(P.sp, ks_o[:, 0:124, :], ck[:, 4:128, :])
                    P.dma(P.sp, vs_o[:, 0:124, :], cv[:, 4:128, :])
                    P.barrier()
                if STAGE >= 2:
                    with ExitStack() as ph:
                        sk = sb(ph, "sk", [128, 8], F32); k_sk = Tk()
                        P.dma(P.sp, sk[:], sinks[0:1, :].to_broadcast([128, 8]), writes=[k_sk])
                        A(lambda: act.activation(sk[:], sk[:], AF.Exp), [k_sk], [k_sk])
                        PT = [sb(ph, "PT%d" % j, [128, 2, 128], BF16) for j in range(8)]; k_PT = [Tk() for _ in range(8)]
                        dn2 = [sb(ph, "dn%d" % i_, [128, 2, 128], F32) for i_ in range(2)]; k_dn2 = [Tk(), Tk()]; ndn = [0]

                        def normalize(R, g, par, O, k_O, Dn, k_Dn, c0, n):
                            h0 = 4 * g + par
                            dnb, k_dnb = dn2[ndn[0] % 2], k_dn2[ndn[0] % 2]
                            ndn[0] += 1
                            for ci in range(2):
                                A(lambda ci=ci: act.activation(dnb[R, ci, 0:n], Dn[:, ci, :], AF.Ln, bias=sk[R, h0 + 2 * ci:h0 + 2 * ci + 1]),
                                  [k_Dn, k_sk], [k_dnb])
                            A(lambda: act.activation(dnb[R, :, 0:n], dnb[R, :, 0:n], AF.Exp, scale=-1.0), [k_dnb], [k_dnb])
                            V(lambda: vec.tensor_tensor(aT[R, 2 * g:2 * g + 2, c0:c0 + n], O, dnb[R, :, 0:n], ALU.mult), [k_O, k_dnb], [k_aT])

                        units = [(blk, g, par) for blk in range(16) for g in range(2) for par in range(2)]

                        def stage_a(u):
                            blk, g, par = u
                            w, c0 = blk // 4, blk * 128
                            R = slice(par * 64, par * 64 + 64)
                            pts = []
                            for kb in ([blk - 1] if blk > 0 else []) + [blk]:
                                S, k_S = nb()
                                P.mm([lambda: pe.matmul(S[:, 0:256], lhsT=kT2[R, g, kb * 128:(kb + 1) * 128],
                                                        rhs=qT[R, 2 * g:2 * g + 2, c0:c0 + 128], start=True, stop=True)],
                                     [k_kTm[kb // 4][g], k_qTm[w][2 * g], k_qTm[w][2 * g + 1]], [k_S])
                                j = pti[0] % 8
                                pti[0] += 1
                                A(lambda: act.activation(PT[j][:].rearrange("p a t -> p (a t)"), S[:, 0:256], AF.Exp), [k_S], [k_PT[j]])
                                msk = maskC if kb == blk else maskP
                                V(lambda: vec.tensor_tensor(PT[j][:], PT[j][:], msk[:].unsqueeze(1).to_broadcast([128, 2, 128]), ALU.mult),
                                  [k_PT[j], k_const], [k_PT[j]])
                                pts.append((j, kb))
                            return pts

                        def stage_b(u, pts):
                            blk, g, par = u
                            c0 = blk * 128
                            R = slice(par * 64, par * 64 + 64)
                            O, k_O = nb()
                            fns = []
                            for ci in range(2):
                                for n_, (j, kb) in enumerate(pts):
                                    if par == 0:
                                        fns.append(lambda ci=ci, j=j, kb=kb, n_=n_: pe.matmul(
                                            O[0:64, ci * 128:(ci + 1) * 128], lhsT=vtok[:, kb, g * 64:(g + 1) * 64], rhs=PT[j][:, ci, :],
                                            start=(n_ == 0), stop=(n_ == len(pts) - 1)))
                                    else:
                                        fns.append(lambda ci=ci, j=j, kb=kb, n_=n_: pe.matmul(
                                            O[:, ci * 128:(ci + 1) * 128], lhsT=vpad[:, kb, g, :], rhs=PT[j][:, ci, :],
                                            start=(n_ == 0), stop=(n_ == len(pts) - 1)))
                            P.mm(fns, [k_PT[j] for j, _ in pts] + [k_v[kb] for _, kb in pts], [k_O])
                            Dn, k_Dn = nb()
                            P.mm([lambda j=j, n_=n_: pe.matmul(Dn[:, 0:256], lhsT=onesb[:], rhs=PT[j][:].rearrange("p a t -> p (a t)"),
                                                                start=(n_ == 0), stop=(n_ == len(pts) - 1)) for n_, (j, kb) in enumerate(pts)],
                                 [k_PT[j] for j, _ in pts] + [k_const], [k_Dn])
                            normalize(R, g, par, O[R, 0:256].rearrange("p (a t) -> p a t", a=2), k_O,
                                      Dn[R, 0:256].rearrange("p (a t) -> p a t", a=2), k_Dn, c0, 128)

                        pti = [0]
                        prev = None
                        for u in units:
                            pts_u = stage_a(u)
                            if prev is not None:
                                stage_b(*prev)
                            prev = (u, pts_u)
                        stage_b(*prev)
                        kcd = sb(ph, "kcd", [128, 16, 2, 2, 64], BF16); k_kcd = Tk()
                        vc = sb(ph, "vc", [128, 16, 128], BF16); vcp = sb(ph, "vcp", [128, 16, 2, 128], BF16); k_vc = Tk()
                        kcT2 = sb(ph, "kcT2", [128, 2, 16, 128], BF16); k_kcT = Tk()
                        V(lambda: vec.memset(vcp[:], 0.0), w=[k_vc])
                        cksrc = ck.rearrange("m s (g d) -> s m g d", g=2)
                        for g in range(2):
                            for dup in range(2):
                                P.dma(P.pool, kcd[:, :, g, dup, :], cksrc[:, :, g, :], writes=[k_kcd])
                        P.dma(P.pool, vc[:], cv.rearrange("m s c -> s m c"), writes=[k_vc])
                        for g in range(2):
                            P.dma(P.pool, vcp[:, :, g, 64:128], cv.rearrange("m s (g d) -> s m g d", g=2)[:, :, g, :], writes=[k_vc])
                        for g in range(2):
                            for half in range(2):
                                tb, k_tb = ntb()
                                P.mm([lambda m=m: pe.transpose(tb[:, (m % 8) * 128:(m % 8 + 1) * 128],
                                                               kcd[:, m, g, :, :].rearrange("p a d -> p (a d)"), idb[:])
                                      for m in range(half * 8, half * 8 + 8)], [k_kcd, k_id], [k_tb])
                                V(lambda: vec.tensor_copy(kcT2[:, g, half * 8:half * 8 + 8, :], tb[:].rearrange("p (m s) -> p m s", m=8)),
                                  [k_tb], [k_kcT])
                        PTc = sb(ph, "PTc", [128, 16, 2, 4], BF16); k_PTc = Tk()
                        PTn = sb(ph, "PTn", [64, 2, 64], BF16); k_PTn = Tk()
                        sc0 = 2048
                        for g in range(2):
                            for par in range(2):
                                R = slice(par * 64, par * 64 + 64)
                                S, k_S = nb()
                                P.mm([lambda m=m: pe.matmul(S[:, m * 8:(m + 1) * 8], lhsT=kcT2[R, g, m, :],
                                                            rhs=qT[R, 2 * g:2 * g + 2, sc0 + 4 * m:sc0 + 4 * m + 4], start=True, stop=True)
                                      for m in range(16)], [k_kcT, k_qTm[4][2 * g], k_qTm[4][2 * g + 1]], [k_S])
                                A(lambda: act.activation(PTc[:].rearrange("p m a t -> p (m a t)"), S[:, 0:128], AF.Exp), [k_S], [k_PTc])
                                V(lambda: vec.tensor_tensor(PTc[:].rearrange("p m a t -> p (m a) t"), PTc[:].rearrange("p m a t -> p (m a) t"),
                                                            maskP[:, 0:4].unsqueeze(1).to_broadcast([128, 32, 4]), ALU.mult),
                                  [k_PTc, k_const], [k_PTc])
                                S2, k_S2 = nb()
                                P.mm([lambda: pe.matmul(S2[0:64, 0:128], lhsT=kT2[R, g, sc0:sc0 + 64], rhs=qT[R, 2 * g:2 * g + 2, sc0:sc0 + 64],
                                                        start=True, stop=True)], [k_kTm[4][g], k_qTm[4][2 * g], k_qTm[4][2 * g + 1]], [k_S2])
                                A(lambda: act.activation(PTn[:].rearrange("p a t -> p (a t)"), S2[0:64, 0:128], AF.Exp), [k_S2], [k_PTn])
                                V(lambda: vec.tensor_tensor(PTn[:], PTn[:], maskN[:].unsqueeze(1).to_broadcast([64, 2, 64]), ALU.mult),
                                  [k_PTn, k_const], [k_PTn])
                                O, k_O = nb()
                                Ov = O[:, 0:128].rearrange("p (a t) -> p a t", a=2)
                                fns = []
                                if par == 0:
                                    fns.append(lambda: pe.matmul(O[0:64, 0:128], lhsT=vtok[0:64, 16, g * 64:(g + 1) * 64],
                                                                 rhs=PTn[:].rearrange("p a t -> p (a t)"), start=True, stop=False))
                                    for m in range(16):
                                        fns.append(lambda m=m: pe.matmul(Ov[0:64, :, 4 * m:4 * m + 4], lhsT=vc[:, m, g * 64:(g + 1) * 64],
                                                                         rhs=PTc[:, m, :, :], start=False, stop=(m == 15), skip_group_check=True))
                                else:
                                    fns.append(lambda: pe.matmul(O[:, 0:128], lhsT=vpad[0:64, 16, g, :],
                                                                 rhs=PTn[:].rearrange("p a t -> p (a t)"), start=True, stop=False))
                                    for m in range(16):
                                        fns.append(lambda m=m: pe.matmul(Ov[:, :, 4 * m:4 * m + 4], lhsT=vcp[:, m, g, :],
                                                                         rhs=PTc[:, m, :, :], start=False, stop=(m == 15), skip_group_check=True))
                                P.mm(fns, [k_PTn, k_PTc, k_vc, k_v[16]], [k_O])
                                Dn, k_Dn = nb()
                                Dv = Dn[:, 0:128].rearrange("p (a t) -> p a t", a=2)
                                fns = [lambda: pe.matmul(Dn[:, 0:128], lhsT=onesb[0:64, :], rhs=PTn[:].rearrange("p a t -> p (a t)"),
                                                         start=True, stop=False)]
                                for m in range(16):
                                    fns.append(lambda m=m: pe.matmul(Dv[:, :, 4 * m:4 * m + 4], lhsT=onesb[:], rhs=PTc[:, m, :, :],
                                                                     start=False, stop=(m == 15), skip_group_check=True))
                                P.mm(fns, [k_PTn, k_PTc, k_const], [k_Dn])
                                normalize(R, g, par, Ov[R], k_O, Dv[R], k_Dn, sc0, 64)
                        P.barrier()
            if STAGE >= 3:
                gT = sb(L0, "gT", [128, 4, NTOK], BF16); k_gT = Tk()
                with ExitStack() as ph:
                    pt = lambda n: sb(ph, "s5_" + n, [128, 16], F32)
                    mag = pt("mag"); th = pt("th"); kf = pt("kf")
                    sn = pt("sn"); cs = pt("cs"); ab = pt("ab"); lbr = pt("lbr"); lbi = pt("lbi"); den = pt("den")
                    a1 = pt("a1"); cfr = pt("cfr"); cfi = pt("cfi"); t0 = pt("t0"); t1s = pt("t1s")
                    C256 = pt("C256"); S256 = pt("S256"); Enc = pt("Enc"); Ens = pt("Ens")
                    ki = sb(ph, "ki", [128, 16], I32)
                    TT = lambda o, a, b, op: V(lambda: vec.tensor_tensor(o, a, b, op), [k_p], [k_p])
                    A(lambda: act.activation(dtt[:], dtt[:], AF.Exp), [k_p], [k_p])
                    TT(t0[:], lre[:], dtt[:], ALU.mult)
                    A(lambda: act.activation(mag[:], t0[:], AF.Exp), [k_p], [k_p])
                    TT(th[:], lim[:], dtt[:], ALU.mult)
                    A(lambda: act.activation(ki[:], th[:], AF.Copy, scale=1.0 / (2 * PI)), [k_p], [k_p])
                    V(lambda: vec.tensor_copy(kf[:], ki[:]), [k_p], [k_p])
                    V(lambda: vec.scalar_tensor_tensor(th[:], kf[:], -2 * PI, th[:], ALU.mult, ALU.add), [k_p], [k_p])
                    V(lambda: vec.tensor_scalar(th[:], th[:], PI, -PI, ALU.min, ALU.max), [k_p], [k_p])
                    A(lambda: act.activation(sn[:], th[:], AF.Sin), [k_p], [k_p])
                    A(lambda: act.activation(ab[:], th[:], AF.Abs), [k_p], [k_p])
                    A(lambda: act.activation(cs[:], ab[:], AF.Sin, scale=-1.0, bias=halfpi[:, 0:1]), [k_p, k_const], [k_p])
                    TT(lbr[:], mag[:], cs[:], ALU.mult); TT(lbi[:], mag[:], sn[:], ALU.mult)
                    TT(den[:], lre[:], lre[:], ALU.mult); TT(t0[:], lim[:], lim[:], ALU.mult); TT(den[:], den[:], t0[:], ALU.add)
                    V(lambda: vec.reciprocal(den[:], den[:]), [k_p], [k_p])
                    V(lambda: vec.tensor_scalar(a1[:], lbr[:], -1.0, None, ALU.add), [k_p], [k_p])
                    TT(t0[:], a1[:], lre[:], ALU.mult); TT(t1s[:], lbi[:], lim[:], ALU.mult); TT(t0[:], t0[:], t1s[:], ALU.add)
                    TT(cfr[:], t0[:], den[:], ALU.mult)
                    TT(t0[:], lbi[:], lre[:], ALU.mult); TT(t1s[:], a1[:], lim[:], ALU.mult); TT(t0[:], t0[:], t1s[:], ALU.subtract)
                    TT(cfi[:], t0[:], den[:], ALU.mult)
                    Bpr = sb(ph, "Bpr", [128, 16, 128], BF16); Bpi = sb(ph, "Bpi", [128, 16, 128], BF16); k_Bp = Tk()
                    Cpr = sb(ph, "Cpr", [128, 16, 128], BF16); Cpi = sb(ph, "Cpi", [128, 16, 128], BF16); k_Cp = Tk()
                    Cnr = sb(ph, "Cnr", [128, 16, 128], BF16)
                    Dd = sb(ph, "Dd", [128, 4, 128], BF16); k_Dd = Tk()
                    ysb = sb(ph, "ysb", [128, 512], F32); y2 = sb(ph, "y2", [128, 512], F32); k_ge = Tk(); k_ysb = Tk()
                    k_gTq = [k_gT, Tk(), Tk(), Tk()]
                    pp = ExitStack()
                    cosT = sb(pp, "cosT", [128, 16, 256], F32); sinT = sb(pp, "sinT", [128, 16, 256], F32); k_E = Tk()
                    with ExitStack() as ph2:
                        bre_t = sb(ph2, "bre_t", [128, 16, 16], F32); bim_t = sb(ph2, "bim_t", [128, 16, 16], F32); k_bb = Tk()
                        P.dma(P.sp, bre_t[:], bre_d.rearrange("(j q) c -> q j c", q=128), writes=[k_bb])
                        P.dma(P.sp, bim_t[:], bim_d.rearrange("(j q) c -> q j c", q=128), writes=[k_bb], part=True)
                        bbr = sb(ph2, "bbr", [128, 16, 16], F32); bbi = sb(ph2, "bbi", [128, 16, 16], F32); tb1 = sb(ph2, "tb1", [128, 16, 16], F32)
                        BTr = sb(ph2, "BTr", [128, 16, 128], F32); BTi = sb(ph2, "BTi", [128, 16, 128], F32)
                        CN = sb(ph2, "CN", [128, 1, 4, 128], F32); CT = sb(ph2, "CT", [128, 2, 4, 128], F32)
                        crb = cfr[:, :].unsqueeze(2).to_broadcast([128, 16, 16]); cib = cfi[:, :].unsqueeze(2).to_broadcast([128, 16, 16])
                        V(lambda: vec.tensor_tensor(bbr[:], bre_t[:], crb, ALU.mult), [k_p, k_bb], [k_p])
                        TT(tb1[:], bim_t[:], cib, ALU.mult); TT(bbr[:], bbr[:], tb1[:], ALU.subtract)
                        TT(bbi[:], bim_t[:], crb, ALU.mult); TT(tb1[:], bre_t[:], cib, ALU.mult); TT(bbi[:], bbi[:], tb1[:], ALU.add)
                        V(lambda: vec.memset(BTr[:], 0.0), [k_p], [k_p]); V(lambda: vec.memset(BTi[:], 0.0), [k_p], [k_p])
                        for g2 in range(2):
                            H = slice(g2 * 64, g2 * 64 + 64)
                            for b_ in range(4):
                                cs_ = slice((2 * b_ + g2) * 16, (2 * b_ + g2) * 16 + 16)
                                V(lambda: vec.tensor_copy(BTr[H, b_::4, cs_], bbr[H, b_::4, :]), [k_p], [k_p])
                                V(lambda: vec.tensor_copy(BTi[H, b_::4, cs_], bbi[H, b_::4, :]), [k_p], [k_p])
                        for (BT, Bp) in ((BTr, Bpr), (BTi, Bpi)):
                            for jj in range(4):
                                bk, k_bk = nb()
                                P.mm([lambda j=j: pe.transpose(bk[:, (j % 4) * 128:(j % 4 + 1) * 128], BT[:, j, :], idf[:])
                                      for j in range(4 * jj, 4 * jj + 4)], [k_p, k_id], [k_bk])
                                A(lambda: act.copy(Bp[:, 4 * jj:4 * jj + 4, :].rearrange("p a b -> p (a b)"), bk[:, :]), [k_bk], [k_Bp])
                        for ri, cd in enumerate((cre_d, cim_d)):
                            src = cd.rearrange("(i q) p -> q i p", q=128)
                            P.dma(P.sp, CN[:, 0, :, 0:64], src, writes=[k_p])
                            P.dma(P.sp, CN[:, 0, :, 64:128], src, writes=[k_p])
                            bk, k_bk = nb()
                            P.mm([lambda i=i: pe.transpose(bk[:, i * 128:(i + 1) * 128], CN[:, 0, i, :], idf[:]) for i in range(4)],
                                 [k_p, k_id], [k_bk])
                            A(lambda: act.copy(CT[:, ri, :, :].rearrange("p a b -> p (a b)"), bk[:, :]), [k_bk], [k_p])
                        V(lambda: vec.memset(Cpr[:], 0.0), w=[k_Cp]); V(lambda: vec.memset(Cpi[:], 0.0), w=[k_Cp])
                        for g2 in range(2):
                            H = slice(g2 * 64, g2 * 64 + 64)
                            for b_ in range(4):
                                cs_ = slice((2 * b_ + g2) * 16, (2 * b_ + g2) * 16 + 16)
                                V(lambda: vec.tensor_copy(Cpr[H, b_::4, cs_], CT[H, 0, :, cs_]), [k_p], [k_Cp])
                                V(lambda: vec.tensor_scalar(Cpi[H, b_::4, cs_], CT[H, 1, :, cs_], -1.0, None, ALU.mult), [k_p], [k_Cp])
                        V(lambda: vec.tensor_scalar(Cnr[:], Cpr[:], -1.0, None, ALU.mult), [k_Cp], [k_Cp])
                        dT, k_dT = colvec(ph2, "dT", dsk_d, 4)
                        for i in range(4):
                            V(lambda: vec.tensor_scalar(Dd[:, i, :], idf[:], dT[:, i:i + 1], None, ALU.mult), [k_dT, k_id], [k_Dd])
                        P.barrier()
                        T1, T2 = BTr, BTi
                        V(lambda: vec.memset(cosT[:, :, 0:1], 1.0), w=[k_E]); V(lambda: vec.memset(sinT[:, :, 0:1], 0.0), w=[k_E])
                        TE = lambda o, a, b, op: V(lambda: vec.tensor_tensor(o, a, b, op), [k_E, k_p], [k_E])
                        n = 1
                        while n <= 256:
                            if n == 1:
                                V(lambda: vec.tensor_copy(Enc[:], cs[:]), [k_p], [k_p]); V(lambda: vec.tensor_copy(Ens[:], sn[:]), [k_p], [k_p])
                            else:
                                TE(t0[:], cosT[:, :, n - 1], cs[:], ALU.mult); TE(t1s[:], sinT[:, :, n - 1], sn[:], ALU.mult)
                                TE(Enc[:], t0[:], t1s[:], ALU.subtract)
                                TE(t0[:], sinT[:, :, n - 1], cs[:], ALU.mult); TE(t1s[:], cosT[:, :, n - 1], sn[:], ALU.mult)
                                TE(Ens[:], t0[:], t1s[:], ALU.add)
                            if n == 256:
                                V(lambda: vec.tensor_copy(C256[:], Enc[:]), [k_p, k_E], [k_p]); V(lambda: vec.tensor_copy(S256[:], Ens[:]), [k_p, k_E], [k_p])
                                break
                            cb_ = Enc[:, :].unsqueeze(2).to_broadcast([128, 16, n]); sb_ = Ens[:, :].unsqueeze(2).to_broadcast([128, 16, n])
                            TE(T1[:, :, 0:n], cosT[:, :, 0:n], cb_, ALU.mult); TE(T2[:, :, 0:n], sinT[:, :, 0:n], sb_, ALU.mult)
                            TE(cosT[:, :, n:2 * n], T1[:, :, 0:n], T2[:, :, 0:n], ALU.subtract)
                            TE(T1[:, :, 0:n], sinT[:, :, 0:n], cb_, ALU.mult); TE(T2[:, :, 0:n], cosT[:, :, 0:n], sb_, ALU.mult)
                            TE(sinT[:, :, n:2 * n], T1[:, :, 0:n], T2[:, :, 0:n], ALU.add)
                            n *= 2
                        P.barrier()

                    def gelu_out(yb, k_yb, q, c0, n):
                        A(lambda: act.copy(ysb[:, 0:n], yb[:, 0:n]), [k_yb], [k_ysb])
                        A(lambda: act.activation(y2[:, 0:n], yb[:, 0:n], AF.Square), [k_yb], [k_ge])
                        V(lambda: vec.tensor_scalar(y2[:, 0:n], y2[:, 0:n], 0.044715, 1.0, ALU.mult, ALU.add), [k_ge], [k_ge])
                        V(lambda: vec.tensor_tensor(y2[:, 0:n], y2[:, 0:n], ysb[:, 0:n], ALU.mult), [k_ge, k_ysb], [k_ge])
                        A(lambda: act.activation(y2[:, 0:n], y2[:, 0:n], AF.Sigmoid, scale=1.5957691216057308), [k_ge], [k_ge])
                        V(lambda: vec.tensor_tensor(gT[:, q, c0:c0 + n], y2[:, 0:n], ysb[:, 0:n], ALU.mult), [k_ge, k_ysb], [k_gTq[q]])

                    W = lambda n_: sb(pp, n_, [128, 512], F32)
                    xre = W("xre"); xim = W("xim"); w1 = W("w1"); w2 = W("w2"); w3 = W("w3"); w4 = W("w4")
                    vre2 = [W("vre0"), W("vre1")]; vim2 = [W("vim0"), W("vim1")]
                    hh2 = [sb(pp, "hh%d" % i_, [128, 4, 512], BF16) for i_ in range(2)]
                    pend = []
                    k_x = Tk(); k_vv2 = [Tk(), Tk()]; k_h2 = [Tk(), Tk()]; k_pw = Tk()
                    k_w = [Tk() for _ in range(4)]; k_xr = Tk(); k_xi = Tk()
                    k_vr2 = [Tk(), Tk()]; k_vi2 = [Tk(), Tk()]; k_cr = Tk(); k_ci = Tk(); k_t0 = Tk(); k_t1 = Tk()
                    gp_ = nc.gpsimd
                    car = sb(pp, "car", [128, 16, 2], F32); ctmp = sb(pp, "ctmp", [128, 2], F32); k_car = Tk()
                    fin = sb(pp, "fin", [128, 2, 16], F32); k_fin = Tk()
                    v3 = lambda t: t[:, :].rearrange("p (a t) -> p a t", a=2)
                    def emit_bu(w_, j_):
                        for ri_, Bp_ in enumerate((Bpr, Bpi)):
                            bk_, k_bk_ = banks[(2 * j_ + ri_) % 4]
                            P.mm([lambda: pe.matmul(bk_[:, :], lhsT=Bp_[:, j_, :], rhs=uT[:, j_ // 4, w_ * 512:w_ * 512 + 512], start=True, stop=True)],
                                 [k_Bp, k_uTm[w_][j_ // 4]], [k_bk_])

                    emit_bu(0, 0)
                    for w in range(4):
                        c0 = w * 512
                        for j in range(16):
                            q = j // 4
                            br_, k_br = banks[(2 * j) % 4]
                            bi_, k_bi = banks[(2 * j + 1) % 4]
                            cb_ = cosT[:, j, :].unsqueeze(1).to_broadcast([128, 2, 256]); sb_ = sinT[:, j, :].unsqueeze(1).to_broadcast([128, 2, 256])
                            V(lambda: vec.tensor_tensor(v3(w1), v3(br_), cb_, ALU.mult), [k_br, k_E], [k_w[0]])
                            V(lambda: vec.tensor_tensor(v3(w2), v3(bi_), sb_, ALU.mult), [k_bi, k_E], [k_w[1]])
                            V(lambda: vec.tensor_tensor(v3(w3), v3(bi_), cb_, ALU.mult), [k_bi, k_E], [k_w[2]])
                            V(lambda: vec.tensor_tensor(v3(w4), v3(br_), sb_, ALU.mult), [k_br, k_E], [k_w[3]])
                            V(lambda: vec.tensor_tensor(xre[:], w1[:], w2[:], ALU.add), [k_w[0], k_w[1]], [k_xr])
                            V(lambda: vec.tensor_tensor(xim[:], w3[:], w4[:], ALU.subtract), [k_w[2], k_w[3]], [k_xi])
                            vre, vim, k_vv = vre2[j % 2], vim2[j % 2], k_vv2[j % 2]
                            hh, k_h = hh2[j % 2], k_h2[j % 2]
                            rb = mag[:, j:j + 1].to_broadcast([128, 256])
                            k_vr, k_vi = k_vr2[j % 2], k_vi2[j % 2]
                            for c in range(2):
                                cs_ = slice(c * 256, c * 256 + 256)
                                first = (w == 0 and c == 0)
                                sc_re = lambda: V(lambda: vec.tensor_tensor_scan(vre[:, cs_], rb, xre[:, cs_], 0.0 if first else car[:, j, 0:1], ALU.mult, ALU.add),
                                                  [k_xr, k_cr, k_p], [k_vr])
                                sc_im = lambda: V(lambda: vec.tensor_tensor_scan(vim[:, cs_], rb, xim[:, cs_], 0.0 if first else car[:, j, 1:2], ALU.mult, ALU.add),
                                                  [k_xi, k_ci, k_p], [k_vi])
                                if c == 0:
                                    sc_re(); sc_im()
                                else:
                                    sc_im(); sc_re()
                                lr = vre[:, c * 256 + 255:c * 256 + 256]; li = vim[:, c * 256 + 255:c * 256 + 256]
                                if w == 3 and c == 1:
                                    V(lambda: vec.tensor_tensor(ctmp[:, 0:1], li, sinT[:, j, 255:256], ALU.mult), [k_vi, k_E], [k_t0])
                                    V(lambda: vec.tensor_tensor(ctmp[:, 1:2], lr, sinT[:, j, 255:256], ALU.mult), [k_vr, k_E], [k_t1])
                                    V(lambda: vec.scalar_tensor_tensor(fin[:, 0, j:j + 1], lr, cosT[:, j, 255:256], ctmp[:, 0:1], ALU.mult, ALU.subtract),
                                      [k_vr, k_E, k_t0], [k_fin])
                                    V(lambda: vec.scalar_tensor_tensor(fin[:, 1, j:j + 1], li, cosT[:, j, 255:256], ctmp[:, 1:2], ALU.mult, ALU.add),
                                      [k_vi, k_E, k_t1], [k_fin])
                                else:
                                    V(lambda: vec.tensor_tensor(ctmp[:, 1:2], lr, S256[:, j:j + 1], ALU.mult), [k_vr, k_p], [k_t1])
                                    V(lambda: vec.tensor_tensor(ctmp[:, 0:1], li, S256[:, j:j + 1], ALU.mult), [k_vi, k_p], [k_t0])
                                    V(lambda: vec.scalar_tensor_tensor(car[:, j, 1:2], li, C256[:, j:j + 1], ctmp[:, 1:2], ALU.mult, ALU.add),
                                      [k_vi, k_p, k_t1], [k_ci])
                                    V(lambda: vec.scalar_tensor_tensor(car[:, j, 0:1], lr, C256[:, j:j + 1], ctmp[:, 0:1], ALU.mult, ALU.subtract),
                                      [k_vr, k_p, k_t0], [k_cr])
                            k_vv = k_vv2[j % 2]
                            if j < 15:
                                emit_bu(w, j + 1)
                            elif w < 3:
                                emit_bu(w + 1, 0)
                            hv = lambda i_: hh[:, i_, :].rearrange("p (a t) -> p a t", a=2)
                            G(lambda: gp_.tensor_tensor(hv(0), v3(vre), cb_, ALU.mult), [k_vr, k_E], [k_h])
                            G(lambda: gp_.tensor_tensor(hv(1), v3(vim), sb_, ALU.mult), [k_vi, k_E], [k_h])
                            G(lambda: gp_.tensor_tensor(hv(2), v3(vre), sb_, ALU.mult), [k_vr, k_E], [k_h])
                            G(lambda: gp_.tensor_tensor(hv(3), v3(vim), cb_, ALU.mult), [k_vi, k_E], [k_h])
                            if pend:
                                gelu_out(*pend.pop())
                            if j % 4 == 0:
                                yb, k_yb = banks[4 + (q % 2)]
                                P.mm([lambda: pe.matmul(yb[:, :], lhsT=Dd[:, q, :], rhs=uT[:, q, c0:c0 + 512], start=True, stop=False)],
                                     [k_Dd, k_uTm[w][q]], [k_yb])
                            P.mm([lambda: pe.matmul(yb[:, :], lhsT=Cpr[:, j, :], rhs=hh[:, 0, :], start=False, stop=False),
                                  lambda: pe.matmul(yb[:, :], lhsT=Cnr[:, j, :], rhs=hh[:, 1, :], start=False, stop=False),
                                  lambda: pe.matmul(yb[:, :], lhsT=Cpi[:, j, :], rhs=hh[:, 2, :], start=False, stop=False),
                                  lambda: pe.matmul(yb[:, :], lhsT=Cpi[:, j, :], rhs=hh[:, 3, :], start=False, stop=(j % 4 == 3))],
                                 [k_Cp, k_h], [k_yb])
                            if j % 4 == 3:
                                pend.append((yb, k_yb, q, c0, 512))
                    if pend:
                        gelu_out(*pend.pop())
                    for ri, o_ in enumerate((s5rp_o, s5ip_o)):
                        bk, k_bk = nb()
                        P.mm([lambda: pe.transpose(bk[0:16, 0:128], fin[:, ri, :], idf[:])], [k_fin, k_id], [k_bk])
                        V(lambda: vec.tensor_copy(w1[0:16, 0:128], bk[0:16, 0:128]), [k_bk], [k_x])
                        P.dma(P.sp, o_[:, :], w1[0:16, 0:128], reads=[k_x])
                        P.barrier()
                    pp.close()
                    SX = sb(ph, "SX", [16, 2048], F32); k_S0 = Tk()
                    hs = sb(ph, "hs", [128, 2, 16, 16, 5], F32); k_hs = Tk()
                    bus = sb(ph, "bus", [128, 2, 16, 64], F32); k_bus = Tk()
                    hsb = sb(ph, "hsb", [128, 2, 16, 64], BF16); k_hsb = Tk()
                    st1 = sb(ph, "st1", [128, 16, 16], F32); st2 = sb(ph, "st2", [128, 16, 16], F32); k_st = Tk()
                    k_SF = k_S0
                    for ri, s_ in enumerate((s5re0, s5im0)):
                        P.dma(P.sp, SX[:, :], s_[:, :], writes=[k_S0])
                        bk, k_bk = nb()
                        P.mm([lambda j=j: pe.transpose(bk[:, j * 16:(j + 1) * 16], SX[0:16, j * 128:(j + 1) * 128], idf[0:16, 0:16])
                              for j in range(16)], [k_S0, k_id], [k_bk])
                        V(lambda: vec.tensor_copy(hs[:, ri, :, :, 0], bk[:, 0:256].rearrange("p (j m) -> p j m", j=16)), [k_bk], [k_hs])
                    for j in range(16):
                        for ri, Bp in enumerate((Bpr, Bpi)):
                            bk, k_bk = nb()
                            P.mm([lambda: pe.matmul(bk[:, 0:64], lhsT=Bp[:, j, :], rhs=uT[:, j // 4, 2048:2112], start=True, stop=True)],
                                 [k_Bp, k_uTm[4][j // 4]], [k_bk])
                            if ri == 0:
                                A(lambda: act.copy(bus[:, ri, j, :], bk[:, 0:64]), [k_bk], [k_bus])
                            else:
                                V(lambda: vec.tensor_copy(bus[:, ri, j, :], bk[:, 0:64]), [k_bk], [k_bus])
                    lrb = lbr[:, :].unsqueeze(2).to_broadcast([128, 16, 16]); lib = lbi[:, :].unsqueeze(2).to_broadcast([128, 16, 16])
                    busv = bus[:].rearrange("p r j (m t) -> p r j m t", t=4)
                    for t in range(4):
                        pr_, pi_ = hs[:, 0, :, :, t], hs[:, 1, :, :, t]
                        R_ = [k_hs, k_p, k_st, k_bus]
                        V(lambda: vec.tensor_tensor(st1[:], pr_, lrb, ALU.mult), R_, [k_st])
                        V(lambda: vec.tensor_tensor(st2[:], pi_, lib, ALU.mult), R_, [k_st])
                        V(lambda: vec.tensor_tensor(st1[:], st1[:], st2[:], ALU.subtract), R_, [k_st])
                        V(lambda: vec.tensor_tensor(hs[:, 0, :, :, t + 1], st1[:], busv[:, 0, :, :, t], ALU.add), R_, [k_hs])
                        V(lambda: vec.tensor_tensor(st1[:], pi_, lrb, ALU.mult), R_, [k_st])
                        V(lambda: vec.tensor_tensor(st2[:], pr_, lib, ALU.mult), R_, [k_st])
                        V(lambda: vec.tensor_tensor(st1[:], st1[:], st2[:], ALU.add), R_, [k_st])
                        V(lambda: vec.tensor_tensor(hs[:, 1, :, :, t + 1], st1[:], busv[:, 1, :, :, t], ALU.add), R_, [k_hs])
                    for ri in range(2):
                        V(lambda: vec.tensor_copy(hsb[:, ri, :, :].rearrange("p j (m t) -> p j m t", t=4), hs[:, ri, :, :, 1:5]), [k_hs], [k_hsb])
                    for q in range(4):
                        yb, k_yb = nb()
                        fns = [lambda: pe.matmul(yb[:, 0:64], lhsT=Dd[:, q, :], rhs=uT[:, q, 2048:2112], start=True, stop=False)]
                        for j in range(4 * q, 4 * q + 4):
                            fns.append(lambda j=j: pe.matmul(yb[:, 0:64], lhsT=Cpr[:, j, :], rhs=hsb[:, 0, j, :], start=False, stop=False))
                            fns.append(lambda j=j: pe.matmul(yb[:, 0:64], lhsT=Cpi[:, j, :], rhs=hsb[:, 1, j, :], start=False, stop=(j == 4 * q + 3)))
                        P.mm(fns, [k_Dd, k_Cp, k_hsb, k_uTm[4][q]], [k_yb])
                        gelu_out(yb, k_yb, q, 2048, 64)
                    for ri, o_ in enumerate((s5rs_o, s5is_o)):
                        for jj in range(4):
                            bk, k_bk = nb()
                            P.mm([lambda j=j: pe.transpose(bk[0:16, (j % 4) * 128:(j % 4 + 1) * 128], hs[:, ri, j, :, 4], idf[:])
                                  for j in range(4 * jj, 4 * jj + 4)], [k_hs, k_id], [k_bk])
                            V(lambda: vec.tensor_copy(SX[:, jj * 512:(jj + 1) * 512], bk[0:16, :]), [k_bk], [k_SF])
                        P.dma(P.sp, o_[:, :], SX[:, :], reads=[k_SF])
                    P.barrier()
                if STAGE >= 4:
                    with ExitStack() as ph:
                        wglu_sb, k_wglu = load_w(ph, "wglu_sb", wglu_d, 512, 512)
                        wout_sb, k_wout = load_w(ph, "wout_sb", wout_d, 1024, 1024)
                        gpost, k_gpost = load_gpost(ph, "gpost0", npo[0])
                        sT = sb(ph, "sT", [128, 4, NTOK], BF16); k_sTw = [Tk() for _ in range(5)]
                        sg = sb(ph, "sgl", [128, 512], F32); k_sg = Tk()
                        junk = sb(ph, "pjunk", [128, 512], BF16); tmp = sb(ph, "ptmp", [128, 512], F32)
                        pbufs = (junk, Tk(), tmp, Tk(), sb(ph, "ptmp2", [128, 512], F32), Tk())
                        for w, (c0, n) in enumerate(WIDE):
                            for m in range(4):
                                bk, k_bk = nb()
                                P.mm([lambda k=k: pe.matmul(bk[:, 0:n], lhsT=wglu_sb[:, k, m * 128:(m + 1) * 128], rhs=gT[:, k, c0:c0 + n],
                                                            start=(k == 0), stop=(k == 3)) for k in range(4)], [k_wglu] + k_gTq, [k_bk])
                                A(lambda: act.activation(sg[:, 0:n], bk[:, 0:n], AF.Sigmoid), [k_bk], [k_sg])
                                V(lambda: vec.tensor_tensor(sT[:, m, c0:c0 + n], gT[:, m, c0:c0 + n], sg[:, 0:n], ALU.mult), [k_sg] + k_gTq, [k_sTw[w]])
                        for i in range(NT):
                            r = rows(i)
                            bk2 = [nb(), nb()]
                            for dh in range(2):
                                fns = [lambda c=c: pe.matmul(bk2[dh][0][0:r, :], lhsT=aT[:, c, i * 128:i * 128 + r], rhs=wout_sb[:, c, dh * 512:(dh + 1) * 512],
                                                             start=(c == 0), stop=False) for c in range(4)]
                                fns += [lambda c=c: pe.matmul(bk2[dh][0][0:r, :], lhsT=sT[:, c, i * 128:i * 128 + r], rhs=wout_sb[:, 4 + c, dh * 512:(dh + 1) * 512],
                                                              start=False, stop=(c == 3)) for c in range(4)]
                                P.mm(fns, [k_aT, k_sTw[min(i // 4, 4)], k_wout], [bk2[dh][1]])
                            post_norm_residual(pbufs, i, bk2, gpost, k_gpost)
                        P.barrier()
        P.barrier()
        if STAGE >= 4:
            ffn(0)
        if STAGE >= 5:
            with ExitStack() as L1:
                gpT = sb(L1, "gpT", [128, 8, 30 + 2048], BF16); k_gpw = [Tk() for _ in range(5)]; k_gp = k_gpw[4]
                gsT = sb(L1, "gsT", [128, 8, 16, 34], BF16); k_gs = Tk()
                b1, k_b1 = colvec(L1, "b_pw1", bpw1_d, 16)
                V(lambda: vec.memset(gpT[:, :, 0:30], 0.0), w=[k_gp])
                with ExitStack() as ph:
                    gtail = sb(ph, "gtail", [128, 8, 30], F32); gsn = sb(ph, "gsn", [128, 8, 64], F32); k_gt = Tk()
                    gcol, k_g = colvec(ph, "g_pm1", npm[1], 8)
                    w1_sb, k_w1 = load_w(ph, "wpw1_sb", wpw1_d, 1024, 2048)
                    hT = sb(ph, "hT1", [128, 8, NTOK], BF16); k_hT = [Tk() for _ in range(NT)]
                    norm_T(norm_bufs(ph), range(NT), gcol, k_g, hT, k_hT, 0)
                    sgb = sb(ph, "sgb", [128, 512], F32); k_sgb = Tk()
                    SC = sb(ph, "SC", [120, 4, 1024], BF16); k_SC = Tk()
                    for tI in range(4):
                        for m_ in range(4):
                            P.dma(P.pool, SC[30 * m_:30 * m_ + 30, tI, :], sconv[4 * tI + m_, :, :], writes=[k_SC], max_dma_last_dim=4096)
                    for tI in range(4):
                        tb, k_tb = ntb()
                        P.mm([lambda c=c: pe.transpose(tb[:, c * 120:(c + 1) * 120], SC[0:120, tI, c * 128:(c + 1) * 128], idb[0:120, 0:120])
                              for c in range(8)], [k_SC, k_id], [k_tb])
                        V(lambda: vec.tensor_copy(gsT[:, :, 4 * tI:4 * tI + 4, 0:30], tb[:, 0:960].rearrange("p (c m r) -> p c m r", c=8, m=4)),
                          [k_tb], [k_gs])
                    P.dma(P.sp, convs_o[:, 0:26, :], sconv[:, 4:30, :])
                    for w, (c0, n) in enumerate(WIDE):
                        kr = [k_hT[i] for i in tiles_of(c0, n)] + [k_w1]
                        for c in range(8):
                            ba, k_ba = nb()
                            P.mm([lambda k=k: pe.matmul(ba[:, 0:n], lhsT=w1_sb[:, k, c * 128:(c + 1) * 128], rhs=hT[:, k, c0:c0 + n],
                                                        start=(k == 0), stop=(k == 7)) for k in range(8)], kr, [k_ba])
                            bb_, k_bb = nb()
                            P.mm([lambda k=k: pe.matmul(bb_[:, 0:n], lhsT=w1_sb[:, k, 1024 + c * 128:1024 + (c + 1) * 128], rhs=hT[:, k, c0:c0 + n],
                                                        start=(k == 0), stop=(k == 7)) for k in range(8)], kr, [k_bb])
                            A(lambda: act.activation(sgb[:, 0:n], bb_[:, 0:n], AF.Sigmoid, bias=b1[:, 8 + c:9 + c]), [k_bb, k_b1], [k_sgb])
                            if w < 4:
                                V(lambda: vec.scalar_tensor_tensor(gpT[:, c, 30 + c0:30 + c0 + n], ba[:, 0:n], b1[:, c:c + 1], sgb[:, 0:n], ALU.add, ALU.mult),
                                  [k_ba, k_b1, k_sgb], [k_gpw[w]])
                                if w == 3:
                                    V(lambda: vec.scalar_tensor_tensor(gtail[:, c, :], ba[:, 482:512], b1[:, c:c + 1], sgb[:, 482:512], ALU.add, ALU.mult),
                                      [k_ba, k_b1, k_sgb], [k_gt])
                            else:
                                V(lambda: vec.scalar_tensor_tensor(gsn[:, c, :], ba[:, 0:64], b1[:, c:c + 1], sgb[:, 0:64], ALU.add, ALU.mult),
                                  [k_ba, k_b1, k_sgb], [k_gt])
                                V(lambda: vec.tensor_copy(gsT[:, c, :, 30:34], gsn[:, c, :].rearrange("p (m t) -> p m t", t=4)), [k_gt], [k_gs])
                    OT = sb(ph, "OT", [64, 1024], F32); k_OT = Tk()
                    for (src_, nr) in ((gtail, 30), (gsn, 64)):
                        for h in range(2):
                            bk, k_bk = nb()
                            P.mm([lambda c=c: pe.transpose(bk[0:nr, (c % 4) * 128:(c % 4 + 1) * 128], src_[:, c, :], idf[:])
                                  for c in range(4 * h, 4 * h + 4)], [k_gt, k_id], [k_bk])
                            V(lambda: vec.tensor_copy(OT[0:nr, h * 512:(h + 1) * 512], bk[0:nr, :]), [k_bk], [k_OT])
                        if nr == 30:
                            P.dma(P.sp, convp_o[:, :], OT[0:30, :], reads=[k_OT])
                        else:
                            for m_ in range(16):
                                P.dma(P.sp, convs_o[m_, 26:30, :], OT[4 * m_:4 * m_ + 4, :], reads=[k_OT])
                    P.barrier()
                with ExitStack() as ph:
                    wdwT = sb(ph, "wdwT", [128, 8, 31], F32); k_wdw = Tk()
                    with ExitStack() as tmps:
                        wdn = sb(tmps, "wdn", [31, 1024], F32); k_wdn = Tk()
                        P.dma(P.sp, wdn[:, :], wdw_d[:, :], writes=[k_wdn])
                        bkw, k_bkw = nb()
                        P.mm([lambda c_=c_: pe.transpose(bkw[:, c_ * 31:(c_ + 1) * 31], wdn[0:31, c_ * 128:(c_ + 1) * 128], idf[0:31, 0:31])
                              for c_ in range(8)], [k_wdn, k_id], [k_bkw])
                        V(lambda: vec.tensor_copy(wdwT[:].rearrange("p c j -> p (c j)"), bkw[:, 0:248]), [k_bkw], [k_wdw])
                        P.barrier()
                    bdw, k_bdw = colvec(ph, "bdw", bdw_d, 8); lng, k_lng = colvec(ph, "lng", lng_d, 8); lnb, k_lnb = colvec(ph, "lnb", lnb_d, 8)
                    w2_sb, k_w2 = load_w(ph, "wpw2_sb", wpw2_d, 1024, 1024)
                    b2row = sb(ph, "b2row", [1, 1024], BF16); k_b2 = Tk()
                    P.dma(P.pool, b2row[:], bpw2_d[:, :], writes=[k_b2], max_dma_last_dim=4096)
                    gpost, k_gpost = load_gpost(ph, "gpost1", npo[1])
                    wdg2 = [sb(ph, "wdg%d" % i_, [128, 31, 128], BF16) for i_ in range(2)]; k_wdg2 = [Tk(), Tk()]
                    cT = sb(ph, "cT", [128, 8, 512], F32); k_cTc = [Tk() for _ in range(8)]
                    cb = sb(ph, "cb", [128, 8, 512], BF16); c2b = sb(ph, "c2b", [128, 8, 512], BF16); k_cb = Tk()
                    actT = sb(ph, "actT", [128, 8, 512], BF16); k_actc = [Tk() for _ in range(8)]; k_c2b = Tk()
                    F5 = lambda n_: sb(ph, n_, [128, 512], F32)
                    mean_sb = F5("mean_sb"); m2 = F5("m2"); rstd_sb = F5("rstd_sb"); tt = [F5("tt0"), F5("tt1")]
                    k_st = Tk(); k_tt = [Tk(), Tk()]
                    junk = sb(ph, "cjunk", [128, 512], BF16); tmp = sb(ph, "ctmp2", [128, 512], F32)
                    pbufs = (junk, Tk(), tmp, Tk())
                    cTs = sb(ph, "cTs", [128, 8, 64], F32); k_cTs = Tk()

                    def conv_w(w):
                        if w == 4:
                            return
                        c0, n = WIDE[w]
                        for c in range(8):
                            wdg, k_wdg = wdg2[c % 2], k_wdg2[c % 2]
                            on_pool = c not in (3, 5, 7)
                            bld = (lambda f: G(f, [k_id, k_wdw], [k_wdg])) if on_pool else (lambda f: V(f, [k_id, k_wdw], [k_wdg]))
                            eng_ = nc.gpsimd if on_pool else vec
                            bld(lambda: eng_.tensor_tensor(wdg[:], idb[:].unsqueeze(1).to_broadcast([128, 31, 128]),
                                                           wdwT[:, c, :].unsqueeze(2).to_broadcast([128, 31, 128]), ALU.mult))
                            bk, k_bk = nb()
                            P.mm([lambda j=j: pe.matmul(bk[:, 0:n], lhsT=wdg[:, j, :], rhs=gpT[:, c, c0 + j:c0 + j + n],
                                                        start=(j == 0), stop=(j == 30)) for j in range(31)], [k_wdg] + k_gpw, [k_bk])
                            A(lambda: act.activation(cT[:, c, 0:n], bk[:, 0:n], AF.Identity, bias=bdw[:, c:c + 1]), [k_bk, k_bdw], [k_cTc[c]])
                            if w == 3:
                                bk2_, k_bk2_ = nb()
                                P.mm([lambda j=j: pe.matmul(bk2_[:, 0:64].rearrange("p (m t) -> p m t", t=4), lhsT=wdg[:, j, :], rhs=gsT[:, c, :, j:j + 4],
                                                            start=(j == 0), stop=(j == 30)) for j in range(31)], [k_wdg, k_gs], [k_bk2_])
                                A(lambda: act.activation(cTs[:, c, :], bk2_[:, 0:64], AF.Identity, bias=bdw[:, c:c + 1]), [k_bk2_, k_bdw], [k_cTs])
                    conv_w(0)
                    for w, (c0, n) in enumerate(WIDE):
                        cX = cT if w < 4 else cTs
                        k_cX = k_cTc if w < 4 else [k_cTs] * 8
                        V(lambda: vec.tensor_copy(cb[:, :, 0:n], cX[:, :, 0:n]), k_cX, [k_cb])
                        A(lambda: act.activation(c2b[:, :, 0:n], cX[:, :, 0:n], AF.Square), k_cX, [k_c2b])
                        bm, k_bm = nb()
                        P.mm([lambda c=c: pe.matmul(bm[:, 0:n], lhsT=onesln[:], rhs=cb[:, c, 0:n], start=(c == 0), stop=(c == 7)) for c in range(8)],
                             [k_cb, k_const], [k_bm])
                        bq, k_bq = nb()
                        P.mm([lambda c=c: pe.matmul(bq[:, 0:n], lhsT=onesln[:], rhs=c2b[:, c, 0:n], start=(c == 0), stop=(c == 7)) for c in range(8)],
                             [k_c2b, k_const], [k_bq])
                        A(lambda: act.copy(mean_sb[:, 0:n], bm[:, 0:n]), [k_bm], [k_st])
                        V(lambda: vec.tensor_tensor(m2[:, 0:n], mean_sb[:, 0:n], mean_sb[:, 0:n], ALU.mult), [k_st], [k_st])
                        V(lambda: vec.tensor_tensor(m2[:, 0:n], bq[:, 0:n], m2[:, 0:n], ALU.subtract), [k_bq, k_st], [k_st])
                        A(lambda: act.activation(rstd_sb[:, 0:n], m2[:, 0:n], AF.Sqrt, bias=lneps_t[:, 0:1]), [k_st, k_const], [k_st])
                        V(lambda: vec.reciprocal(rstd_sb[:, 0:n], rstd_sb[:, 0:n]), [k_st], [k_st])
                        for c in range(8):
                            t_, k_t = tt[c % 2], k_tt[c % 2]
                            V(lambda: vec.tensor_tensor(t_[:, 0:n], cX[:, c, 0:n], mean_sb[:, 0:n], ALU.subtract), [k_cX[c], k_st], [k_t])
                            V(lambda: vec.tensor_tensor(t_[:, 0:n], t_[:, 0:n], rstd_sb[:, 0:n], ALU.mult), [k_t, k_st], [k_t])
                            A(lambda: act.activation(actT[:, c, 0:n], t_[:, 0:n], AF.Silu, scale=lng[:, c:c + 1], bias=lnb[:, c:c + 1]),
                              [k_t, k_lng, k_lnb], [k_actc[c]])
                        if w + 1 < len(WIDE):
                            conv_w(w + 1)
                        for i in tiles_of(c0, n):
                            r = rows(i)
                            lc = i * 128 - c0
                            bk2 = [nb(), nb()]
                            for dh in range(2):
                                fns = [lambda c=c: pe.matmul(bk2[dh][0][0:r, :], lhsT=actT[:, c, lc:lc + r], rhs=w2_sb[:, c, dh * 512:(dh + 1) * 512],
                                                             start=(c == 0), stop=False) for c in range(8)]
                                fns.append(lambda: pe.matmul(bk2[dh][0][0:r, :], lhsT=onesb[0:1, 0:r], rhs=b2row[0:1, dh * 512:(dh + 1) * 512],
                                                             start=False, stop=True))
                                P.mm(fns, k_actc + [k_w2, k_b2, k_const], [bk2[dh][1]])
                            post_norm_residual(pbufs, i, bk2, gpost, k_gpost)
                    P.barrier()
            if STAGE >= 6:
                ffn(1)
        if True:
            for i in range(NT):
                r = rows(i)
                P.dma(P.sp, y_o[i * 128:i * 128 + r, :], X[0:r, i, :], reads=[kX[i]])
        P.finish()
    return nc


_NC_CACHE = {}


def kernel(**inp):
    f = lambda a: np.ascontiguousarray(np.asarray(a, dtype=np.float32))
    I = {k: f(v) for k, v in inp.items()}
    w_in = I["w_in_ab"][0]
    q, k, v, u = w_in[:, 0:512], w_in[:, 512:640], w_in[:, 640:768], w_in[:, 768:1280]
    win = np.concatenate([q, k[:, 0:64], k[:, 0:64], k[:, 64:128], k[:, 64:128], k, v, u], axis=1)
    shared = dict(
        npm=I["norm_pre_mix"], npo=I["norm_post_mix"], nfp=I["norm_pre_ffn"], nfo=I["norm_post_ffn"],
        win=f(win), sinks=I["attn_sinks"], lre=I["s5_lambda_re"].reshape(2048), lim=I["s5_lambda_im"].reshape(2048),
        lst=I["s5_log_step"].reshape(32), bre=I["s5_b_re"].reshape(2048, 16), bim=I["s5_b_im"].reshape(2048, 16),
        cre=I["s5_c_re"].reshape(512, 64), cim=I["s5_c_im"].reshape(512, 64), dsk=I["s5_d"].reshape(512),
        wglu=I["w_glu"][0], wout=I["w_out_ab"][0], wpw1=I["w_pw1"][0], bpw1=I["b_pw1"].reshape(2048),
        wdw=I["w_dw"][0], bdw=I["b_dw"].reshape(1024), lng=I["conv_ln_g"].reshape(1024), lnb=I["conv_ln_b"].reshape(1024),
        wpw2=I["w_pw2"][0], bpw2=I["b_pw2"].reshape(1, 1024), wg=I["w_ffn_gate"], wu=I["w_ffn_up"], wd=I["w_ffn_down"])
    in_maps = []
    for b in range(8):
        sl = slice(16 * b, 16 * b + 16)
        m = dict(shared)
        m["xin"] = f(np.concatenate([I["x_prompt"][b], I["x_sample"][sl].reshape(64, 1024)], axis=0))
        m["ck"] = f(I["cache_k"][0, sl].reshape(16, 128, 128)); m["cv"] = f(I["cache_v"][0, sl].reshape(16, 128, 128))
        m["s5re0"] = f(I["state_s5_re"][0, sl].reshape(16, 2048)); m["s5im0"] = f(I["state_s5_im"][0, sl].reshape(16, 2048))
        m["sconv"] = f(I["state_conv"][0, sl])
        in_maps.append(m)
    if "nc" not in _NC_CACHE:
        _NC_CACHE["nc"] = build_nc()
    res = run_bass_kernel_spmd(_NC_CACHE["nc"], in_maps[:NCORES], core_ids=list(range(NCORES)))
    R = res.results
    cat = lambda key: np.stack([np.asarray(R[min(b, NCORES - 1)][key], dtype=np.float32) for b in range(8)])
    y = cat("y")
    y_prompt = y[:, :2048, :]
    y_sample = y[:, 2048:, :].reshape(128, 4, 1024)
    k_prompt = cat("kp").reshape(1, 8, 128, 2, 64); v_prompt = cat("vp").reshape(1, 8, 128, 2, 64)
    s5rp = cat("s5rp").reshape(1, 8, 32, 64); s5ip = cat("s5ip").reshape(1, 8, 32, 64)
    convp = cat("convp").reshape(1, 8, 30, 1024)
    k_sample = cat("ks").reshape(1, 128, 128, 2, 64); v_sample = cat("vs").reshape(1, 128, 128, 2, 64)
    s5rs = cat("s5rs").reshape(1, 128, 32, 64); s5is = cat("s5is").reshape(1, 128, 32, 64)
    convs = cat("convs").reshape(1, 128, 30, 1024)
    return (np.ascontiguousarray(y_prompt), np.ascontiguousarray(y_sample), k_prompt, v_prompt, s5rp, s5ip, convp,
            k_sample, v_sample, s5rs, s5is, convs)
```

```python
import numpy as np
from contextlib import ExitStack
import concourse.bass as bass
import concourse.mybir as mybir
from concourse.bass_utils import run_bass_kernel_spmd

AF = mybir.ActivationFunctionType
ALU = mybir.AluOpType
F32 = mybir.dt.float32
BF16 = mybir.dt.bfloat16
I32 = mybir.dt.int32

STAGE = 99
SUB = 99
NCORES = 8
SAME_ENG_SYNC = True
PI = float(np.pi)


class Tk:
    __slots__ = ("name", "w", "r", "x")

    def __init__(self, name="", x=False):
        self.name = name
        self.w = []
        self.r = {}
        self.x = x


class Eng:
    def __init__(self, name, obj, sem, unit=1):
        self.name = name
        self.obj = obj
        self.sem = sem
        self.unit = unit
        self.count = 0
        self.seen = {}


class Prog:
    NSLOT = 8

    def __init__(self, nc, stack):
        self.nc = nc
        mk = lambda n: stack.enter_context(nc.semaphore(n))
        self.pe = Eng("pe", nc.tensor, mk("s_pe"))
        self.act = Eng("act", nc.scalar, mk("s_act"))
        self.dve = Eng("dve", nc.vector, mk("s_dve"))
        self.pool = Eng("pool", nc.gpsimd, mk("s_pool"))
        self.sp = Eng("sp", nc.sync, None)
        self.compute = [self.pe, self.act, self.dve, self.pool]
        self.slots = {}
        self.slot_i = {}
        for q in (self.sp, self.pool):
            self.slots[q.name] = [Eng("d_%s%d" % (q.name, i), None, mk("s_d%s%d" % (q.name, i)), 16)
                                  for i in range(self.NSLOT)]
            self.slot_i[q.name] = 0
        self.n_inst = 0

    def _wait(self, eng, dep, cnt):
        if eng.seen.get(dep, 0) >= cnt:
            return
        eng.obj.wait_ge(dep.sem, cnt * dep.unit)
        eng.seen[dep] = cnt

    def _deps(self, eng, reads, writes, part=False):
        deps = {}

        def add(e, c):
            if deps.get(e, 0) < c:
                deps[e] = c
        reads, xr = [t for t in reads if not t.x], [t for t in reads if t.x]
        writes = list(writes) + xr
        for t in reads:
            for e, c in t.w:
                add(e, c)
        for t in writes:
            if not part:
                for e, c in t.w:
                    add(e, c)
            for e, c in t.r.items():
                add(e, c)
        for e, c in deps.items():
            if e is eng and (eng is self.pe or not SAME_ENG_SYNC):
                continue
            self._wait(eng, e, c)

    def _mark(self, eng, reads, writes, part=False):
        writes = list(writes) + [t for t in reads if t.x]
        for t in reads:
            if not t.x:
                t.r[eng] = eng.count
        for t in writes:
            if part:
                t.w = [(e, c) for e, c in t.w if e is not eng] + [(eng, eng.count)]
            else:
                t.w = [(eng, eng.count)]
                t.r = {}

    def op(self, eng, fn, reads=(), writes=()):
        self._deps(eng, reads, writes)
        ins = fn()
        eng.count += 1
        ins.then_inc(eng.sem, 1)
        self._mark(eng, reads, writes)
        self.n_inst += 1

    def mm(self, fns, reads=(), writes=()):
        eng = self.pe
        self._deps(eng, reads, writes)
        ins = None
        for fn in fns:
            ins = fn()
            self.n_inst += 1
        eng.count += 1
        ins.then_inc(eng.sem, 1)
        self._mark(eng, reads, writes)

    def dma(self, q, out, in_, reads=(), writes=(), part=False, **kw):
        self._deps(q, reads, writes, part)
        sl = self.slots[q.name]
        i = self.slot_i[q.name]
        self.slot_i[q.name] = (i + 1) % len(sl)
        s = sl[i]
        if s.count > 0:
            self._wait(q, s, s.count)
        ins = q.obj.dma_start(out=out, in_=in_, **kw)
        s.count += 1
        ins.then_inc(s.sem, 16)
        self._mark(s, reads, writes, part)
        self.n_inst += 1

    def barrier(self):
        allsl = [s for v in self.slots.values() for s in v if s.count > 0]
        for e in self.compute + [self.sp]:
            for o in self.compute:
                if o is not e and o.count > 0:
                    self._wait(e, o, o.count)
            for s in allsl:
                self._wait(e, s, s.count)

    def finish(self):
        allsl = [s for v in self.slots.values() for s in v if s.count > 0]
        for o in self.compute:
            if o.count > 0:
                self._wait(self.sp, o, o.count)
        for s in allsl:
            self._wait(self.sp, s, s.count)


NT = 17
NTOK = 2112
WIDE = [(0, 512), (512, 512), (1024, 512), (1536, 512), (2048, 64)]
GROUPS = [list(range(0, 6)), list(range(6, 12)), list(range(12, 17))]
EPS = 1e-6
LN_EPS = 1e-5


def rows(i):
    return 128 if i < 16 else 64


def tiles_of(c0, n):
    return [i for i in range(NT) if i * 128 >= c0 and i * 128 < c0 + n]


def build_nc():
    nc = bass.Bass("TRN2", target_bir_lowering=False)
    di = lambda n, shp: nc.dram_tensor(n, shp, F32, kind="ExternalInput").ap()
    do = lambda n, shp: nc.dram_tensor(n, shp, F32, kind="ExternalOutput").ap()
    xin = di("xin", [NTOK, 1024]); ck = di("ck", [16, 128, 128]); cv = di("cv", [16, 128, 128])
    s5re0 = di("s5re0", [16, 2048]); s5im0 = di("s5im0", [16, 2048]); sconv = di("sconv", [16, 30, 1024])
    npm = di("npm", [2, 1024]); npo = di("npo", [2, 1024]); nfp = di("nfp", [2, 1024]); nfo = di("nfo", [2, 1024])
    win = di("win", [1024, 1536]); sinks = di("sinks", [1, 8])
    lre_d = di("lre", [2048]); lim_d = di("lim", [2048]); lst_d = di("lst", [32])
    bre_d = di("bre", [2048, 16]); bim_d = di("bim", [2048, 16]); cre_d = di("cre", [512, 64]); cim_d = di("cim", [512, 64])
    dsk_d = di("dsk", [512]); wglu_d = di("wglu", [512, 512]); wout_d = di("wout", [1024, 1024])
    wpw1_d = di("wpw1", [1024, 2048]); bpw1_d = di("bpw1", [2048]); wdw_d = di("wdw", [31, 1024]); bdw_d = di("bdw", [1024])
    lng_d = di("lng", [1024]); lnb_d = di("lnb", [1024]); wpw2_d = di("wpw2", [1024, 1024]); bpw2_d = di("bpw2", [1, 1024])
    wg_d = di("wg", [2, 1024, 2816]); wu_d = di("wu", [2, 1024, 2816]); wd_d = di("wd", [2, 2816, 1024])
    y_o = do("y", [NTOK, 1024]); kp_o = do("kp", [128, 128]); vp_o = do("vp", [128, 128])
    s5rp_o = do("s5rp", [16, 128]); s5ip_o = do("s5ip", [16, 128]); convp_o = do("convp", [30, 1024])
    ks_o = do("ks", [16, 128, 128]); vs_o = do("vs", [16, 128, 128])
    s5rs_o = do("s5rs", [16, 2048]); s5is_o = do("s5is", [16, 2048]); convs_o = do("convs", [16, 30, 1024])

    with ExitStack() as st:
        P = Prog(nc, st)
        V = lambda fn, r=(), w=(): P.op(P.dve, fn, r, w)
        A = lambda fn, r=(), w=(): P.op(P.act, fn, r, w)
        G = lambda fn, r=(), w=(): P.op(P.pool, fn, r, w)
        vec, act, pe = nc.vector, nc.scalar, nc.tensor

        uid = [0]

        def sb(stk, n, shp, dt):
            uid[0] += 1
            return stk.enter_context(nc.sbuf_tensor("%s_%d" % (n, uid[0]), shp, dt))

        banks = [(st.enter_context(nc.psum_tensor("psA%d" % i, [128, 512], F32)), Tk("psA%d" % i, True)) for i in range(6)]
        tbanks = [(st.enter_context(nc.psum_tensor("psT%d" % i, [128, 1024], BF16)), Tk("psT%d" % i, True)) for i in range(2)]
        bi = [0, 0]

        def nb():
            bi[0] = (bi[0] + 1) % 6
            return banks[bi[0]]

        def ntb():
            bi[1] = (bi[1] + 1) % 2
            return tbanks[bi[1]]

        X = sb(st, "X", [128, NT, 1024], F32)
        kX = [Tk("X%d" % i) for i in range(NT)]
        io = sb(st, "io", [128, 128], I32); k_io = Tk()
        idb = sb(st, "idb", [128, 128], BF16); idf = sb(st, "idf", [128, 128], F32); k_id = Tk()
        onesb = sb(st, "onesb", [128, 128], BF16); onesln = sb(st, "onesln", [128, 128], BF16)
        maskP = sb(st, "maskP", [128, 128], BF16); maskC = sb(st, "maskC", [128, 128], BF16)
        maskN = sb(st, "maskN", [64, 64], BF16); tm4 = sb(st, "tm4", [64, 64], I32); mtmp = sb(st, "mtmp", [64, 64], BF16)
        k_const = Tk()
        small = sb(st, "small", [128, 64], F32)
        k_small = Tk()

        G(lambda: nc.gpsimd.iota(io[:], pattern=[[1, 128]], base=0, channel_multiplier=-1), w=[k_io])
        G(lambda: nc.gpsimd.iota(tm4[:], pattern=[[0, 16], [1, 4]], base=0, channel_multiplier=0), w=[k_io])
        V(lambda: vec.tensor_single_scalar(idb[:], io[:], 0, ALU.is_equal), [k_io], [k_id])
        V(lambda: vec.tensor_single_scalar(idf[:], io[:], 0, ALU.is_equal), [k_io], [k_id])
        V(lambda: vec.memset(onesb[:], 1.0), w=[k_const])
        V(lambda: vec.memset(onesln[:], 1.0 / 1024.0), w=[k_const])
        V(lambda: vec.tensor_single_scalar(maskP[:], io[:], 0, ALU.is_lt), [k_io], [k_const])
        V(lambda: vec.tensor_single_scalar(maskC[:], io[:], 0, ALU.is_ge), [k_io], [k_const])
        V(lambda: vec.tensor_single_scalar(maskN[:], io[0:64, 0:64], 0, ALU.is_ge), [k_io], [k_const])
        V(lambda: vec.tensor_tensor(mtmp[:], io[0:64, 0:64], tm4[:], ALU.is_le), [k_io], [k_const])
        V(lambda: vec.tensor_tensor(maskN[:], maskN[:], mtmp[:], ALU.mult), [k_const], [k_const])

        def load_X(tiles):
            for i in tiles:
                r = rows(i)
                P.dma(P.sp, X[0:r, i, :], xin[i * 128:i * 128 + r, :], writes=[kX[i]])
        load_X(range(2))

        def colvec(stk, name, dram_flat, ncol):
            t = sb(stk, name, [128, ncol], F32)
            k = Tk(name)
            with nc.allow_non_contiguous_dma(reason="small param vector"):
                P.dma(P.sp, t[:], dram_flat.rearrange("(c p) -> p c", p=128), writes=[k])
            return t, k

        def load_w(stk, name, dram2d, K, N, n0=0):
            kc = K // 128
            t = sb(stk, name, [128, kc, N], BF16)
            k = Tk(name)
            src = dram2d.rearrange("(c p) n -> p c n", p=128)
            c = 0
            while c < N:
                nbk = min(1024, N - c)
                P.dma(P.pool, t[:, :, c:c + nbk], src[:, :, n0 + c:n0 + c + nbk], writes=[k], part=(c > 0))
                c += nbk
            return t, k

        def rstd_from_ssq(ssq_ap, out_ap, n, scale, eps, k=None):
            k = k or k_small
            A(lambda: act.activation(out_ap, ssq_ap, AF.Sqrt, scale=scale, bias=eps_t[0:n, 0:1] if eps == EPS else lneps_t[0:n, 0:1]),
              [k, k_const], [k])
            V(lambda: vec.reciprocal(out_ap, out_ap), [k], [k])

        k_sm_n = [Tk() for _ in range(4)]
        k_sm_p = [Tk() for _ in range(4)]
        pn_cnt = [0]

        eps_t = sb(st, "eps_t", [128, 1], F32); lneps_t = sb(st, "lneps_t", [128, 1], F32)
        halfpi = sb(st, "halfpi", [128, 1], F32)
        V(lambda: vec.memset(eps_t[:], EPS), w=[k_const])
        V(lambda: vec.memset(lneps_t[:], LN_EPS), w=[k_const])
        V(lambda: vec.memset(halfpi[:], PI / 2), w=[k_const])

        def norm_bufs(stk):
            junk = sb(stk, "nt_junk", [128, 1024], BF16)
            hb = [sb(stk, "nt_hb%d" % j, [128, 1024], BF16) for j in range(2)]
            return (junk, Tk(), hb, [Tk(), Tk()])

        def norm_T(nbufs, tiles, gcol, k_g, hT, k_hT, col0):
            junk, k_junk, hb, k_hb = nbufs
            for n_, i in enumerate(tiles):
                r = rows(i)
                j = n_ % 2
                sl = n_ % 4
                ks = k_sm_n[sl]
                ssq = small[0:r, 16 + 2 * sl:17 + 2 * sl]; rs = small[0:r, 17 + 2 * sl:18 + 2 * sl]
                A(lambda: act.activation(junk[0:r, :], X[0:r, i, :], AF.Square, accum_out=ssq), [kX[i]], [ks])
                rstd_from_ssq(ssq, rs, r, 1.0 / 1024.0, EPS, ks)
                V(lambda: vec.tensor_scalar(hb[j][0:r, :], X[0:r, i, :], rs, None, ALU.mult), [kX[i], ks], [k_hb[j]])
                tb, k_tb = ntb()
                P.mm([lambda c=c: pe.transpose(tb[:, c * 128:c * 128 + r], hb[j][0:r, c * 128:(c + 1) * 128], idb[0:r, 0:r])
                      for c in range(8)], [k_hb[j], k_id], [k_tb])
                c0 = i * 128 - col0
                V(lambda: vec.tensor_tensor(hT[:, :, c0:c0 + r],
                                            tb[:].rearrange("p (c t) -> p c t", c=8)[:, :, 0:r],
                                            gcol[:, :].unsqueeze(2).to_broadcast([128, 8, r]), ALU.mult),
                  [k_tb, k_g], [k_hT[i]])

        def post_norm_residual(stk_tmp_bufs, i, bk, gpost, k_gpost):
            r = rows(i)
            junk, k_junk, tmp, k_tmp = stk_tmp_bufs[:4]
            tmps = [(tmp, k_tmp), (stk_tmp_bufs[4], stk_tmp_bufs[5]) if len(stk_tmp_bufs) > 4 else (tmp, k_tmp)]
            sl = pn_cnt[0] % 4
            pn_cnt[0] += 1
            ks = k_sm_p[sl]
            b0 = 32 + 4 * sl
            for dh in range(2):
                A(lambda dh=dh: act.activation(junk[0:r, :], bk[dh][0][0:r, :], AF.Square, accum_out=small[0:r, b0 + dh:b0 + dh + 1]),
                  [bk[dh][1]], [ks])
            V(lambda: vec.tensor_tensor(small[0:r, b0 + 2:b0 + 3], small[0:r, b0:b0 + 1], small[0:r, b0 + 1:b0 + 2], ALU.add), [ks], [ks])
            rstd_from_ssq(small[0:r, b0 + 2:b0 + 3], small[0:r, b0 + 3:b0 + 4], r, 1.0 / 1024.0, EPS, ks)
            for dh in range(2):
                V(lambda dh=dh: vec.scalar_tensor_tensor(tmps[dh][0][0:r, :], bk[dh][0][0:r, :], small[0:r, b0 + 3:b0 + 4],
                                                         gpost[0:r, dh * 512:(dh + 1) * 512], ALU.mult, ALU.mult),
                  [bk[dh][1], ks, k_gpost], [tmps[dh][1]])
                if len(stk_tmp_bufs) <= 4:
                    V(lambda dh=dh: vec.tensor_tensor(X[0:r, i, dh * 512:(dh + 1) * 512], X[0:r, i, dh * 512:(dh + 1) * 512],
                                                      tmp[0:r, :], ALU.add), [k_tmp, kX[i]], [kX[i]])
            if len(stk_tmp_bufs) > 4:
                for dh in range(2):
                    V(lambda dh=dh: vec.tensor_tensor(X[0:r, i, dh * 512:(dh + 1) * 512], X[0:r, i, dh * 512:(dh + 1) * 512],
                                                      tmps[dh][0][0:r, :], ALU.add), [tmps[dh][1], kX[i]], [kX[i]])

        def load_gpost(stk, name, dram_row):
            t = sb(stk, name, [128, 1024], F32); k = Tk(name)
            P.dma(P.sp, t[:], dram_row.partition_broadcast(128), writes=[k])
            return t, k

        def ffn(layer):
            with ExitStack() as ph:
                gcol, k_g = colvec(ph, "ffn_g", nfp[layer], 8)
                gpost, k_gpost = load_gpost(ph, "ffn_gpost", nfo[layer])
                wd_sb = sb(ph, "wd_sb", [128, 22, 1024], BF16)
                k_wd = Tk()
                wgu = [(sb(ph, "wg%d" % j, [128, 8, 256], BF16), sb(ph, "wu%d" % j, [128, 8, 256], BF16), Tk()) for j in range(3)]
                hT = sb(ph, "ffn_hT", [128, 8, 768], BF16); k_hT = [Tk() for _ in range(NT)]
                hid = sb(ph, "ffn_hid", [128, 22, 768], BF16); k_hidf = [Tk() for _ in range(22)]
                sg = [sb(ph, "ffn_sg%d" % j, [128, 512], F32) for j in range(2)]; k_sg = [Tk(), Tk()]
                junk = sb(ph, "ffn_junk", [128, 512], BF16); tmp = sb(ph, "ffn_tmp", [128, 512], F32)
                pbufs = (junk, Tk(), tmp, Tk(), sb(ph, "ffn_tmp2", [128, 512], F32), Tk())
                nbufs = norm_bufs(ph)
                wdsrc = wd_d[layer].rearrange("(c p) n -> p c n", p=128)
                wgsrc = wg_d[layer].rearrange("(c p) n -> p c n", p=128)
                wusrc = wu_d[layer].rearrange("(c p) n -> p c n", p=128)
                first = True
                cnt = 0
                norm_T(nbufs, GROUPS[0], gcol, k_g, hT, k_hT, GROUPS[0][0] * 128)
                for gi, grp in enumerate(GROUPS):
                    col0 = grp[0] * 128
                    ncols = sum(rows(i) for i in grp)
                    pieces = []
                    c = 0
                    while c < ncols:
                        n = min(512, ncols - c)
                        pieces.append((c, n))
                        c += n
                    for fg in range(11):
                        wgt, wut, k_w = wgu[cnt % 3]
                        cnt += 1
                        P.dma(P.pool, wgt[:, :, :], wgsrc[:, :, fg * 256:(fg + 1) * 256], writes=[k_w])
                        P.dma(P.pool, wut[:, :, :], wusrc[:, :, fg * 256:(fg + 1) * 256], writes=[k_w], part=True)
                        if first and fg == 2:
                            P.dma(P.pool, wd_sb[:, 0:11, :], wdsrc[:, 0:11, :], writes=[k_wd])
                            P.dma(P.pool, wd_sb[:, 11:22, :], wdsrc[:, 11:22, :], writes=[k_wd], part=True)
                            first = False
                        for (pc, pn) in pieces:
                            kr = [k_hT[i] for i in tiles_of(col0 + pc, pn)]
                            for fc in range(2):
                                f = fg * 2 + fc
                                bg, k_bg = nb()
                                P.mm([lambda k=k: pe.matmul(bg[:, 0:pn], lhsT=wgt[:, k, fc * 128:(fc + 1) * 128], rhs=hT[:, k, pc:pc + pn],
                                                            start=(k == 0), stop=(k == 7)) for k in range(8)], kr + [k_w], [k_bg])
                                bu, k_bu = nb()
                                P.mm([lambda k=k: pe.matmul(bu[:, 0:pn], lhsT=wut[:, k, fc * 128:(fc + 1) * 128], rhs=hT[:, k, pc:pc + pn],
                                                            start=(k == 0), stop=(k == 7)) for k in range(8)], kr + [k_w], [k_bu])
                                j = f % 2
                                A(lambda: act.activation(sg[j][:, 0:pn], bg[:, 0:pn], AF.Silu), [k_bg], [k_sg[j]])
                                V(lambda: vec.tensor_tensor(hid[:, f, pc:pc + pn], sg[j][:, 0:pn], bu[:, 0:pn], ALU.mult),
                                  [k_sg[j], k_bu], [k_hidf[f]])
                    if gi + 1 < len(GROUPS):
                        norm_T(nbufs, GROUPS[gi + 1], gcol, k_g, hT, k_hT, GROUPS[gi + 1][0] * 128)
                    for i in grp:
                        r = rows(i)
                        c0 = i * 128 - col0
                        bk = [nb(), nb()]
                        for dh in range(2):
                            P.mm([lambda f=f: pe.matmul(bk[dh][0][0:r, :], lhsT=hid[:, f, c0:c0 + r], rhs=wd_sb[:, f, dh * 512:(dh + 1) * 512],
                                                        start=(f == 0), stop=(f == 21)) for f in range(22)], k_hidf + [k_wd], [bk[dh][1]])
                        post_norm_residual(pbufs, i, bk, gpost, k_gpost)
                P.barrier()

        with ExitStack() as L0:
            aT = sb(L0, "aT", [128, 4, NTOK], BF16); k_aT = Tk()
            uT = sb(L0, "uT", [128, 4, NTOK], BF16); k_uTm = [[Tk() for _ in range(4)] for _ in range(5)]
            k_p = Tk()
            lre = sb(L0, "s5_lre", [128, 16], F32); lim = sb(L0, "s5_lim", [128, 16], F32); dtt = sb(L0, "s5_dtt", [128, 16], F32)
            with ExitStack() as Lq:
                qT = sb(Lq, "qT", [128, 4, NTOK], BF16); k_qTm = [[Tk() for _ in range(4)] for _ in range(5)]
                kT2 = sb(Lq, "kT2", [128, 2, NTOK], BF16); k_kTm = [[Tk() for _ in range(2)] for _ in range(5)]
                vtok = sb(Lq, "vtok", [128, NT, 128], BF16); vpad = sb(Lq, "vpad", [128, NT, 2, 128], BF16)
                k_v = [Tk() for _ in range(NT)]
                kvf = sb(Lq, "kvf", [128, 2, 256], F32); k_kvf = Tk()
                V(lambda: vec.memset(vpad[:], 0.0), w=k_v)
                with ExitStack() as ph:
                    gcol, k_g = colvec(ph, "g_pm0", npm[0], 8)
                    load_X(range(2, NT))
                    win_sb, k_win = load_w(ph, "win_sb", win, 1024, 1536)
                    hT = sb(ph, "hT", [128, 8, NTOK], BF16); k_hT = [Tk() for _ in range(NT)]
                    with nc.allow_non_contiguous_dma(reason="s5 params"):
                        P.dma(P.sp, lre[:], lre_d.rearrange("(j q) -> q j", q=128), writes=[k_p])
                        P.dma(P.sp, lim[:], lim_d.rearrange("(j q) -> q j", q=128), writes=[k_p])
                        lst2 = lst_d.rearrange("(j t) -> t j", t=2)
                        for g2 in range(2):
                            P.dma(P.sp, dtt[g2 * 64:(g2 + 1) * 64, :], lst2[g2:g2 + 1, :].to_broadcast([64, 16]), writes=[k_p])
                    norm_T(norm_bufs(ph), range(NT), gcol, k_g, hT, k_hT, 0)
                    for w, (c0, n) in enumerate(WIDE if SUB >= 2 else []):
                        kr = [k_hT[i] for i in tiles_of(c0, n)] + [k_win]
                        for m in range(10):
                            woff = m * 128 if m < 4 else (512 + (m - 4) * 128 if m < 6 else 1024 + (m - 6) * 128)
                            b, k_b = nb()
                            P.mm([lambda k=k: pe.matmul(b[:, 0:n], lhsT=win_sb[:, k, woff:woff + 128], rhs=hT[:, k, c0:c0 + n],
                                                        start=(k == 0), stop=(k == 7)) for k in range(8)], kr, [k_b])
                            if m < 4:
                                A(lambda: act.activation(qT[:, m, c0:c0 + n], b[:, 0:n], AF.Copy, scale=0.125), [k_b], [k_qTm[w][m]])
                            elif m < 6:
                                V(lambda: vec.tensor_copy(kT2[:, m - 4, c0:c0 + n], b[:, 0:n]), [k_b], [k_kTm[w][m - 4]])
                            else:
                                A(lambda: act.copy(uT[:, m - 6, c0:c0 + n], b[:, 0:n]), [k_b], [k_uTm[w][m - 6]])
                    for i in (range(NT) if SUB >= 3 else []):
                        r = rows(i)
                        b, k_b = nb()
                        P.mm([lambda k=k: pe.matmul(b[0:r, 0:256], lhsT=hT[:, k, i * 128:i * 128 + r], rhs=win_sb[:, k, 768:1024],
                                                    start=(k == 0), stop=(k == 7)) for k in range(8)], [k_hT[i], k_win], [k_b])
                        V(lambda: vec.tensor_copy(vtok[0:r, i, :], b[0:r, 128:256]), [k_b], [k_v[i]])
                        for g_ in (range(2) if SUB >= 5 else []):
                            A(lambda g_=g_: act.copy(vpad[0:r, i, g_, 64:128], b[0:r, 128 + g_ * 64:192 + g_ * 64]), [k_b], [k_v[i]])
                        if i >= 15 and SUB >= 6:
                            V(lambda: vec.tensor_copy(kvf[0:r, i - 15, :], b[0:r, 0:256]), [k_b], [k_kvf])
                    if SUB >= 8:
                      P.dma(P.sp, kp_o[:, :], kvf[:, 0, 0:128], reads=[k_kvf])
                      P.dma(P.sp, vp_o[:, :], kvf[:, 0, 128:256], reads=[k_kvf])
                    for m in (range(16) if SUB >= 7 else []):
                        P.dma(P.sp, ks_o[m, 124:128, :], kvf[4 * m:4 * m + 4, 1, 0:128], reads=[k_kvf])
                        P.dma(P.sp, vs_o[m, 124:128, :], kvf[4 * m:4 * m + 4, 1, 128:256], reads=[k_kvf])
                    P.dma(P.sp, ks_o[:, 0:124, :], ck[:, 4:128, :])
                    P.dma(P.sp, vs_o[:, 0:124, :], cv[:, 4:128, :])
                    P.barrier()
                if STAGE >= 2:
                    with ExitStack() as ph:
                        sk = sb(ph, "sk", [128, 8], F32); k_sk = Tk()
                        P.dma(P.sp, sk[:], sinks[0:1, :].to_broadcast([128, 8]), writes=[k_sk])
                        A(lambda: act.activation(sk[:], sk[:], AF.Exp), [k_sk], [k_sk])
                        PT = [sb(ph, "PT%d" % j, [128, 2, 128], BF16) for j in range(8)]; k_PT = [Tk() for _ in range(8)]
                        dn2 = [sb(ph, "dn%d" % i_, [128, 2, 128], F32) for i_ in range(2)]; k_dn2 = [Tk(), Tk()]; ndn = [0]

                        def normalize(R, g, par, O, k_O, Dn, k_Dn, c0, n):
                            h0 = 4 * g + par
                            dnb, k_dnb = dn2[ndn[0] % 2], k_dn2[ndn[0] % 2]
                            ndn[0] += 1
                            for ci in range(2):
                                A(lambda ci=ci: act.activation(dnb[R, ci, 0:n], Dn[:, ci, :], AF.Ln, bias=sk[R, h0 + 2 * ci:h0 + 2 * ci + 1]),
                                  [k_Dn, k_sk], [k_dnb])
                            A(lambda: act.activation(dnb[R, :, 0:n], dnb[R, :, 0:n], AF.Exp, scale=-1.0), [k_dnb], [k_dnb])
                            V(lambda: vec.tensor_tensor(aT[R, 2 * g:2 * g + 2, c0:c0 + n], O, dnb[R, :, 0:n], ALU.mult), [k_O, k_dnb], [k_aT])

                        units = [(blk, g, par) for blk in range(16) for g in range(2) for par in range(2)]

                        def stage_a(u):
                            blk, g, par = u
                            w, c0 = blk // 4, blk * 128
                            R = slice(par * 64, par * 64 + 64)
                            pts = []
                            for kb in ([blk - 1] if blk > 0 else []) + [blk]:
                                S, k_S = nb()
                                P.mm([lambda: pe.matmul(S[:, 0:256], lhsT=kT2[R, g, kb * 128:(kb + 1) * 128],
                                                        rhs=qT[R, 2 * g:2 * g + 2, c0:c0 + 128], start=True, stop=True)],
                                     [k_kTm[kb // 4][g], k_qTm[w][2 * g], k_qTm[w][2 * g + 1]], [k_S])
                                j = pti[0] % 8
                                pti[0] += 1
                                A(lambda: act.activation(PT[j][:].rearrange("p a t -> p (a t)"), S[:, 0:256], AF.Exp), [k_S], [k_PT[j]])
                                msk = maskC if kb == blk else maskP
                                V(lambda: vec.tensor_tensor(PT[j][:], PT[j][:], msk[:].unsqueeze(1).to_broadcast([128, 2, 128]), ALU.mult),
                                  [k_PT[j], k_const], [k_PT[j]])
                                pts.append((j, kb))
                            return pts

                        def stage_b(u, pts):
                            blk, g, par = u
                            c0 = blk * 128
                            R = slice(par * 64, par * 64 + 64)
                            O, k_O = nb()
                            fns = []
                            for ci in range(2):
                                for n_, (j, kb) in enumerate(pts):
                                    if par == 0:
                                        fns.append(lambda ci=ci, j=j, kb=kb, n_=n_: pe.matmul(
                                            O[0:64, ci * 128:(ci + 1) * 128], lhsT=vtok[:, kb, g * 64:(g + 1) * 64], rhs=PT[j][:, ci, :],
                                            start=(n_ == 0), stop=(n_ == len(pts) - 1)))
                                    else:
                                        fns.append(lambda ci=ci, j=j, kb=kb, n_=n_: pe.matmul(
                                            O[:, ci * 128:(ci + 1) * 128], lhsT=vpad[:, kb, g, :], rhs=PT[j][:, ci, :],
                                            start=(n_ == 0), stop=(n_ == len(pts) - 1)))
                            P.mm(fns, [k_PT[j] for j, _ in pts] + [k_v[kb] for _, kb in pts], [k_O])
                            Dn, k_Dn = nb()
                            P.mm([lambda j=j, n_=n_: pe.matmul(Dn[:, 0:256], lhsT=onesb[:], rhs=PT[j][:].rearrange("p a t -> p (a t)"),
                                                                start=(n_ == 0), stop=(n_ == len(pts) - 1)) for n_, (j, kb) in enumerate(pts)],
                                 [k_PT[j] for j, _ in pts] + [k_const], [k_Dn])
                            normalize(R, g, par, O[R, 0:256].rearrange("p (a t) -> p a t", a=2), k_O,
                                      Dn[R, 0:256].rearrange("p (a t) -> p a t", a=2), k_Dn, c0, 128)

                        pti = [0]
                        prev = None
                        for u in units:
                            pts_u = stage_a(u)
                            if prev is not None:
                                stage_b(*prev)
                            prev = (u, pts_u)
                        stage_b(*prev)
                        kcd = sb(ph, "kcd", [128, 16, 2, 2, 64], BF16); k_kcd = Tk()
                        vc = sb(ph, "vc", [128, 16, 128], BF16); vcp = sb(ph, "vcp", [128, 16, 2, 128], BF16); k_vc = Tk()
                        kcT2 = sb(ph, "kcT2", [128, 2, 16, 128], BF16); k_kcT = Tk()
                        V(lambda: vec.memset(vcp[:], 0.0), w=[k_vc])
                        cksrc = ck.rearrange("m s (g d) -> s m g d", g=2)
                        for g in range(2):
                            for dup in range(2):
                                P.dma(P.pool, kcd[:, :, g, dup, :], cksrc[:, :, g, :], writes=[k_kcd])
                        P.dma(P.pool, vc[:], cv.rearrange("m s c -> s m c"), writes=[k_vc])
                        for g in range(2):
                            P.dma(P.pool, vcp[:, :, g, 64:128], cv.rearrange("m s (g d) -> s m g d", g=2)[:, :, g, :], writes=[k_vc])
                        for g in range(2):
                            for half in range(2):
                                tb, k_tb = ntb()
                                P.mm([lambda m=m: pe.transpose(tb[:, (m % 8) * 128:(m % 8 + 1) * 128],
                                                               kcd[:, m, g, :, :].rearrange("p a d -> p (a d)"), idb[:])
                                      for m in range(half * 8, half * 8 + 8)], [k_kcd, k_id], [k_tb])
                                V(lambda: vec.tensor_copy(kcT2[:, g, half * 8:half * 8 + 8, :], tb[:].rearrange("p (m s) -> p m s", m=8)),
                                  [k_tb], [k_kcT])
                        PTc = sb(ph, "PTc", [128, 16, 2, 4], BF16); k_PTc = Tk()
                        PTn = sb(ph, "PTn", [64, 2, 64], BF16); k_PTn = Tk()
                        sc0 = 2048
                        for g in range(2):
                            for par in range(2):
                                R = slice(par * 64, par * 64 + 64)
                                S, k_S = nb()
                                P.mm([lambda m=m: pe.matmul(S[:, m * 8:(m + 1) * 8], lhsT=kcT2[R, g, m, :],
                                                            rhs=qT[R, 2 * g:2 * g + 2, sc0 + 4 * m:sc0 + 4 * m + 4], start=True, stop=True)
                                      for m in range(16)], [k_kcT, k_qTm[4][2 * g], k_qTm[4][2 * g + 1]], [k_S])
                                A(lambda: act.activation(PTc[:].rearrange("p m a t -> p (m a t)"), S[:, 0:128], AF.Exp), [k_S], [k_PTc])
                                V(lambda: vec.tensor_tensor(PTc[:].rearrange("p m a t -> p (m a) t"), PTc[:].rearrange("p m a t -> p (m a) t"),
                                                            maskP[:, 0:4].unsqueeze(1).to_broadcast([128, 32, 4]), ALU.mult),
                                  [k_PTc, k_const], [k_PTc])
                                S2, k_S2 = nb()
                                P.mm([lambda: pe.matmul(S2[0:64, 0:128], lhsT=kT2[R, g, sc0:sc0 + 64], rhs=qT[R, 2 * g:2 * g + 2, sc0:sc0 + 64],
                                                        start=True, stop=True)], [k_kTm[4][g], k_qTm[4][2 * g], k_qTm[4][2 * g + 1]], [k_S2])
                                A(lambda: act.activation(PTn[:].rearrange("p a t -> p (a t)"), S2[0:64, 0:128], AF.Exp), [k_S2], [k_PTn])
                                V(lambda: vec.tensor_tensor(PTn[:], PTn[:], maskN[:].unsqueeze(1).to_broadcast([64, 2, 64]), ALU.mult),
                                  [k_PTn, k_const], [k_PTn])
                                O, k_O = nb()
                                Ov = O[:, 0:128].rearrange("p (a t) -> p a t", a=2)
                                fns = []
                                if par == 0:
                                    fns.append(lambda: pe.matmul(O[0:64, 0:128], lhsT=vtok[0:64, 16, g * 64:(g + 1) * 64],
                                                                 rhs=PTn[:].rearrange("p a t -> p (a t)"), start=True, stop=False))
                                    for m in range(16):
                                        fns.append(lambda m=m: pe.matmul(Ov[0:64, :, 4 * m:4 * m + 4], lhsT=vc[:, m, g * 64:(g + 1) * 64],
                                                                         rhs=PTc[:, m, :, :], start=False, stop=(m == 15), skip_group_check=True))
                                else:
                                    fns.append(lambda: pe.matmul(O[:, 0:128], lhsT=vpad[0:64, 16, g, :],
                                                                 rhs=PTn[:].rearrange("p a t -> p (a t)"), start=True, stop=False))
                                    for m in range(16):
                                        fns.append(lambda m=m: pe.matmul(Ov[:, :, 4 * m:4 * m + 4], lhsT=vcp[:, m, g, :],
                                                                         rhs=PTc[:, m, :, :], start=False, stop=(m == 15), skip_group_check=True))
                                P.mm(fns, [k_PTn, k_PTc, k_vc, k_v[16]], [k_O])
                                Dn, k_Dn = nb()
                                Dv = Dn[:, 0:128].rearrange("p (a t) -> p a t", a=2)
                                fns = [lambda: pe.matmul(Dn[:, 0:128], lhsT=onesb[0:64, :], rhs=PTn[:].rearrange("p a t -> p (a t)"),
                                                         start=True, stop=False)]
                                for m in range(16):
                                    fns.append(lambda m=m: pe.matmul(Dv[:, :, 4 * m:4 * m + 4], lhsT=onesb[:], rhs=PTc[:, m, :, :],
                                                                     start=False, stop=(m == 15), skip_group_check=True))
                                P.mm(fns, [k_PTn, k_PTc, k_const], [k_Dn])
                                normalize(R, g, par, Ov[R], k_O, Dv[R], k_Dn, sc0, 64)
                        P.barrier()
            if STAGE >= 3:
                gT = sb(L0, "gT", [128, 4, NTOK], BF16); k_gT = Tk()
                with ExitStack() as ph:
                    pt = lambda n: sb(ph, "s5_" + n, [128, 16], F32)
                    mag = pt("mag"); th = pt("th"); kf = pt("kf")
                    sn = pt("sn"); cs = pt("cs"); ab = pt("ab"); lbr = pt("lbr"); lbi = pt("lbi"); den = pt("den")
                    a1 = pt("a1"); cfr = pt("cfr"); cfi = pt("cfi"); t0 = pt("t0"); t1s = pt("t1s")
                    C256 = pt("C256"); S256 = pt("S256"); Enc = pt("Enc"); Ens = pt("Ens")
                    ki = sb(ph, "ki", [128, 16], I32)
                    TT = lambda o, a, b, op: V(lambda: vec.tensor_tensor(o, a, b, op), [k_p], [k_p])
                    A(lambda: act.activation(dtt[:], dtt[:], AF.Exp), [k_p], [k_p])
                    TT(t0[:], lre[:], dtt[:], ALU.mult)
                    A(lambda: act.activation(mag[:], t0[:], AF.Exp), [k_p], [k_p])
                    TT(th[:], lim[:], dtt[:], ALU.mult)
                    A(lambda: act.activation(ki[:], th[:], AF.Copy, scale=1.0 / (2 * PI)), [k_p], [k_p])
                    V(lambda: vec.tensor_copy(kf[:], ki[:]), [k_p], [k_p])
                    V(lambda: vec.scalar_tensor_tensor(th[:], kf[:], -2 * PI, th[:], ALU.mult, ALU.add), [k_p], [k_p])
                    V(lambda: vec.tensor_scalar(th[:], th[:], PI, -PI, ALU.min, ALU.max), [k_p], [k_p])
                    A(lambda: act.activation(sn[:], th[:], AF.Sin), [k_p], [k_p])
                    A(lambda: act.activation(ab[:], th[:], AF.Abs), [k_p], [k_p])
                    A(lambda: act.activation(cs[:], ab[:], AF.Sin, scale=-1.0, bias=halfpi[:, 0:1]), [k_p, k_const], [k_p])
                    TT(lbr[:], mag[:], cs[:], ALU.mult); TT(lbi[:], mag[:], sn[:], ALU.mult)
                    TT(den[:], lre[:], lre[:], ALU.mult); TT(t0[:], lim[:], lim[:], ALU.mult); TT(den[:], den[:], t0[:], ALU.add)
                    V(lambda: vec.reciprocal(den[:], den[:]), [k_p], [k_p])
                    V(lambda: vec.tensor_scalar(a1[:], lbr[:], -1.0, None, ALU.add), [k_p], [k_p])
                    TT(t0[:], a1[:], lre[:], ALU.mult); TT(t1s[:], lbi[:], lim[:], ALU.mult); TT(t0[:], t0[:], t1s[:], ALU.add)
                    TT(cfr[:], t0[:], den[:], ALU.mult)
                    TT(t0[:], lbi[:], lre[:], ALU.mult); TT(t1s[:], a1[:], lim[:], ALU.mult); TT(t0[:], t0[:], t1s[:], ALU.subtract)
                    TT(cfi[:], t0[:], den[:], ALU.mult)
                    Bpr = sb(ph, "Bpr", [128, 16, 128], BF16); Bpi = sb(ph, "Bpi", [128, 16, 128], BF16); k_Bp = Tk()
                    Cpr = sb(ph, "Cpr", [128, 16, 128], BF16); Cpi = sb(ph, "Cpi", [128, 16, 128], BF16); k_Cp = Tk()
                    Cnr = sb(ph, "Cnr", [128, 16, 128], BF16)
                    Dd = sb(ph, "Dd", [128, 4, 128], BF16); k_Dd = Tk()
                    ysb = sb(ph, "ysb", [128, 512], F32); y2 = sb(ph, "y2", [128, 512], F32); k_ge = Tk(); k_ysb = Tk()
                    k_gTq = [k_gT, Tk(), Tk(), Tk()]
                    pp = ExitStack()
                    cosT = sb(pp, "cosT", [128, 16, 256], F32); sinT = sb(pp, "sinT", [128, 16, 256], F32); k_E = Tk()
                    with ExitStack() as ph2:
                        bre_t = sb(ph2, "bre_t", [128, 16, 16], F32); bim_t = sb(ph2, "bim_t", [128, 16, 16], F32); k_bb = Tk()
                        P.dma(P.sp, bre_t[:], bre_d.rearrange("(j q) c -> q j c", q=128), writes=[k_bb])
                        P.dma(P.sp, bim_t[:], bim_d.rearrange("(j q) c -> q j c", q=128), writes=[k_bb], part=True)
                        bbr = sb(ph2, "bbr", [128, 16, 16], F32); bbi = sb(ph2, "bbi", [128, 16, 16], F32); tb1 = sb(ph2, "tb1", [128, 16, 16], F32)
                        BTr = sb(ph2, "BTr", [128, 16, 128], F32); BTi = sb(ph2, "BTi", [128, 16, 128], F32)
                        CN = sb(ph2, "CN", [128, 1, 4, 128], F32); CT = sb(ph2, "CT", [128, 2, 4, 128], F32)
                        crb = cfr[:, :].unsqueeze(2).to_broadcast([128, 16, 16]); cib = cfi[:, :].unsqueeze(2).to_broadcast([128, 16, 16])
                        V(lambda: vec.tensor_tensor(bbr[:], bre_t[:], crb, ALU.mult), [k_p, k_bb], [k_p])
                        TT(tb1[:], bim_t[:], cib, ALU.mult); TT(bbr[:], bbr[:], tb1[:], ALU.subtract)
                        TT(bbi[:], bim_t[:], crb, ALU.mult); TT(tb1[:], bre_t[:], cib, ALU.mult); TT(bbi[:], bbi[:], tb1[:], ALU.add)
                        V(lambda: vec.memset(BTr[:], 0.0), [k_p], [k_p]); V(lambda: vec.memset(BTi[:], 0.0), [k_p], [k_p])
                        for g2 in range(2):
                            H = slice(g2 * 64, g2 * 64 + 64)
                            for b_ in range(4):
                                cs_ = slice((2 * b_ + g2) * 16, (2 * b_ + g2) * 16 + 16)
                                V(lambda: vec.tensor_copy(BTr[H, b_::4, cs_], bbr[H, b_::4, :]), [k_p], [k_p])
                                V(lambda: vec.tensor_copy(BTi[H, b_::4, cs_], bbi[H, b_::4, :]), [k_p], [k_p])
                        for (BT, Bp) in ((BTr, Bpr), (BTi, Bpi)):
                            for jj in range(4):
                                bk, k_bk = nb()
                                P.mm([lambda j=j: pe.transpose(bk[:, (j % 4) * 128:(j % 4 + 1) * 128], BT[:, j, :], idf[:])
                                      for j in range(4 * jj, 4 * jj + 4)], [k_p, k_id], [k_bk])
                                A(lambda: act.copy(Bp[:, 4 * jj:4 * jj + 4, :].rearrange("p a b -> p (a b)"), bk[:, :]), [k_bk], [k_Bp])
                        for ri, cd in enumerate((cre_d, cim_d)):
                            src = cd.rearrange("(i q) p -> q i p", q=128)
                            P.dma(P.sp, CN[:, 0, :, 0:64], src, writes=[k_p])
                            P.dma(P.sp, CN[:, 0, :, 64:128], src, writes=[k_p])
                            bk, k_bk = nb()
                            P.mm([lambda i=i: pe.transpose(bk[:, i * 128:(i + 1) * 128], CN[:, 0, i, :], idf[:]) for i in range(4)],
                                 [k_p, k_id], [k_bk])
                            A(lambda: act.copy(CT[:, ri, :, :].rearrange("p a b -> p (a b)"), bk[:, :]), [k_bk], [k_p])
                        V(lambda: vec.memset(Cpr[:], 0.0), w=[k_Cp]); V(lambda: vec.memset(Cpi[:], 0.0), w=[k_Cp])
                        for g2 in range(2):
                            H = slice(g2 * 64, g2 * 64 + 64)
                            for b_ in range(4):
                                cs_ = slice((2 * b_ + g2) * 16, (2 * b_ + g2) * 16 + 16)
                                V(lambda: vec.tensor_copy(Cpr[H, b_::4, cs_], CT[H, 0, :, cs_]), [k_p], [k_Cp])
                                V(lambda: vec.tensor_scalar(Cpi[H, b_::4, cs_], CT[H, 1, :, cs_], -1.0, None, ALU.mult), [k_p], [k_Cp])
                        V(lambda: vec.tensor_scalar(Cnr[:], Cpr[:], -1.0, None, ALU.mult), [k_Cp], [k_Cp])
                        dT, k_dT = colvec(ph2, "dT", dsk_d, 4)
                        for i in range(4):
                            V(lambda: vec.tensor_scalar(Dd[:, i, :], idf[:], dT[:, i:i + 1], None, ALU.mult), [k_dT, k_id], [k_Dd])
                        P.barrier()
                        T1, T2 = BTr, BTi
                        kc, ks_, kT1, kT2, kEc, kEs, kt0, kt1 = [Tk() for _ in range(8)]
                        V(lambda: vec.memset(cosT[:, :, 0:1], 1.0), w=[kc]); V(lambda: vec.memset(sinT[:, :, 0:1], 0.0), w=[ks_])
                        VT = lambda o, a, b, op, r, w: V(lambda: vec.tensor_tensor(o, a, b, op), r, w)
                        n = 1
                        while n <= 256:
                            if n == 1:
                                V(lambda: vec.tensor_copy(Enc[:], cs[:]), [k_p], [kEc]); V(lambda: vec.tensor_copy(Ens[:], sn[:]), [k_p], [kEs])
                            else:
                                VT(t0[:], cosT[:, :, n - 1], cs[:], ALU.mult, [kc, k_p], [kt0]); VT(t1s[:], sinT[:, :, n - 1], sn[:], ALU.mult, [ks_, k_p], [kt1])
                                VT(Enc[:], t0[:], t1s[:], ALU.subtract, [kt0, kt1], [kEc])
                                VT(t0[:], sinT[:, :, n - 1], cs[:], ALU.mult, [ks_, k_p], [kt0]); VT(t1s[:], cosT[:, :, n - 1], sn[:], ALU.mult, [kc, k_p], [kt1])
                                VT(Ens[:], t0[:], t1s[:], ALU.add, [kt0, kt1], [kEs])
                            if n == 256:
                                V(lambda: vec.tensor_copy(C256[:], Enc[:]), [kEc], [k_p]); V(lambda: vec.tensor_copy(S256[:], Ens[:]), [kEs], [k_p])
                                break
                            cb_ = Enc[:, :].unsqueeze(2).to_broadcast([128, 16, n]); sb_ = Ens[:, :].unsqueeze(2).to_broadcast([128, 16, n])
                            VT(T1[:, :, 0:n], cosT[:, :, 0:n], cb_, ALU.mult, [kc, kEc], [kT1]); VT(T2[:, :, 0:n], sinT[:, :, 0:n], sb_, ALU.mult, [ks_, kEs], [kT2])
                            VT(cosT[:, :, n:2 * n], T1[:, :, 0:n], T2[:, :, 0:n], ALU.subtract, [kT1, kT2], [kc])
                            VT(T1[:, :, 0:n], sinT[:, :, 0:n], cb_, ALU.mult, [ks_, kEc], [kT1]); VT(T2[:, :, 0:n], cosT[:, :, 0:n], sb_, ALU.mult, [kc, kEs], [kT2])
                            VT(sinT[:, :, n:2 * n], T1[:, :, 0:n], T2[:, :, 0:n], ALU.add, [kT1, kT2], [ks_])
                            n *= 2
                        P.barrier()

                    def gelu_out(yb, k_yb, q, c0, n):
                        A(lambda: act.copy(ysb[:, 0:n], yb[:, 0:n]), [k_yb], [k_ysb])
                        A(lambda: act.activation(y2[:, 0:n], yb[:, 0:n], AF.Square), [k_yb], [k_ge])
                        V(lambda: vec.tensor_scalar(y2[:, 0:n], y2[:, 0:n], 0.044715, 1.0, ALU.mult, ALU.add), [k_ge], [k_ge])
                        V(lambda: vec.tensor_tensor(y2[:, 0:n], y2[:, 0:n], ysb[:, 0:n], ALU.mult), [k_ge, k_ysb], [k_ge])
                        A(lambda: act.activation(y2[:, 0:n], y2[:, 0:n], AF.Sigmoid, scale=1.5957691216057308), [k_ge], [k_ge])
                        V(lambda: vec.tensor_tensor(gT[:, q, c0:c0 + n], y2[:, 0:n], ysb[:, 0:n], ALU.mult), [k_ge, k_ysb], [k_gTq[q]])

                    W = lambda n_: sb(pp, n_, [128, 512], F32)
                    xre = W("xre"); xim = W("xim"); w1 = W("w1"); w2 = W("w2"); w3 = W("w3"); w4 = W("w4")
                    vre2 = [W("vre0"), W("vre1")]; vim2 = [W("vim0"), W("vim1")]
                    hh2 = [sb(pp, "hh%d" % i_, [128, 4, 512], BF16) for i_ in range(2)]
                    pend = []
                    k_x = Tk(); k_vv2 = [Tk(), Tk()]; k_h2 = [Tk(), Tk()]; k_pw = Tk()
                    k_w = [Tk() for _ in range(4)]; k_xr = Tk(); k_xi = Tk()
                    k_vr2 = [Tk(), Tk()]; k_vi2 = [Tk(), Tk()]; k_cr = Tk(); k_ci = Tk(); k_t0 = Tk(); k_t1 = Tk()
                    gp_ = nc.gpsimd
                    car = sb(pp, "car", [128, 16, 2], F32); ctmp = sb(pp, "ctmp", [128, 2], F32); k_car = Tk()
                    fin = sb(pp, "fin", [128, 2, 16], F32); k_fin = Tk()
                    v3 = lambda t: t[:, :].rearrange("p (a t) -> p a t", a=2)
                    def emit_bu(w_, j_):
                        for ri_, Bp_ in enumerate((Bpr, Bpi)):
                            bk_, k_bk_ = banks[(2 * j_ + ri_) % 4]
                            P.mm([lambda: pe.matmul(bk_[:, :], lhsT=Bp_[:, j_, :], rhs=uT[:, j_ // 4, w_ * 512:w_ * 512 + 512], start=True, stop=True)],
                                 [k_Bp, k_uTm[w_][j_ // 4]], [k_bk_])

                    emit_bu(0, 0)
                    for w in range(4):
                        c0 = w * 512
                        for j in range(16):
                            q = j // 4
                            br_, k_br = banks[(2 * j) % 4]
                            bi_, k_bi = banks[(2 * j + 1) % 4]
                            cb_ = cosT[:, j, :].unsqueeze(1).to_broadcast([128, 2, 256]); sb_ = sinT[:, j, :].unsqueeze(1).to_broadcast([128, 2, 256])
                            V(lambda: vec.tensor_tensor(v3(w1), v3(br_), cb_, ALU.mult), [k_br, k_E], [k_w[0]])
                            V(lambda: vec.tensor_tensor(v3(w2), v3(bi_), sb_, ALU.mult), [k_bi, k_E], [k_w[1]])
                            V(lambda: vec.tensor_tensor(v3(w3), v3(bi_), cb_, ALU.mult), [k_bi, k_E], [k_w[2]])
                            V(lambda: vec.tensor_tensor(v3(w4), v3(br_), sb_, ALU.mult), [k_br, k_E], [k_w[3]])
                            V(lambda: vec.tensor_tensor(xre[:], w1[:], w2[:], ALU.add), [k_w[0], k_w[1]], [k_xr])
                            V(lambda: vec.tensor_tensor(xim[:], w3[:], w4[:], ALU.subtract), [k_w[2], k_w[3]], [k_xi])
                            vre, vim, k_vv = vre2[j % 2], vim2[j % 2], k_vv2[j % 2]
                            hh, k_h = hh2[j % 2], k_h2[j % 2]
                            rb = mag[:, j:j + 1].to_broadcast([128, 256])
                            k_vr, k_vi = k_vr2[j % 2], k_vi2[j % 2]
                            for c in range(2):
                                cs_ = slice(c * 256, c * 256 + 256)
                                first = (w == 0 and c == 0)
                                sc_re = lambda: V(lambda: vec.tensor_tensor_scan(vre[:, cs_], rb, xre[:, cs_], 0.0 if first else car[:, j, 0:1], ALU.mult, ALU.add),
                                                  [k_xr, k_cr, k_p], [k_vr])
                                sc_im = lambda: V(lambda: vec.tensor_tensor_scan(vim[:, cs_], rb, xim[:, cs_], 0.0 if first else car[:, j, 1:2], ALU.mult, ALU.add),
                                                  [k_xi, k_ci, k_p], [k_vi])
                                if c == 0:
                                    sc_re(); sc_im()
                                else:
                                    sc_im(); sc_re()
                                lr = vre[:, c * 256 + 255:c * 256 + 256]; li = vim[:, c * 256 + 255:c * 256 + 256]
                                if w == 3 and c == 1:
                                    V(lambda: vec.tensor_tensor(ctmp[:, 0:1], li, sinT[:, j, 255:256], ALU.mult), [k_vi, k_E], [k_t0])
                                    V(lambda: vec.tensor_tensor(ctmp[:, 1:2], lr, sinT[:, j, 255:256], ALU.mult), [k_vr, k_E], [k_t1])
                                    V(lambda: vec.scalar_tensor_tensor(fin[:, 0, j:j + 1], lr, cosT[:, j, 255:256], ctmp[:, 0:1], ALU.mult, ALU.subtract),
                                      [k_vr, k_E, k_t0], [k_fin])
                                    V(lambda: vec.scalar_tensor_tensor(fin[:, 1, j:j + 1], li, cosT[:, j, 255:256], ctmp[:, 1:2], ALU.mult, ALU.add),
                                      [k_vi, k_E, k_t1], [k_fin])
                                else:
                                    V(lambda: vec.tensor_tensor(ctmp[:, 1:2], lr, S256[:, j:j + 1], ALU.mult), [k_vr, k_p], [k_t1])
                                    V(lambda: vec.tensor_tensor(ctmp[:, 0:1], li, S256[:, j:j + 1], ALU.mult), [k_vi, k_p], [k_t0])
                                    V(lambda: vec.scalar_tensor_tensor(car[:, j, 1:2], li, C256[:, j:j + 1], ctmp[:, 1:2], ALU.mult, ALU.add),
                                      [k_vi, k_p, k_t1], [k_ci])
                                    V(lambda: vec.scalar_tensor_tensor(car[:, j, 0:1], lr, C256[:, j:j + 1], ctmp[:, 0:1], ALU.mult, ALU.subtract),
                                      [k_vr, k_p, k_t0], [k_cr])
                            k_vv = k_vv2[j % 2]
                            if j < 15:
                                emit_bu(w, j + 1)
                            elif w < 3:
                                emit_bu(w + 1, 0)
                            hv = lambda i_: hh[:, i_, :].rearrange("p (a t) -> p a t", a=2)
                            G(lambda: gp_.tensor_tensor(hv(0), v3(vre), cb_, ALU.mult), [k_vr, k_E], [k_h])
                            G(lambda: gp_.tensor_tensor(hv(1), v3(vim), sb_, ALU.mult), [k_vi, k_E], [k_h])
                            G(lambda: gp_.tensor_tensor(hv(2), v3(vre), sb_, ALU.mult), [k_vr, k_E], [k_h])
                            G(lambda: gp_.tensor_tensor(hv(3), v3(vim), cb_, ALU.mult), [k_vi, k_E], [k_h])
                            if pend:
                                gelu_out(*pend.pop())
                            if j % 4 == 0:
                                yb, k_yb = banks[4 + (q % 2)]
                                P.mm([lambda: pe.matmul(yb[:, :], lhsT=Dd[:, q, :], rhs=uT[:, q, c0:c0 + 512], start=True, stop=False)],
                                     [k_Dd, k_uTm[w][q]], [k_yb])
                            P.mm([lambda: pe.matmul(yb[:, :], lhsT=Cpr[:, j, :], rhs=hh[:, 0, :], start=False, stop=False),
                                  lambda: pe.matmul(yb[:, :], lhsT=Cnr[:, j, :], rhs=hh[:, 1, :], start=False, stop=False),
                                  lambda: pe.matmul(yb[:, :], lhsT=Cpi[:, j, :], rhs=hh[:, 2, :], start=False, stop=False),
                                  lambda: pe.matmul(yb[:, :], lhsT=Cpi[:, j, :], rhs=hh[:, 3, :], start=False, stop=(j % 4 == 3))],
                                 [k_Cp, k_h], [k_yb])
                            if j % 4 == 3:
                                pend.append((yb, k_yb, q, c0, 512))
                    if pend:
                        gelu_out(*pend.pop())
                    for ri, o_ in enumerate((s5rp_o, s5ip_o)):
                        bk, k_bk = nb()
                        P.mm([lambda: pe.transpose(bk[0:16, 0:128], fin[:, ri, :], idf[:])], [k_fin, k_id], [k_bk])
                        V(lambda: vec.tensor_copy(w1[0:16, 0:128], bk[0:16, 0:128]), [k_bk], [k_x])
                        P.dma(P.sp, o_[:, :], w1[0:16, 0:128], reads=[k_x])
                        P.barrier()
                    pp.close()
                    SX = sb(ph, "SX", [16, 2048], F32); k_S0 = Tk()
                    hs = sb(ph, "hs", [128, 2, 16, 16, 5], F32); k_hs = Tk()
                    bus = sb(ph, "bus", [128, 2, 16, 64], F32); k_bus = Tk()
                    hsb = sb(ph, "hsb", [128, 2, 16, 64], BF16); k_hsb = Tk()
                    st1 = sb(ph, "st1", [128, 16, 16], F32); st2 = sb(ph, "st2", [128, 16, 16], F32); k_st = Tk()
                    k_SF = k_S0
                    for ri, s_ in enumerate((s5re0, s5im0)):
                        P.dma(P.sp, SX[:, :], s_[:, :], writes=[k_S0])
                        bk, k_bk = nb()
                        P.mm([lambda j=j: pe.transpose(bk[:, j * 16:(j + 1) * 16], SX[0:16, j * 128:(j + 1) * 128], idf[0:16, 0:16])
                              for j in range(16)], [k_S0, k_id], [k_bk])
                        V(lambda: vec.tensor_copy(hs[:, ri, :, :, 0], bk[:, 0:256].rearrange("p (j m) -> p j m", j=16)), [k_bk], [k_hs])
                    for j in range(16):
                        for ri, Bp in enumerate((Bpr, Bpi)):
                            bk, k_bk = nb()
                            P.mm([lambda: pe.matmul(bk[:, 0:64], lhsT=Bp[:, j, :], rhs=uT[:, j // 4, 2048:2112], start=True, stop=True)],
                                 [k_Bp, k_uTm[4][j // 4]], [k_bk])
                            if ri == 0:
                                A(lambda: act.copy(bus[:, ri, j, :], bk[:, 0:64]), [k_bk], [k_bus])
                            else:
                                V(lambda: vec.tensor_copy(bus[:, ri, j, :], bk[:, 0:64]), [k_bk], [k_bus])
                    lrb = lbr[:, :].unsqueeze(2).to_broadcast([128, 16, 16]); lib = lbi[:, :].unsqueeze(2).to_broadcast([128, 16, 16])
                    busv = bus[:].rearrange("p r j (m t) -> p r j m t", t=4)
                    for t in range(4):
                        pr_, pi_ = hs[:, 0, :, :, t], hs[:, 1, :, :, t]
                        R_ = [k_hs, k_p, k_st, k_bus]
                        V(lambda: vec.tensor_tensor(st1[:], pr_, lrb, ALU.mult), R_, [k_st])
                        V(lambda: vec.tensor_tensor(st2[:], pi_, lib, ALU.mult), R_, [k_st])
                        V(lambda: vec.tensor_tensor(st1[:], st1[:], st2[:], ALU.subtract), R_, [k_st])
                        V(lambda: vec.tensor_tensor(hs[:, 0, :, :, t + 1], st1[:], busv[:, 0, :, :, t], ALU.add), R_, [k_hs])
                        V(lambda: vec.tensor_tensor(st1[:], pi_, lrb, ALU.mult), R_, [k_st])
                        V(lambda: vec.tensor_tensor(st2[:], pr_, lib, ALU.mult), R_, [k_st])
                        V(lambda: vec.tensor_tensor(st1[:], st1[:], st2[:], ALU.add), R_, [k_st])
                        V(lambda: vec.tensor_tensor(hs[:, 1, :, :, t + 1], st1[:], busv[:, 1, :, :, t], ALU.add), R_, [k_hs])
                    for ri in range(2):
                        V(lambda: vec.tensor_copy(hsb[:, ri, :, :].rearrange("p j (m t) -> p j m t", t=4), hs[:, ri, :, :, 1:5]), [k_hs], [k_hsb])
                    for q in range(4):
                        yb, k_yb = nb()
                        fns = [lambda: pe.matmul(yb[:, 0:64], lhsT=Dd[:, q, :], rhs=uT[:, q, 2048:2112], start=True, stop=False)]
                        for j in range(4 * q, 4 * q + 4):
                            fns.append(lambda j=j: pe.matmul(yb[:, 0:64], lhsT=Cpr[:, j, :], rhs=hsb[:, 0, j, :], start=False, stop=False))
                            fns.append(lambda j=j: pe.matmul(yb[:, 0:64], lhsT=Cpi[:, j, :], rhs=hsb[:, 1, j, :], start=False, stop=(j == 4 * q + 3)))
                        P.mm(fns, [k_Dd, k_Cp, k_hsb, k_uTm[4][q]], [k_yb])
                        gelu_out(yb, k_yb, q, 2048, 64)
                    for ri, o_ in enumerate((s5rs_o, s5is_o)):
                        for jj in range(4):
                            bk, k_bk = nb()
                            P.mm([lambda j=j: pe.transpose(bk[0:16, (j % 4) * 128:(j % 4 + 1) * 128], hs[:, ri, j, :, 4], idf[:])
                                  for j in range(4 * jj, 4 * jj + 4)], [k_hs, k_id], [k_bk])
                            V(lambda: vec.tensor_copy(SX[:, jj * 512:(jj + 1) * 512], bk[0:16, :]), [k_bk], [k_SF])
                        P.dma(P.sp, o_[:, :], SX[:, :], reads=[k_SF])
                    P.barrier()
                if STAGE >= 4:
                    with ExitStack() as ph:
                        wglu_sb, k_wglu = load_w(ph, "wglu_sb", wglu_d, 512, 512)
                        wout_sb, k_wout = load_w(ph, "wout_sb", wout_d, 1024, 1024)
                        gpost, k_gpost = load_gpost(ph, "gpost0", npo[0])
                        sT = sb(ph, "sT", [128, 4, NTOK], BF16); k_sTw = [Tk() for _ in range(5)]
                        sg = sb(ph, "sgl", [128, 512], F32); k_sg = Tk()
                        junk = sb(ph, "pjunk", [128, 512], BF16); tmp = sb(ph, "ptmp", [128, 512], F32)
                        pbufs = (junk, Tk(), tmp, Tk(), sb(ph, "ptmp2", [128, 512], F32), Tk())
                        for w, (c0, n) in enumerate(WIDE):
                            for m in range(4):
                                bk, k_bk = nb()
                                P.mm([lambda k=k: pe.matmul(bk[:, 0:n], lhsT=wglu_sb[:, k, m * 128:(m + 1) * 128], rhs=gT[:, k, c0:c0 + n],
                                                            start=(k == 0), stop=(k == 3)) for k in range(4)], [k_wglu] + k_gTq, [k_bk])
                                A(lambda: act.activation(sg[:, 0:n], bk[:, 0:n], AF.Sigmoid), [k_bk], [k_sg])
                                V(lambda: vec.tensor_tensor(sT[:, m, c0:c0 + n], gT[:, m, c0:c0 + n], sg[:, 0:n], ALU.mult), [k_sg] + k_gTq, [k_sTw[w]])
                        for i in range(NT):
                            r = rows(i)
                            bk2 = [nb(), nb()]
                            for dh in range(2):
                                fns = [lambda c=c: pe.matmul(bk2[dh][0][0:r, :], lhsT=aT[:, c, i * 128:i * 128 + r], rhs=wout_sb[:, c, dh * 512:(dh + 1) * 512],
                                                             start=(c == 0), stop=False) for c in range(4)]
                                fns += [lambda c=c: pe.matmul(bk2[dh][0][0:r, :], lhsT=sT[:, c, i * 128:i * 128 + r], rhs=wout_sb[:, 4 + c, dh * 512:(dh + 1) * 512],
                                                              start=False, stop=(c == 3)) for c in range(4)]
                                P.mm(fns, [k_aT, k_sTw[min(i // 4, 4)], k_wout], [bk2[dh][1]])
                            post_norm_residual(pbufs, i, bk2, gpost, k_gpost)
                        P.barrier()
        P.barrier()
        if STAGE >= 4:
            ffn(0)
        if STAGE >= 5:
            with ExitStack() as L1:
                gpT = sb(L1, "gpT", [128, 8, 30 + 2048], BF16); k_gpw = [Tk() for _ in range(5)]; k_gp = k_gpw[4]
                gsT = sb(L1, "gsT", [128, 8, 16, 34], BF16); k_gs = Tk()
                b1, k_b1 = colvec(L1, "b_pw1", bpw1_d, 16)
                V(lambda: vec.memset(gpT[:, :, 0:30], 0.0), w=[k_gp])
                with ExitStack() as ph:
                    gtail = sb(ph, "gtail", [128, 8, 30], F32); gsn = sb(ph, "gsn", [128, 8, 64], F32); k_gt = Tk()
                    gcol, k_g = colvec(ph, "g_pm1", npm[1], 8)
                    w1_sb, k_w1 = load_w(ph, "wpw1_sb", wpw1_d, 1024, 2048)
                    hT = sb(ph, "hT1", [128, 8, NTOK], BF16); k_hT = [Tk() for _ in range(NT)]
                    norm_T(norm_bufs(ph), range(NT), gcol, k_g, hT, k_hT, 0)
                    sgb = sb(ph, "sgb", [128, 512], F32); k_sgb = Tk()
                    SC = sb(ph, "SC", [120, 4, 1024], BF16); k_SC = Tk()
                    for tI in range(4):
                        for m_ in range(4):
                            P.dma(P.pool, SC[30 * m_:30 * m_ + 30, tI, :], sconv[4 * tI + m_, :, :], writes=[k_SC], max_dma_last_dim=4096)
                    for tI in range(4):
                        tb, k_tb = ntb()
                        P.mm([lambda c=c: pe.transpose(tb[:, c * 120:(c + 1) * 120], SC[0:120, tI, c * 128:(c + 1) * 128], idb[0:120, 0:120])
                              for c in range(8)], [k_SC, k_id], [k_tb])
                        V(lambda: vec.tensor_copy(gsT[:, :, 4 * tI:4 * tI + 4, 0:30], tb[:, 0:960].rearrange("p (c m r) -> p c m r", c=8, m=4)),
                          [k_tb], [k_gs])
                    P.dma(P.sp, convs_o[:, 0:26, :], sconv[:, 4:30, :])
                    for w, (c0, n) in enumerate(WIDE):
                        kr = [k_hT[i] for i in tiles_of(c0, n)] + [k_w1]
                        for c in range(8):
                            ba, k_ba = nb()
                            P.mm([lambda k=k: pe.matmul(ba[:, 0:n], lhsT=w1_sb[:, k, c * 128:(c + 1) * 128], rhs=hT[:, k, c0:c0 + n],
                                                        start=(k == 0), stop=(k == 7)) for k in range(8)], kr, [k_ba])
                            bb_, k_bb = nb()
                            P.mm([lambda k=k: pe.matmul(bb_[:, 0:n], lhsT=w1_sb[:, k, 1024 + c * 128:1024 + (c + 1) * 128], rhs=hT[:, k, c0:c0 + n],
                                                        start=(k == 0), stop=(k == 7)) for k in range(8)], kr, [k_bb])
                            A(lambda: act.activation(sgb[:, 0:n], bb_[:, 0:n], AF.Sigmoid, bias=b1[:, 8 + c:9 + c]), [k_bb, k_b1], [k_sgb])
                            if w < 4:
                                V(lambda: vec.scalar_tensor_tensor(gpT[:, c, 30 + c0:30 + c0 + n], ba[:, 0:n], b1[:, c:c + 1], sgb[:, 0:n], ALU.add, ALU.mult),
                                  [k_ba, k_b1, k_sgb], [k_gpw[w]])
                                if w == 3:
                                    V(lambda: vec.scalar_tensor_tensor(gtail[:, c, :], ba[:, 482:512], b1[:, c:c + 1], sgb[:, 482:512], ALU.add, ALU.mult),
                                      [k_ba, k_b1, k_sgb], [k_gt])
                            else:
                                V(lambda: vec.scalar_tensor_tensor(gsn[:, c, :], ba[:, 0:64], b1[:, c:c + 1], sgb[:, 0:64], ALU.add, ALU.mult),
                                  [k_ba, k_b1, k_sgb], [k_gt])
                                V(lambda: vec.tensor_copy(gsT[:, c, :, 30:34], gsn[:, c, :].rearrange("p (m t) -> p m t", t=4)), [k_gt], [k_gs])
                    OT = sb(ph, "OT", [64, 1024], F32); k_OT = Tk()
                    for (src_, nr) in ((gtail, 30), (gsn, 64)):
                        for h in range(2):
                            bk, k_bk = nb()
                            P.mm([lambda c=c: pe.transpose(bk[0:nr, (c % 4) * 128:(c % 4 + 1) * 128], src_[:, c, :], idf[:])
                                  for c in range(4 * h, 4 * h + 4)], [k_gt, k_id], [k_bk])
                            V(lambda: vec.tensor_copy(OT[0:nr, h * 512:(h + 1) * 512], bk[0:nr, :]), [k_bk], [k_OT])
                        if nr == 30:
                            P.dma(P.sp, convp_o[:, :], OT[0:30, :], reads=[k_OT])
                        else:
                            for m_ in range(16):
                                P.dma(P.sp, convs_o[m_, 26:30, :], OT[4 * m_:4 * m_ + 4, :], reads=[k_OT])
                    P.barrier()
                with ExitStack() as ph:
                    wdwT = sb(ph, "wdwT", [128, 8, 31], F32); k_wdw = Tk()
                    with ExitStack() as tmps:
                        wdn = sb(tmps, "wdn", [31, 1024], F32); k_wdn = Tk()
                        P.dma(P.sp, wdn[:, :], wdw_d[:, :], writes=[k_wdn])
                        bkw, k_bkw = nb()
                        P.mm([lambda c_=c_: pe.transpose(bkw[:, c_ * 31:(c_ + 1) * 31], wdn[0:31, c_ * 128:(c_ + 1) * 128], idf[0:31, 0:31])
                              for c_ in range(8)], [k_wdn, k_id], [k_bkw])
                        V(lambda: vec.tensor_copy(wdwT[:].rearrange("p c j -> p (c j)"), bkw[:, 0:248]), [k_bkw], [k_wdw])
                        P.barrier()
                    bdw, k_bdw = colvec(ph, "bdw", bdw_d, 8); lng, k_lng = colvec(ph, "lng", lng_d, 8); lnb, k_lnb = colvec(ph, "lnb", lnb_d, 8)
                    w2_sb, k_w2 = load_w(ph, "wpw2_sb", wpw2_d, 1024, 1024)
                    b2row = sb(ph, "b2row", [1, 1024], BF16); k_b2 = Tk()
                    P.dma(P.pool, b2row[:], bpw2_d[:, :], writes=[k_b2], max_dma_last_dim=4096)
                    gpost, k_gpost = load_gpost(ph, "gpost1", npo[1])
                    wdg2 = [sb(ph, "wdg%d" % i_, [128, 31, 128], BF16) for i_ in range(2)]; k_wdg2 = [Tk(), Tk()]
                    cT = sb(ph, "cT", [128, 8, 512], F32); k_cTc = [Tk() for _ in range(8)]
                    cb = sb(ph, "cb", [128, 8, 512], BF16); c2b = sb(ph, "c2b", [128, 8, 512], BF16); k_cb = Tk()
                    actT = sb(ph, "actT", [128, 8, 512], BF16); k_actc = [Tk() for _ in range(8)]; k_c2b = Tk()
                    F5 = lambda n_: sb(ph, n_, [128, 512], F32)
                    mean_sb = F5("mean_sb"); m2 = F5("m2"); rstd_sb = F5("rstd_sb"); tt = [F5("tt0"), F5("tt1")]
                    k_st = Tk(); k_tt = [Tk(), Tk()]
                    junk = sb(ph, "cjunk", [128, 512], BF16); tmp = sb(ph, "ctmp2", [128, 512], F32)
                    pbufs = (junk, Tk(), tmp, Tk())
                    cTs = sb(ph, "cTs", [128, 8, 64], F32); k_cTs = Tk()

                    def conv_w(w):
                        if w == 4:
                            return
                        c0, n = WIDE[w]
                        for c in range(8):
                            wdg, k_wdg = wdg2[c % 2], k_wdg2[c % 2]
                            on_pool = c not in (3, 5, 7)
                            bld = (lambda f: G(f, [k_id, k_wdw], [k_wdg])) if on_pool else (lambda f: V(f, [k_id, k_wdw], [k_wdg]))
                            eng_ = nc.gpsimd if on_pool else vec
                            bld(lambda: eng_.tensor_tensor(wdg[:], idb[:].unsqueeze(1).to_broadcast([128, 31, 128]),
                                                           wdwT[:, c, :].unsqueeze(2).to_broadcast([128, 31, 128]), ALU.mult))
                            bk, k_bk = nb()
                            P.mm([lambda j=j: pe.matmul(bk[:, 0:n], lhsT=wdg[:, j, :], rhs=gpT[:, c, c0 + j:c0 + j + n],
                                                        start=(j == 0), stop=(j == 30)) for j in range(31)], [k_wdg] + k_gpw, [k_bk])
                            A(lambda: act.activation(cT[:, c, 0:n], bk[:, 0:n], AF.Identity, bias=bdw[:, c:c + 1]), [k_bk, k_bdw], [k_cTc[c]])
                            if w == 3:
                                bk2_, k_bk2_ = nb()
                                P.mm([lambda j=j: pe.matmul(bk2_[:, 0:64].rearrange("p (m t) -> p m t", t=4), lhsT=wdg[:, j, :], rhs=gsT[:, c, :, j:j + 4],
                                                            start=(j == 0), stop=(j == 30)) for j in range(31)], [k_wdg, k_gs], [k_bk2_])
                                A(lambda: act.activation(cTs[:, c, :], bk2_[:, 0:64], AF.Identity, bias=bdw[:, c:c + 1]), [k_bk2_, k_bdw], [k_cTs])
                    conv_w(0)
                    for w, (c0, n) in enumerate(WIDE):
                        cX = cT if w < 4 else cTs
                        k_cX = k_cTc if w < 4 else [k_cTs] * 8
                        V(lambda: vec.tensor_copy(cb[:, :, 0:n], cX[:, :, 0:n]), k_cX, [k_cb])
                        A(lambda: act.activation(c2b[:, :, 0:n], cX[:, :, 0:n], AF.Square), k_cX, [k_c2b])
                        bm, k_bm = nb()
                        P.mm([lambda c=c: pe.matmul(bm[:, 0:n], lhsT=onesln[:], rhs=cb[:, c, 0:n], start=(c == 0), stop=(c == 7)) for c in range(8)],
                             [k_cb, k_const], [k_bm])
                        bq, k_bq = nb()
                        P.mm([lambda c=c: pe.matmul(bq[:, 0:n], lhsT=onesln[:], rhs=c2b[:, c, 0:n], start=(c == 0), stop=(c == 7)) for c in range(8)],
                             [k_c2b, k_const], [k_bq])
                        A(lambda: act.copy(mean_sb[:, 0:n], bm[:, 0:n]), [k_bm], [k_st])
                        V(lambda: vec.tensor_tensor(m2[:, 0:n], mean_sb[:, 0:n], mean_sb[:, 0:n], ALU.mult), [k_st], [k_st])
                        V(lambda: vec.tensor_tensor(m2[:, 0:n], bq[:, 0:n], m2[:, 0:n], ALU.subtract), [k_bq, k_st], [k_st])
                        A(lambda: act.activation(rstd_sb[:, 0:n], m2[:, 0:n], AF.Sqrt, bias=lneps_t[:, 0:1]), [k_st, k_const], [k_st])
                        V(lambda: vec.reciprocal(rstd_sb[:, 0:n], rstd_sb[:, 0:n]), [k_st], [k_st])
                        for c in range(8):
                            t_, k_t = tt[c % 2], k_tt[c % 2]
                            V(lambda: vec.tensor_tensor(t_[:, 0:n], cX[:, c, 0:n], mean_sb[:, 0:n], ALU.subtract), [k_cX[c], k_st], [k_t])
                            V(lambda: vec.tensor_tensor(t_[:, 0:n], t_[:, 0:n], rstd_sb[:, 0:n], ALU.mult), [k_t, k_st], [k_t])
                            A(lambda: act.activation(actT[:, c, 0:n], t_[:, 0:n], AF.Silu, scale=lng[:, c:c + 1], bias=lnb[:, c:c + 1]),
                              [k_t, k_lng, k_lnb], [k_actc[c]])
                        if w + 1 < len(WIDE):
                            conv_w(w + 1)
                        for i in tiles_of(c0, n):
                            r = rows(i)
                            lc = i * 128 - c0
                            bk2 = [nb(), nb()]
                            for dh in range(2):
                                fns = [lambda c=c: pe.matmul(bk2[dh][0][0:r, :], lhsT=actT[:, c, lc:lc + r], rhs=w2_sb[:, c, dh * 512:(dh + 1) * 512],
                                                             start=(c == 0), stop=False) for c in range(8)]
                                fns.append(lambda: pe.matmul(bk2[dh][0][0:r, :], lhsT=onesb[0:1, 0:r], rhs=b2row[0:1, dh * 512:(dh + 1) * 512],
                                                             start=False, stop=True))
                                P.mm(fns, k_actc + [k_w2, k_b2, k_const], [bk2[dh][1]])
                            post_norm_residual(pbufs, i, bk2, gpost, k_gpost)
                    P.barrier()
            if STAGE >= 6:
                ffn(1)
        if True:
            for i in range(NT):
                r = rows(i)
                P.dma(P.sp, y_o[i * 128:i * 128 + r, :], X[0:r, i, :], reads=[kX[i]])
        P.finish()
    return nc


_NC_CACHE = {}


def kernel(**inp):
    f = lambda a: np.ascontiguousarray(np.asarray(a, dtype=np.float32))
    I = {k: f(v) for k, v in inp.items()}
    w_in = I["w_in_ab"][0]
    q, k, v, u = w_in[:, 0:512], w_in[:, 512:640], w_in[:, 640:768], w_in[:, 768:1280]
    win = np.concatenate([q, k[:, 0:64], k[:, 0:64], k[:, 64:128], k[:, 64:128], k, v, u], axis=1)
    shared = dict(
        npm=I["norm_pre_mix"], npo=I["norm_post_mix"], nfp=I["norm_pre_ffn"], nfo=I["norm_post_ffn"],
        win=f(win), sinks=I["attn_sinks"], lre=I["s5_lambda_re"].reshape(2048), lim=I["s5_lambda_im"].reshape(2048),
        lst=I["s5_log_step"].reshape(32), bre=I["s5_b_re"].reshape(2048, 16), bim=I["s5_b_im"].reshape(2048, 16),
        cre=I["s5_c_re"].reshape(512, 64), cim=I["s5_c_im"].reshape(512, 64), dsk=I["s5_d"].reshape(512),
        wglu=I["w_glu"][0], wout=I["w_out_ab"][0], wpw1=I["w_pw1"][0], bpw1=I["b_pw1"].reshape(2048),
        wdw=I["w_dw"][0], bdw=I["b_dw"].reshape(1024), lng=I["conv_ln_g"].reshape(1024), lnb=I["conv_ln_b"].reshape(1024),
        wpw2=I["w_pw2"][0], bpw2=I["b_pw2"].reshape(1, 1024), wg=I["w_ffn_gate"], wu=I["w_ffn_up"], wd=I["w_ffn_down"])
    in_maps = []
    for b in range(8):
        sl = slice(16 * b, 16 * b + 16)
        m = dict(shared)
        m["xin"] = f(np.concatenate([I["x_prompt"][b], I["x_sample"][sl].reshape(64, 1024)], axis=0))
        m["ck"] = f(I["cache_k"][0, sl].reshape(16, 128, 128)); m["cv"] = f(I["cache_v"][0, sl].reshape(16, 128, 128))
        m["s5re0"] = f(I["state_s5_re"][0, sl].reshape(16, 2048)); m["s5im0"] = f(I["state_s5_im"][0, sl].reshape(16, 2048))
        m["sconv"] = f(I["state_conv"][0, sl])
        in_maps.append(m)
    if "nc" not in _NC_CACHE:
        _NC_CACHE["nc"] = build_nc()
    res = run_bass_kernel_spmd(_NC_CACHE["nc"], in_maps[:NCORES], core_ids=list(range(NCORES)))
    R = res.results
    cat = lambda key: np.stack([np.asarray(R[min(b, NCORES - 1)][key], dtype=np.float32) for b in range(8)])
    y = cat("y")
    y_prompt = y[:, :2048, :]
    y_sample = y[:, 2048:, :].reshape(128, 4, 1024)
    k_prompt = cat("kp").reshape(1, 8, 128, 2, 64); v_prompt = cat("vp").reshape(1, 8, 128, 2, 64)
    s5rp = cat("s5rp").reshape(1, 8, 32, 64); s5ip = cat("s5ip").reshape(1, 8, 32, 64)
    convp = cat("convp").reshape(1, 8, 30, 1024)
    k_sample = cat("ks").reshape(1, 128, 128, 2, 64); v_sample = cat("vs").reshape(1, 128, 128, 2, 64)
    s5rs = cat("s5rs").reshape(1, 128, 32, 64); s5is = cat("s5is").reshape(1, 128, 32, 64)
    convs = cat("convs").reshape(1, 128, 30, 1024)
    return (np.ascontiguousarray(y_prompt), np.ascontiguousarray(y_sample), k_prompt, v_prompt, s5rp, s5ip, convp,
            k_sample, v_sample, s5rs, s5is, convs)
```
